# Optimizing a Trainium2 kernel written in Bass

```python
import math
import jax
import jax.numpy as jnp
from jax import lax
import numpy as np

D_MODEL = 1024
BATCH = 16
SEQ = 2048
DEPTH = 1
DEC_BATCH = 32
DEC_SEQ = 64
PAST_LEN = 4096

CHUNK = 64
LEFT_CHUNKS = 8
BAND = (LEFT_CHUNKS + 1) * CHUNK
ATT_REACH = LEFT_CHUNKS * CHUNK
ATT_WIDTH = D_MODEL // 2
N_HEADS = 8
HEAD_DIM = ATT_WIDTH // N_HEADS
REL_CLIP = 256
ATT_SCALE = HEAD_DIM ** -0.5
SSM_WIDTH = D_MODEL // 2
SSM_GROUP = 16
N_GROUPS = SSM_WIDTH // SSM_GROUP
STATE_DIM = 64
DT_MIN = 1e-3
DT_MAX = 1e-1
D_FF = ((8 * D_MODEL + 3 * 256 - 1) // (3 * 256)) * 256
IN_WIDTH = 3 * ATT_WIDTH + SSM_WIDTH + 2 * D_MODEL
ALPHA = (2 * DEPTH) ** 0.25
BETA = (8 * DEPTH) ** -0.25
LN_EPS = 1e-5
NEG_INF = -1e30

kernel_name = "hybrid_chunk_band_attn_s5_stream_step"


def layer_norm(x, gain=None, bias=None):
    xf = x.astype(jnp.float32)
    xc = xf - jnp.mean(xf, axis=-1, keepdims=True)
    y = xc * lax.rsqrt(jnp.mean(xc * xc, axis=-1, keepdims=True) + LN_EPS)
    if gain is not None:
        y = y * gain.astype(jnp.float32) + bias.astype(jnp.float32)
    return y.astype(x.dtype)


def modulate(x, shift, scale):
    return layer_norm(x) * (1 + scale) + shift


def rel_bias_lookup(rel_bias, dist):
    idx = jnp.clip(dist, -REL_CLIP, REL_CLIP) + REL_CLIP
    return rel_bias[:, idx].astype(jnp.float32)


def band_attention_prompt(q, k, v, rel_bias):
    bsz, seq, nh, hd = q.shape
    n_chunks = seq // CHUNK
    pad = LEFT_CHUNKS * CHUNK
    kp = jnp.pad(k, ((0, 0), (pad, 0), (0, 0), (0, 0)))
    vp = jnp.pad(v, ((0, 0), (pad, 0), (0, 0), (0, 0)))
    band_idx = jnp.arange(BAND)
    bias = rel_bias_lookup(rel_bias, jnp.arange(CHUNK)[:, None] + pad - band_idx[None, :])
    qc = q.reshape(bsz, n_chunks, CHUNK, nh, hd).swapaxes(0, 1)

    def one_chunk(args):
        q_c, c_idx = args
        k_b = lax.dynamic_slice_in_dim(kp, c_idx * CHUNK, BAND, axis=1)
        v_b = lax.dynamic_slice_in_dim(vp, c_idx * CHUNK, BAND, axis=1)
        s = jnp.einsum('bqhd,bkhd->bhqk', q_c, k_b).astype(jnp.float32) * ATT_SCALE + bias
        valid = band_idx >= (LEFT_CHUNKS - c_idx) * CHUNK
        s = jnp.where(valid, s, NEG_INF)
        p = jax.nn.softmax(s, axis=-1).astype(v_b.dtype)
        return jnp.einsum('bhqk,bkhd->bqhd', p, v_b)

    o = lax.map(one_chunk, (qc, jnp.arange(n_chunks)))
    return o.swapaxes(0, 1).reshape(bsz, seq, nh, hd)


def band_attention_sample(q, k, v, cache_k, cache_v, rel_bias):
    n_new = q.shape[1]
    win = cache_k.shape[1]
    kk = jnp.concatenate([cache_k.astype(k.dtype), k], axis=1)
    vv = jnp.concatenate([cache_v.astype(v.dtype), v], axis=1)
    dist = jnp.arange(n_new)[:, None] + win - jnp.arange(win + n_new)[None, :]
    s = jnp.einsum('bqhd,bkhd->bhqk', q, kk).astype(jnp.float32) * ATT_SCALE + rel_bias_lookup(rel_bias, dist)
    p = jax.nn.softmax(s, axis=-1).astype(vv.dtype)
    return jnp.einsum('bhqk,bkhd->bqhd', p, vv)


def s5_discretize(a_re, a_im, log_dt, b_re, b_im):
    a_re = a_re.astype(jnp.float32)
    a_im = a_im.astype(jnp.float32)
    b_re = b_re.astype(jnp.float32)
    b_im = b_im.astype(jnp.float32)
    dt = jnp.exp(log_dt.astype(jnp.float32))[:, None]
    mag = jnp.exp(dt * a_re)
    ang = dt * a_im
    abar_re = mag * jnp.cos(ang)
    abar_im = mag * jnp.sin(ang)
    den = a_re * a_re + a_im * a_im
    n_re = abar_re - 1
    f_re = (n_re * a_re + abar_im * a_im) / den
    f_im = (abar_im * a_re - n_re * a_im) / den
    bbar_re = f_re[..., None] * b_re - f_im[..., None] * b_im
    bbar_im = f_re[..., None] * b_im + f_im[..., None] * b_re
    return abar_re, abar_im, bbar_re, bbar_im


def complex_affine_combine(e1, e2):
    a1r, a1i, b1r, b1i = e1
    a2r, a2i, b2r, b2i = e2
    return (a2r * a1r - a2i * a1i,
            a2r * a1i + a2i * a1r,
            a2r * b1r - a2i * b1i + b2r,
            a2r * b1i + a2i * b1r + b2i)


def s5_layer(u, s0_re, s0_im, a_re, a_im, log_dt, b_re, b_im, c_re, c_im, d_skip):
    bsz, seq, _ = u.shape
    uf = u.astype(jnp.float32)
    ug = uf.reshape(bsz, seq, N_GROUPS, SSM_GROUP)
    abar_re, abar_im, bbar_re, bbar_im = s5_discretize(a_re, a_im, log_dt, b_re, b_im)
    bu_re = jnp.einsum('blgm,gpm->lbgp', ug, bbar_re)
    bu_im = jnp.einsum('blgm,gpm->lbgp', ug, bbar_im)
    s0r = s0_re.astype(jnp.float32)
    s0i = s0_im.astype(jnp.float32)
    bu_re = bu_re.at[0].add(abar_re * s0r - abar_im * s0i)
    bu_im = bu_im.at[0].add(abar_re * s0i + abar_im * s0r)
    a_r = jnp.broadcast_to(abar_re, (seq, 1, N_GROUPS, STATE_DIM))
    a_i = jnp.broadcast_to(abar_im, (seq, 1, N_GROUPS, STATE_DIM))
    _, _, s_re, s_im = lax.associative_scan(complex_affine_combine, (a_r, a_i, bu_re, bu_im), axis=0)
    y = (jnp.einsum('lbgp,gmp->blgm', s_re, c_re.astype(jnp.float32))
         - jnp.einsum('lbgp,gmp->blgm', s_im, c_im.astype(jnp.float32)))
    y = y.reshape(bsz, seq, SSM_WIDTH) + d_skip.astype(jnp.float32) * uf
    return y.astype(u.dtype), s_re[-1].astype(u.dtype), s_im[-1].astype(u.dtype)


def encoder_layer(x, c, cache_k, cache_v, s0_re, s0_im, p):
    (w_ada, b_ada, w_in, rel_bias, a_re, a_im, log_dt, b_re, b_im, c_re, c_im, d_skip,
     w_attn_proj, w_glu, w_out, ln1_g, ln1_b, w_ffn_in, w_ffn_out, ln2_g, ln2_b) = p
    bsz, seq, _ = x.shape
    mod = (jax.nn.silu(c) @ w_ada + b_ada)[:, None, :]
    sh1, sc1, g1, sh2, sc2, g2 = jnp.split(mod, 6, axis=-1)

    h = modulate(x, sh1, sc1)
    z = h @ w_in
    q, k, v, u, ga, gb = jnp.split(
        z, [ATT_WIDTH, 2 * ATT_WIDTH, 3 * ATT_WIDTH, 3 * ATT_WIDTH + SSM_WIDTH,
            3 * ATT_WIDTH + SSM_WIDTH + D_MODEL], axis=-1)
    q = q.reshape(bsz, seq, N_HEADS, HEAD_DIM)
    k = k.reshape(bsz, seq, N_HEADS, HEAD_DIM)
    v = v.reshape(bsz, seq, N_HEADS, HEAD_DIM)
    if cache_k is None:
        o_att = band_attention_prompt(q, k, v, rel_bias)
        keep = min(ATT_REACH, seq)
        k_state, v_state = k[:, seq - keep:], v[:, seq - keep:]
    else:
        o_att = band_attention_sample(q, k, v, cache_k, cache_v, rel_bias)
        k_state, v_state = k, v
    o_att = o_att.reshape(bsz, seq, ATT_WIDTH) @ w_attn_proj

    y_ssm, s_re, s_im = s5_layer(u, s0_re, s0_im, a_re, a_im, log_dt, b_re, b_im, c_re, c_im, d_skip)
    glu_a, glu_b = jnp.split(jax.nn.gelu(y_ssm) @ w_glu, 2, axis=-1)
    o_ssm = glu_a * jax.nn.sigmoid(glu_b)

    mixed = jax.nn.sigmoid(ga) * o_att + jax.nn.sigmoid(gb) * o_ssm
    x = layer_norm(ALPHA * x + g1 * (mixed @ w_out), ln1_g, ln1_b)

    h = modulate(x, sh2, sc2)
    f_gate, f_up = jnp.split(h @ w_ffn_in, 2, axis=-1)
    f = (jax.nn.silu(f_gate) * f_up) @ w_ffn_out
    x = layer_norm(ALPHA * x + g2 * f, ln2_g, ln2_b)
    return x, k_state, v_state, s_re, s_im


def setup_inputs(seed: int = 0) -> dict:
    key = jax.random.key(seed)
    ks = jax.random.split(key, 32)
    f32 = jnp.float32

    def nrm(k, shape, scale):
        return jax.random.normal(k, shape, f32) * scale

    cache_rows = min(ATT_REACH, PAST_LEN)
    a_im_init = math.pi * jnp.arange(STATE_DIM, dtype=f32)
    return {
        'x_prompt': nrm(ks[0], (BATCH, SEQ, D_MODEL), 1.0),
        'x_sample': nrm(ks[1], (DEC_BATCH, DEC_SEQ, D_MODEL), 1.0),
        'c_prompt': nrm(ks[2], (BATCH, D_MODEL), 1.0),
        'c_sample': nrm(ks[3], (DEC_BATCH, D_MODEL), 1.0),
        'cache_attn_k': nrm(ks[4], (DEPTH, DEC_BATCH, cache_rows, N_HEADS, HEAD_DIM), 1.0),
        'cache_attn_v': nrm(ks[5], (DEPTH, DEC_BATCH, cache_rows, N_HEADS, HEAD_DIM), 1.0),
        'state_ssm_re': nrm(ks[6], (DEPTH, DEC_BATCH, N_GROUPS, STATE_DIM), 0.1),
        'state_ssm_im': nrm(ks[7], (DEPTH, DEC_BATCH, N_GROUPS, STATE_DIM), 0.1),
        'w_ada': nrm(ks[8], (DEPTH, D_MODEL, 6 * D_MODEL), D_MODEL ** -0.5),
        'b_ada': nrm(ks[9], (DEPTH, 6 * D_MODEL), 0.01),
        'w_in': nrm(ks[10], (DEPTH, D_MODEL, IN_WIDTH), D_MODEL ** -0.5),
        'rel_bias': nrm(ks[11], (DEPTH, N_HEADS, 2 * REL_CLIP + 1), 0.1),
        'ssm_a_re': -0.5 + nrm(ks[12], (DEPTH, N_GROUPS, STATE_DIM), 0.01),
        'ssm_a_im': a_im_init + nrm(ks[13], (DEPTH, N_GROUPS, STATE_DIM), 0.01),
        'ssm_log_dt': jax.random.uniform(ks[14], (DEPTH, N_GROUPS), f32, math.log(DT_MIN), math.log(DT_MAX)),
        'ssm_b_re': nrm(ks[15], (DEPTH, N_GROUPS, STATE_DIM, SSM_GROUP), (2 * SSM_GROUP) ** -0.5),
        'ssm_b_im': nrm(ks[16], (DEPTH, N_GROUPS, STATE_DIM, SSM_GROUP), (2 * SSM_GROUP) ** -0.5),
        'ssm_c_re': nrm(ks[17], (DEPTH, N_GROUPS, SSM_GROUP, STATE_DIM), (2 * STATE_DIM) ** -0.5),
        'ssm_c_im': nrm(ks[18], (DEPTH, N_GROUPS, SSM_GROUP, STATE_DIM), (2 * STATE_DIM) ** -0.5),
        'ssm_d': nrm(ks[19], (DEPTH, SSM_WIDTH), 1.0),
        'w_attn_proj': nrm(ks[20], (DEPTH, ATT_WIDTH, D_MODEL), ATT_WIDTH ** -0.5),
        'w_glu': nrm(ks[21], (DEPTH, SSM_WIDTH, 2 * D_MODEL), SSM_WIDTH ** -0.5),
        'w_out': nrm(ks[22], (DEPTH, D_MODEL, D_MODEL), BETA * D_MODEL ** -0.5),
        'ln1_g': 1.0 + nrm(ks[23], (DEPTH, D_MODEL), 0.01),
        'ln1_b': nrm(ks[24], (DEPTH, D_MODEL), 0.01),
        'w_ffn_in': nrm(ks[25], (DEPTH, D_MODEL, 2 * D_FF), D_MODEL ** -0.5),
        'w_ffn_out': nrm(ks[26], (DEPTH, D_FF, D_MODEL), BETA * D_FF ** -0.5),
        'ln2_g': 1.0 + nrm(ks[27], (DEPTH, D_MODEL), 0.01),
        'ln2_b': nrm(ks[28], (DEPTH, D_MODEL), 0.01),
    }


def reference(x_prompt, x_sample, c_prompt, c_sample, cache_attn_k, cache_attn_v, state_ssm_re, state_ssm_im,
              w_ada, b_ada, w_in, rel_bias, ssm_a_re, ssm_a_im, ssm_log_dt, ssm_b_re, ssm_b_im, ssm_c_re, ssm_c_im,
              ssm_d, w_attn_proj, w_glu, w_out, ln1_g, ln1_b, w_ffn_in, w_ffn_out, ln2_g, ln2_b):
    y_prompt, y_sample = x_prompt, x_sample
    k_p, v_p, re_p, im_p = [], [], [], []
    k_s, v_s, re_s, im_s = [], [], [], []
    for l in range(DEPTH):
        p = (w_ada[l], b_ada[l], w_in[l], rel_bias[l], ssm_a_re[l], ssm_a_im[l], ssm_log_dt[l],
             ssm_b_re[l], ssm_b_im[l], ssm_c_re[l], ssm_c_im[l], ssm_d[l], w_attn_proj[l], w_glu[l],
             w_out[l], ln1_g[l], ln1_b[l], w_ffn_in[l], w_ffn_out[l], ln2_g[l], ln2_b[l])
        s0 = jnp.zeros((x_prompt.shape[0], N_GROUPS, STATE_DIM), x_prompt.dtype)
        y_prompt, kp, vp, srp, sip = encoder_layer(y_prompt, c_prompt, None, None, s0, s0, p)
        y_sample, kss, vss, srs, sis = encoder_layer(y_sample, c_sample, cache_attn_k[l], cache_attn_v[l],
                                                     state_ssm_re[l], state_ssm_im[l], p)
        k_p.append(kp)
        v_p.append(vp)
        re_p.append(srp)
        im_p.append(sip)
        k_s.append(kss)
        v_s.append(vss)
        re_s.append(srs)
        im_s.append(sis)
    return (y_prompt, y_sample, jnp.stack(k_p), jnp.stack(v_p), jnp.stack(re_p), jnp.stack(im_p),
            jnp.stack(k_s), jnp.stack(v_s), jnp.stack(re_s), jnp.stack(im_s))
```

```python
import contextlib
import math
import numpy as np
import concourse.bass as bass
import concourse.mybir as mybir
from concourse.bass_utils import run_bass_kernel_spmd

F32 = mybir.dt.float32
BF16 = mybir.dt.bfloat16
I32 = mybir.dt.int32
AF = mybir.ActivationFunctionType
ALU = mybir.AluOpType

NCORES = 8
D = 1024
SEQ = 2048
TL = 256
NTILES = SEQ // TL
DFF = 2816
ALPHA = 2.0 ** 0.25
LN_EPS = 1e-5
TWO_PI = 2.0 * math.pi


class Prog:
    ENG = ['pe', 'act', 'dve', 'pool', 'sp']

    def __init__(self, nc):
        self.nc = nc
        self.streams = {e: [] for e in self.ENG}
        self.count = {e: 0 for e in self.ENG}
        self.waited = {e: {} for e in self.ENG}
        self.last_write = {}
        self.readers = {}
        self.dmasem_count = {}

    def _deps(self, eng, reads, writes):
        deps = []
        for k in reads:
            lw = self.last_write.get(k)
            if lw is not None:
                deps.append(lw)
        for k in writes:
            lw = self.last_write.get(k)
            if lw is not None:
                deps.append(lw)
            rd = self.readers.get(k)
            if rd:
                for src, val in rd.items():
                    if src != eng or eng != 'pe':
                        deps.append((src, val))
        need = {}
        for src, val in deps:
            if src == eng and eng == 'pe':
                continue
            if self.waited[eng].get(src, 0) >= val:
                continue
            if need.get(src, 0) < val:
                need[src] = val
        for src, val in need.items():
            self.waited[eng][src] = val
            self.streams[eng].append(('wait', src, val))

    def _commit(self, src, val, reads, writes):
        for k in writes:
            self.last_write[k] = (src, val)
            self.readers[k] = {}
        for k in reads:
            d = self.readers.setdefault(k, {})
            if d.get(src, 0) < val:
                d[src] = val

    def op(self, eng, fn, reads=(), writes=()):
        writes = list(writes) + [k for k in reads if k.startswith('ps') and k[2:].isdigit() and k not in writes]
        self._deps(eng, reads, writes)
        self.count[eng] += 1
        self.streams[eng].append(('op', fn))
        self._commit(eng, self.count[eng], reads, writes)

    def dma(self, eng, fn, semkey, reads=(), writes=()):
        self._deps(eng, reads, writes)
        prev = self.dmasem_count.get(semkey, 0)
        if prev and self.waited[eng].get('dma:' + semkey, 0) < prev:
            self.waited[eng]['dma:' + semkey] = prev
            self.streams[eng].append(('wait', 'dma:' + semkey, prev))
        val = self.dmasem_count.get(semkey, 0) + 16
        self.dmasem_count[semkey] = val
        self.streams[eng].append(('dma', fn, semkey))
        self._commit('dma:' + semkey, val, reads, writes)

    def barrier(self, skip_prefix=None):
        for e in self.ENG:
            self.wait_all(e, skip_prefix)

    def wait_all(self, eng, skip_prefix=None):
        for e in self.ENG:
            if e != eng and self.count[e] > self.waited[eng].get(e, 0):
                self.streams[eng].append(('wait', e, self.count[e]))
                self.waited[eng][e] = self.count[e]
        for k, v in self.dmasem_count.items():
            s = 'dma:' + k
            if skip_prefix and k.startswith(skip_prefix):
                continue
            if v > self.waited[eng].get(s, 0):
                self.streams[eng].append(('wait', s, v))
                self.waited[eng][s] = v

    def emit(self):
        nc = self.nc
        with contextlib.ExitStack() as es:
            sems = {}
            for e in self.ENG:
                sems[e] = es.enter_context(nc.semaphore('s_' + e))
            for k in self.dmasem_count:
                sems['dma:' + k] = es.enter_context(nc.semaphore('d_' + k))
            block = es.enter_context(nc.Block())
            streams = self.streams

            def run(engname):
                def f(eng):
                    for it in streams[engname]:
                        if it[0] == 'wait':
                            eng.wait_ge(sems[it[1]], it[2])
                        elif it[0] == 'op':
                            it[1](eng).then_inc(sems[engname], 1)
                        else:
                            it[1](eng).then_inc(sems['dma:' + it[2]], 16)
                return f
            block.tensor(run('pe'))
            block.scalar(run('act'))
            block.vector(run('dve'))
            block.gpsimd(run('pool'))
            block.sync(run('sp'))


ATT_W = 1
SSM_F = 40
SSM_G = 12
SSM_W = 1
YLAG = 1


def limited(g, n):
    for i, _ in enumerate(g):
        if i >= n:
            return
        yield


def interleave(gens, weights):
    alive = list(gens)
    ws = list(weights)
    while alive:
        for i in range(len(alive) - 1, -1, -1):
            pass
        nxt = []
        nws = []
        for g, w in zip(alive, ws):
            ok = True
            for _ in range(w):
                try:
                    next(g)
                except StopIteration:
                    ok = False
                    break
            if ok:
                nxt.append(g)
                nws.append(w)
        alive, ws = nxt, nws


def dap(t, offset, ap):
    return bass.AP(tensor=t.tensor, offset=offset, ap=ap)


def build_nc():
    nc = bass.Bass("TRN2", target_bir_lowering=False)

    def din(name, shape):
        return nc.dram_tensor(name, shape, F32, kind="ExternalInput").ap()

    def dout(name, shape):
        return nc.dram_tensor(name, shape, F32, kind="ExternalOutput").ap()

    def dscr(name, shape, dt):
        return nc.dram_tensor(name, shape, dt, kind="Internal").ap()

    xp = din("xp", [2, SEQ, D]); xs = din("xs", [4, 64, D]); cc = din("cc", [6, D])
    ck = din("ck", [4, 512, 512]); cv = din("cv", [4, 512, 512])
    sre = din("sre", [4, 32, 64]); sim = din("sim", [4, 32, 64])
    w_ada = din("w_ada", [D, 6 * D]); b_ada = din("b_ada", [6 * D])
    w_in = din("w_in", [D, 4096]); rel_bias = din("rel_bias", [8, 513])
    a_re = din("a_re", [32, 64]); a_im = din("a_im", [32, 64]); log_dt = din("log_dt", [32])
    b_re = din("b_re", [32, 64, 16]); b_im = din("b_im", [32, 64, 16])
    c_re = din("c_re", [32, 16, 64]); c_im = din("c_im", [32, 16, 64]); ssm_d = din("ssm_d", [512])
    w_attn = din("w_attn", [512, D]); w_glu = din("w_glu", [512, 2 * D]); w_out = din("w_out", [D, D])
    ln1_g = din("ln1_g", [D]); ln1_b = din("ln1_b", [D])
    w_ffn_in = din("w_ffn_in", [D, 2 * DFF]); w_ffn_out = din("w_ffn_out", [DFF, D])
    ln2_g = din("ln2_g", [D]); ln2_b = din("ln2_b", [D])

    yp = dout("yp", [2, SEQ, D]); ys = dout("ys", [4, 64, D])
    kp = dout("kp", [2, 512, 512]); vp = dout("vp", [2, 512, 512])
    rep = dout("rep", [2, 32, 64]); imp = dout("imp", [2, 32, 64])
    ks = dout("ks", [4, 64, 512]); vs = dout("vs", [4, 64, 512])
    res = dout("res", [4, 32, 64]); ims = dout("ims", [4, 32, 64])

    s_ada = dscr("s_ada", [12, 128, 8, 512], BF16)
    s_in = dscr("s_in", [8, 128, 8, 512], BF16)
    s_attn = dscr("s_attn", [1, 128, 4, 1024], BF16)
    s_glu = dscr("s_glu", [2, 128, 4, 1024], BF16)
    s_out = dscr("s_out", [2, 128, 8, 512], BF16)
    s_fin = dscr("s_fin", [11, 128, 8, 512], BF16)
    s_fout = dscr("s_fout", [8, 128, 6, 512], BF16)
    ext_d = dscr("ext_d", [8, 128, 768], BF16)
    mod_d = dscr("mod_d", [6, 6 * D], F32)

    es = contextlib.ExitStack()
    with es:
        def sb(name, shape, dt):
            return es.enter_context(nc.sbuf_tensor(name, shape, dt))

        P = Prog(nc)

        class StopBuild(Exception):
            pass

        def ckpt(n):
            P.barrier()
            if DBG['stop'] == n:
                raise StopBuild()

        def tck(n):
            if DBG['stop'] == n:
                P.barrier()
                raise StopBuild()
        ident = sb("ident", [128, 128], F32)
        xt = sb("xt", [128, 4, D], F32)
        xn = sb("xn", [128, 1, D], F32)
        act8 = sb("act8", [128, 8, 512], BF16)
        R1 = sb("R1", [128, 14336], BF16)
        qT = R1[:, 0:2048].rearrange("p (k n) -> p k n", n=512)
        uT = R1[:, 12288:14336].rearrange("p (k n) -> p k n", n=512)
        sga = R1[:, 4096:8192].rearrange("p (k n) -> p k n", n=512)
        sgb = R1[:, 8192:12288].rearrange("p (k n) -> p k n", n=512)
        oT = R1[:, 2048:4096].rearrange("p (k n) -> p k n", n=512)
        actT = R1[:, 0:11264].rearrange("p (k n) -> p k n", n=512)
        kring = sb("kring", [128, 4, 12, 128], BF16)
        vring = sb("vring", [128, 12, 512], BF16)
        wsl = [sb("wsl%d" % i, [128, 4096], BF16) for i in range(2)]
        Eh = sb("Eh", [128, 8, 640], BF16)
        pt = sb("pt", [128, 2, 640], BF16)
        lnbc = sb("lnbc", [128, 2, D], F32)
        gbc = sb("gbc", [128, 1, 2, D], F32)
        modT = sb("modT", [128, 48, 6], F32)
        small = sb("small", [128, 4, 8], F32)
        bnst = sb("bnst", [128, 4, 12], F32)
        small2 = sb("small2", [128, 4, 8], F32)
        bnst2 = sb("bnst2", [128, 4, 12], F32)
        epsT = sb("epsT", [128, 1], F32)
        npiT = sb("npiT", [128, 1], F32)
        oneT = sb("oneT", [128, 1], F32)
        Ctab = sb("Ctab", [128, 16, 256], F32)
        Stab = sb("Stab", [128, 16, 256], F32)
        rco = sb("rco", [128, 16], F32)
        cend = sb("cend", [128, 2, 3, 16], F32)
        BbT = sb("BbT", [128, 16, 2, 128], BF16)
        CTw = sb("CTw", [128, 16, 2, 128], BF16)
        Dcol = sb("Dcol", [128, 4], F32)
        stre = sb("stre", [128, 16, 4], F32)
        stim = sb("stim", [128, 16, 4], F32)
        ssmf = sb("ssmf", [128, 8, 512], F32)
        ssmb = sb("ssmb", [128, 2, 2, 512], BF16)
        gT = sb("gT", [128, 4, 512], BF16)
        knew = qT[:, 0:4, 256:512]
        tmpa = R1[:, 12288:14336].bitcast(F32).rearrange("p (a n) -> p a n", n=512)
        tmpb = sb("tmpb", [128, 2, 512], BF16)
        rden = tmpb[:, :, :].rearrange("p a n -> p (a n)").bitcast(F32)
        stg2 = sb("stg2", [128, 1, 512], F32)
        ident_bf = sb("ident_bf", [128, 128], BF16)
        stg = stg2[:, 0, :]
        ones_bf = sb("ones_bf", [128, 64], BF16)

        PS = [es.enter_context(nc.psum_tensor("PS%d" % i, [128, 1024], F32)) for i in range(4)]

        def bank(i):
            return PS[i // 2][:, (i % 2) * 512:(i % 2) * 512 + 512]

        bank_rr = [0]

        bank_allowed = [list(range(8))]

        def nbank():
            while True:
                b = bank_rr[0]
                bank_rr[0] = (b + 1) % 8
                if b in bank_allowed[0]:
                    return b, bank(b), 'ps%d' % b

        try:
            P.op('pool', lambda e: e.memset(ident[:], 0.0), writes=['ident'])
            P.op('pool', lambda e: e.affine_select(out=ident[:], in_=ident[:], pattern=[[-1, 128]],
                                                   compare_op=ALU.not_equal, fill=1.0, base=0, channel_multiplier=1),
                 reads=['ident'], writes=['ident'])
            P.op('dve', lambda e: e.tensor_copy(out=ident_bf[:], in_=ident[:]), reads=['ident'], writes=['identb'])
            P.op('dve', lambda e: e.memset(epsT[:], LN_EPS), writes=['epsT'])
            P.op('dve', lambda e: e.memset(ones_bf[:], 1.0), writes=['ones'])
            P.op('dve', lambda e: e.memset(stre[:].rearrange('p j s -> p (j s)'), 0.0), writes=['stre'])
            P.op('dve', lambda e: e.memset(stim[:].rearrange('p j s -> p (j s)'), 0.0), writes=['stim'])
            P.op('dve', lambda e: e.memset(npiT[:], -math.pi), writes=['npiT'])
            P.op('dve', lambda e: e.memset(oneT[:], 1.0), writes=['oneT'])
            cast_jobs = {}

            def cast_blk(w, N, kcn, kc0, col0, bc, dst, dkey, dcol0=0, dcols=None):
                src = dap(w, kc0 * 128 * N + col0, [[N, 128], [128 * N, kcn], [1, bc]])
                d = dst if dcols is None else dst[:, :, dcol0:dcol0 + dcols]
                cast_jobs.setdefault(dkey, []).append(
                    lambda d=d, src=src, dkey=dkey, dcol0=dcol0: P.dma('pool', lambda e: e.dma_start(out=d, in_=src),
                                                                        'cast_' + dkey + ('_%d' % dcol0), writes=[dkey]))

            def issue_cast(dkey):
                for f_ in cast_jobs.pop(dkey, []):
                    f_()

            ada_slots = [R1[:, 0:4096], R1[:, 4096:8192], R1[:, 8192:12288], act8[:].rearrange('p k n -> p (k n)'),
                         xt[:, 0:2, :].rearrange('p a n -> p (a n)').bitcast(BF16)[:, 0:4096], xt[:, 2:4, :].rearrange('p a n -> p (a n)').bitcast(BF16)[:, 0:4096],
                         Ctab[:].rearrange('p a n -> p (a n)').bitcast(BF16)[:, 0:4096], Ctab[:].rearrange('p a n -> p (a n)').bitcast(BF16)[:, 4096:8192],
                         Stab[:].rearrange('p a n -> p (a n)').bitcast(BF16)[:, 0:4096], Stab[:].rearrange('p a n -> p (a n)').bitcast(BF16)[:, 4096:8192],
                         kring[:].rearrange('p a b c -> p (a b c)')[:, 0:4096], vring[:].rearrange('p a n -> p (a n)')[:, 0:4096]]
            ada_views = []
            for b in range(12):
                v_ = ada_slots[b].rearrange('p (k n) -> p k n', n=512)
                src_ = dap(w_ada, b * 512, [[6 * D, 128], [128 * 6 * D, 8], [1, 512]])
                P.dma('pool', lambda e, v_=v_, src_=src_: e.dma_start(out=v_, in_=src_), 'adaL%d' % b, writes=['adaS%d' % b])
                ada_views.append((v_, 'adaS%d' % b))
            for b in (3, 0, 1, 2, 4, 5, 6, 7):
                cast_blk(w_in, 4096, 8, 0, b * 512, 512, s_in[b], 's_in%d' % b)
                issue_cast('s_in%d' % b)
            wcnt = [0]

            def load_w(scr, blk, kcn, bc, skey):
                i = wcnt[0] % 2
                wcnt[0] += 1
                view = wsl[i][:, 0:kcn * bc].rearrange("p (k n) -> p k n", n=bc)
                key = 'wsl%d' % i
                P.dma('sp', lambda e, view=view, src=scr[blk][:, 0:kcn, :]: e.dma_start(out=view, in_=src), key,
                      reads=[skey], writes=[key])
                return view, key

            W_IN = {b: (s_in, b, 8, 512, 's_in%d' % b) for b in range(8)}
            W_FRONT = ([W_IN[b] for b in (0, 1, 2, 4, 5, 6, 7)] + [(s_attn, 0, 4, 1024, 's_attn0')] +
                       [(s_glu, b, 4, 1024, 's_glu%d' % b) for b in range(2)] +
                       [(s_out, b, 8, 512, 's_out%d' % b) for b in range(2)])
            W_BACK = ([(s_fin, b, 8, 512, 's_fin%d' % b) for b in range(11)] +
                      [(s_fout, b, (6, 5, 6, 5)[b % 4], 512, 's_fout%d' % b) for b in range(8)])
            worder = []

            def build_wseq(ntl, early):
                worder.append(W_IN[3])
                for i_ in range(ntl):
                    worder.extend(W_FRONT)
                    if i_ + 1 < ntl:
                        if early:
                            worder.append(W_IN[3])
                    worder.extend(W_BACK)
                    if i_ + 1 < ntl and not early:
                        worder.append(W_IN[3])
            wq = {'pending': None, 'pos': 0}

            CAST_AHEAD = 6

            def next_w(prefetch=True):
                for m_ in range(wq['pos'], min(len(worder), wq['pos'] + CAST_AHEAD)):
                    issue_cast(worder[m_][4])
                if wq['pending'] is None:
                    wq['pending'] = load_w(*worder[wq['pos']])
                cur = wq['pending']
                wq['pending'] = None
                wq['pos'] += 1
                if prefetch:
                    prefetch_w()
                return cur

            def prefetch_w():
                if wq['pending'] is None and wq['pos'] < len(worder):
                    wq['pending'] = load_w(*worder[wq['pos']])

            csb = tmpa[0:6, :, :].rearrange("p a n -> p (a n)")
            P.dma('sp', lambda e: e.dma_start(out=csb, in_=cc), 'cc', writes=['tmpa'])
            P.op('act', lambda e: e.activation(out=csb, in_=csb, func=AF.Silu), reads=['tmpa'], writes=['tmpa'])
            b0, pb0, k0 = 6, bank(6), 'ps6'
            def f_ct(e):
                ins = None
                for kc in range(8):
                    ins = e.transpose(pb0[:, kc * 6:kc * 6 + 6], csb[:, kc * 128:(kc + 1) * 128], ident[0:6, 0:6])
                return ins
            P.op('pe', f_ct, reads=['tmpa', 'ident'], writes=[k0])
            scT = tmpb[:, 0, 0:48].rearrange("p (k r) -> p k r", r=6)
            P.op('dve', lambda e: e.tensor_copy(out=scT, in_=pb0[:, 0:48].rearrange("p (k r) -> p k r", r=6)),
                 reads=[k0], writes=['tmpb'])
            mst = stg[0:6, :]
            bst = xn[0:6, 0, 0:512]
            bT, pbT, kT_ = 7, bank(7), 'ps7'
            for blk in range(12):
                wv, wk = ada_views[blk]
                b1, pb1, k1 = blk % 4, bank(blk % 4), 'ps%d' % (blk % 4)
                def f_mm(e, wv=wv, pb1=pb1):
                    ins = None
                    for kc in range(8):
                        ins = e.matmul(pb1[0:6, :], lhsT=scT[:, kc, :], rhs=wv[:, kc, :], start=(kc == 0), stop=(kc == 7))
                    return ins
                P.op('pe', f_mm, reads=['tmpb', wk], writes=[k1])
                P.dma('sp', lambda e, blk=blk: e.dma_start(out=bst, in_=dap(b_ada, blk * 512, [[0, 6], [1, 512]])),
                      'bst', writes=['bst'])
                P.op('dve', lambda e, pb1=pb1: e.tensor_tensor(out=mst, in0=pb1[0:6, :], in1=bst, op=ALU.add),
                     reads=[k1, 'bst'], writes=['mst'])
                P.dma('sp', lambda e, blk=blk: e.dma_start(out=mod_d[:, blk * 512:(blk + 1) * 512], in_=mst),
                      'modd', reads=['mst'], writes=['mod_d'])
                def f_tp(e, blk=blk):
                    ins = None
                    for q in range(4):
                        ft = blk * 4 + q
                        ins = e.transpose(pbT[:, ft * 6:ft * 6 + 6], mst[:, q * 128:(q + 1) * 128], ident[0:6, 0:6])
                    return ins
                P.op('pe', f_tp, reads=['mst', 'ident'], writes=[kT_])
            P.op('dve', lambda e: e.tensor_copy(out=modT[:].rearrange("p k r -> p (k r)"), in_=pbT[:, 0:288]),
                 reads=[kT_], writes=['modT'])
            for sec in (1, 4):
                P.op('dve', lambda e, sec=sec: e.tensor_scalar(out=modT[:, sec * 8:sec * 8 + 8, :], in0=modT[:, sec * 8:sec * 8 + 8, :],
                                                                scalar1=1.0, scalar2=None, op0=ALU.add),
                     reads=['modT'], writes=['modT'])

            ckpt(1)
            rbs = tmpa[0:8, 0, :]
            ext = tmpa[0:8, 1, :]
            ext = ssmf[0:8, 0:2, :].rearrange("p a n -> p (a n)")[:, 0:768]
            extb = tmpb[0:8, :, :].rearrange("p a n -> p (a n)")[:, 0:768]
            P.dma('sp', lambda e: e.dma_start(out=ext[:, 0:384], in_=rel_bias[:, 129:513]), 'rb', writes=['ext'])
            rlast = tmpa[0:8, 0, 0:1]
            P.dma('sp', lambda e: e.dma_start(out=rlast, in_=dap(rel_bias, 512, [[513, 8], [1, 1]]), allow_slow_non_contiguous=True), 'rb2', writes=['rlast'])
            P.op('dve', lambda e: e.tensor_copy(out=ext[:, 384:768], in_=rlast.to_broadcast([8, 384])), reads=['rlast'], writes=['ext2'])
            P.op('act', lambda e: e.activation(out=extb, in_=ext, func=AF.Copy, scale=8.0), reads=['ext', 'ext2', 'tmpb'], writes=['tmpb'])
            P.dma('sp', lambda e: e.dma_start(out=ext_d, in_=extb.unsqueeze(1).to_broadcast([8, 128, 768])), 'extd', reads=['tmpb'], writes=['ext_d'])
            P.dma('sp', lambda e: e.dma_start(out=Eh[:, :, :], in_=dap(ext_d, 127, [[767, 128], [128 * 768, 8], [1, 640]])), 'ehrow',
                  reads=['ext_d'], writes=['Eh'])
            P.op('pool', lambda e: e.memset(Eh[0:64, :, 576:640], -30000.0), reads=['Eh'], writes=['Eh'])
            P.op('pool', lambda e: e.memset(Eh[64:128, :, 0:64], -30000.0), reads=['Eh'], writes=['Eh'])

            ckpt(2)
            sA = ssmf[:, 2, :]
            are = sA[:, 0:16]; aim = sA[:, 16:32]; dtt = sA[:, 32:48]; th = sA[:, 48:64]
            fre = sA[:, 64:80]; fim = sA[:, 80:96]; den = sA[:, 96:112]; t0 = sA[:, 112:128]; t1_ = sA[:, 128:144]
            abr = sA[:, 144:160]; abi = sA[:, 160:176]
            A2 = ssmf[0:32, 3, 0:256]
            A2v = A2.rearrange("g (a d p) -> g a d p", a=2, d=2)
            for ai, arr in enumerate((a_re, a_im)):
                for d_ in range(2):
                    P.dma('sp', lambda e, ai=ai, arr=arr, d_=d_: e.dma_start(out=A2v[:, ai, d_, :], in_=arr), 'ssmld', writes=['A2_%d%d' % (ai, d_)])
            for ai, dstA, nm in ((0, are, 'are'), (1, aim, 'aim')):
                pbA = bank(ai)
                P.op('pe', lambda e, ai=ai, pbA=pbA: e.transpose(pbA[:, 0:32], A2v[:, ai, :, :].rearrange("g d p -> g (d p)"), ident[0:32, 0:32]),
                     reads=['A2_%d0' % ai, 'A2_%d1' % ai, 'ident'], writes=['ps%d' % ai])
                for two in range(2):
                    P.op('dve', lambda e, pbA=pbA, dstA=dstA, two=two: e.tensor_copy(out=dstA[two * 64:(two + 1) * 64, :], in_=pbA[two * 64:(two + 1) * 64, two:32:2]),
                         reads=['ps%d' % ai], writes=[nm])
            Lbc = ssmf[:, 3, 256:288]
            P.dma('sp', lambda e: e.dma_start(out=Lbc, in_=dap(log_dt, 0, [[0, 128], [1, 32]])), 'ssmld3', writes=['Lbc'])
            for two in range(2):
                P.op('dve', lambda e, two=two: e.tensor_copy(out=dtt[two * 64:(two + 1) * 64, :], in_=Lbc[two * 64:(two + 1) * 64, two:32:2]),
                     reads=['Lbc'], writes=['dtt%d' % two])
            P.op('act', lambda e: e.activation(out=dtt, in_=dtt, func=AF.Exp), reads=['dtt0', 'dtt1'], writes=['dtt'])
            P.op('dve', lambda e: e.tensor_tensor(out=t0, in0=dtt, in1=are, op=ALU.mult), reads=['dtt', 'are'], writes=['t0'])
            P.op('act', lambda e: e.activation(out=rco[:], in_=t0, func=AF.Exp), reads=['t0'], writes=['rco'])
            P.op('dve', lambda e: e.tensor_tensor(out=th, in0=dtt, in1=aim, op=ALU.mult), reads=['dtt', 'aim'], writes=['th'])
            P.barrier()
            iot = ssmf[:, 3, 0:256]
            P.op('pool', lambda e: e.iota(iot, pattern=[[1, 256]], base=1, channel_multiplier=0,
                                          allow_small_or_imprecise_dtypes=True), reads=['ssmf3'], writes=['iot'])
            ang = ssmf[:, 4:6, :].rearrange("p a n -> p (a n)")
            kq = ssmf[:, 6:8, :].rearrange("p a n -> p (a n)")
            kqi = kq.bitcast(I32)
            mq = ssmf[:, 0:2, :].rearrange("p a n -> p (a n)")
            C1 = 6.28125
            C2 = float(np.float32(TWO_PI - C1))
            C3 = float(TWO_PI - C1 - C2)
            for jg in range(4):
                for jj in range(4):
                    j = jg * 4 + jj
                    P.op('dve', lambda e, j=j, jj=jj: e.tensor_scalar(out=ang[:, jj * 256:(jj + 1) * 256], in0=iot, scalar1=th[:, j:j + 1],
                                                                     scalar2=None, op0=ALU.mult),
                         reads=['iot', 'th'], writes=['ang'])
                P.op('dve', lambda e: e.tensor_scalar(out=kqi, in0=ang, scalar1=1.0 / TWO_PI, scalar2=None, op0=ALU.mult),
                     reads=['ang'], writes=['kq'])
                P.op('dve', lambda e: e.tensor_copy(out=kq, in_=kqi), reads=['kq'], writes=['kq'])
                P.op('dve', lambda e: e.scalar_tensor_tensor(out=ang, in0=kq, scalar=-C1, in1=ang, op0=ALU.mult, op1=ALU.add),
                     reads=['ang', 'kq'], writes=['ang'])
                P.op('dve', lambda e: e.scalar_tensor_tensor(out=ang, in0=kq, scalar=-C2, in1=ang, op0=ALU.mult, op1=ALU.add),
                     reads=['ang', 'kq'], writes=['ang'])
                for (tab, shift, nm) in ((Stab, 0.0, 'Stab'), (Ctab, math.pi / 2, 'Ctab')):
                    P.op('dve', lambda e, shift=shift: e.tensor_scalar(out=kq, in0=ang, scalar1=shift, scalar2=None, op0=ALU.add),
                         reads=['ang', 'Stab', 'Ctab'], writes=['kq'])
                    for (cmp_, thr, corr) in ((ALU.is_gt, math.pi, -TWO_PI), (ALU.is_lt, -math.pi, TWO_PI),
                                              (ALU.is_gt, math.pi, -TWO_PI)):
                        P.op('dve', lambda e, cmp_=cmp_, thr=thr: e.tensor_scalar(out=mq, in0=kq, scalar1=thr, scalar2=None, op0=cmp_),
                             reads=['kq'], writes=['mq'])
                        P.op('dve', lambda e, corr=corr: e.scalar_tensor_tensor(out=kq, in0=mq, scalar=corr, in1=kq, op0=ALU.mult, op1=ALU.add),
                             reads=['kq', 'mq'], writes=['kq'])
                    P.op('act', lambda e, tab=tab, jg=jg: e.activation(
                        out=tab[:, jg * 4:jg * 4 + 4, :].rearrange("p a n -> p (a n)"), in_=kq, func=AF.Sin),
                        reads=['kq'], writes=[nm])
            ckpt(3)
            for pt_i, tl in ((0, 256), (1, 64)):
                P.op('dve', lambda e, pt_i=pt_i, tl=tl: e.tensor_copy(out=cend[:, pt_i, 0, :], in_=Ctab[:, :, tl - 1]),
                     reads=['Ctab'], writes=['cend'])
                P.op('dve', lambda e, pt_i=pt_i, tl=tl: e.tensor_copy(out=cend[:, pt_i, 1, :], in_=Stab[:, :, tl - 1]),
                     reads=['Stab'], writes=['cend'])
                P.op('dve', lambda e, pt_i=pt_i, tl=tl: e.tensor_scalar(out=cend[:, pt_i, 2, :], in0=Stab[:, :, tl - 1],
                                                                       scalar1=-1.0, scalar2=None, op0=ALU.mult),
                     reads=['Stab'], writes=['cend'])
            P.op('dve', lambda e: e.tensor_tensor(out=abr, in0=rco[:], in1=Ctab[:, :, 0], op=ALU.mult), reads=['rco', 'Ctab'], writes=['abr'])
            P.op('dve', lambda e: e.tensor_tensor(out=abi, in0=rco[:], in1=Stab[:, :, 0], op=ALU.mult), reads=['rco', 'Stab'], writes=['abi'])
            P.op('dve', lambda e: e.tensor_scalar(out=abr, in0=abr, scalar1=-1.0, scalar2=None, op0=ALU.add), reads=['abr'], writes=['abr'])
            P.op('dve', lambda e: e.tensor_tensor(out=den, in0=are, in1=are, op=ALU.mult), reads=['are'], writes=['den'])
            P.op('dve', lambda e: e.tensor_tensor(out=t0, in0=aim, in1=aim, op=ALU.mult), reads=['aim', 'rco'], writes=['t0'])
            P.op('dve', lambda e: e.tensor_tensor(out=den, in0=den, in1=t0, op=ALU.add), reads=['den', 't0'], writes=['den'])
            P.op('dve', lambda e: e.reciprocal(out=den, in_=den), reads=['den'], writes=['den'])
            P.op('dve', lambda e: e.tensor_tensor(out=fre, in0=abr, in1=are, op=ALU.mult), reads=['abr', 'are'], writes=['fre'])
            P.op('dve', lambda e: e.tensor_tensor(out=t0, in0=abi, in1=aim, op=ALU.mult), reads=['abi', 'aim', 'den'], writes=['t0'])
            P.op('dve', lambda e: e.tensor_tensor(out=fre, in0=fre, in1=t0, op=ALU.add), reads=['fre', 't0'], writes=['fre'])
            P.op('dve', lambda e: e.tensor_tensor(out=fre, in0=fre, in1=den, op=ALU.mult), reads=['fre', 'den'], writes=['fre'])
            P.op('dve', lambda e: e.tensor_tensor(out=fim, in0=abi, in1=are, op=ALU.mult), reads=['abi', 'are'], writes=['fim'])
            P.op('dve', lambda e: e.tensor_tensor(out=t0, in0=abr, in1=aim, op=ALU.mult), reads=['abr', 'aim', 'fre'], writes=['t0'])
            P.op('dve', lambda e: e.tensor_tensor(out=fim, in0=fim, in1=t0, op=ALU.subtract), reads=['fim', 't0'], writes=['fim'])
            P.op('dve', lambda e: e.tensor_tensor(out=fim, in0=fim, in1=den, op=ALU.mult), reads=['fim', 'den'], writes=['fim'])
            ckpt(4)
            Bre = ssmf[:, 4, 0:256].rearrange("p (j m) -> p j m", m=16)
            Bim = ssmf[:, 5, 0:256].rearrange("p (j m) -> p j m", m=16)
            bbr = ssmf[:, 6, 0:256].rearrange("p (j m) -> p j m", m=16)
            bbi = ssmf[:, 7, 0:256].rearrange("p (j m) -> p j m", m=16)
            btm = ssmf[:, 3, 256:512].rearrange("p (j m) -> p j m", m=16)
            P.dma('sp', lambda e: e.dma_start(out=Bre, in_=dap(b_re, 0, [[16, 128], [2048, 16], [1, 16]])), 'ssmld4',
                  reads=['Stab', 'Ctab', 'ang', 'kq'], writes=['Bre'])
            P.dma('sp', lambda e: e.dma_start(out=Bim, in_=dap(b_im, 0, [[16, 128], [2048, 16], [1, 16]])), 'ssmld5',
                  reads=['Stab', 'Ctab', 'ang', 'kq'], writes=['Bim'])
            freb = fre.unsqueeze(2).to_broadcast([128, 16, 16])
            fimb = fim.unsqueeze(2).to_broadcast([128, 16, 16])
            P.op('dve', lambda e: e.tensor_tensor(out=bbr, in0=Bre, in1=freb, op=ALU.mult), reads=['Bre', 'fre', 'kq'], writes=['bbr'])
            P.op('dve', lambda e: e.tensor_tensor(out=btm, in0=Bim, in1=fimb, op=ALU.mult), reads=['Bim', 'fim', 'iot'], writes=['btm'])
            P.op('dve', lambda e: e.tensor_tensor(out=bbr, in0=bbr, in1=btm, op=ALU.subtract), reads=['bbr', 'btm'], writes=['bbr'])
            P.op('dve', lambda e: e.tensor_tensor(out=bbi, in0=Bim, in1=freb, op=ALU.mult), reads=['Bim', 'fre', 'kq'], writes=['bbi'])
            P.op('dve', lambda e: e.tensor_tensor(out=btm, in0=Bre, in1=fimb, op=ALU.mult), reads=['Bre', 'fim', 'bbr'], writes=['btm'])
            P.op('dve', lambda e: e.tensor_tensor(out=bbi, in0=bbi, in1=btm, op=ALU.add), reads=['bbi', 'btm'], writes=['bbi'])
            ckpt(5)
            Mbig = xt[:, :, :].rearrange("p a n -> p (a n)")
            Mv = Mbig.rearrange("p (j r c) -> p j r c", r=2, c=128)
            P.op('pool', lambda e: e.memset(Mbig, 0.0), writes=['xt'])
            for ri, bb in ((0, bbr), (1, bbi)):
                for jj in range(4):
                    for two in range(2):
                        c0 = 32 * jj + 16 * two
                        P.op('dve', lambda e, ri=ri, bb=bb, jj=jj, two=two, c0=c0: e.tensor_copy(
                            out=Mv[two * 64:(two + 1) * 64, jj::4, ri, c0:c0 + 16], in_=bb[two * 64:(two + 1) * 64, jj::4, :]),
                            reads=['bbr', 'bbi', 'xt'], writes=['xt'])
            ckpt(6)
            for j in range(16):
                b2, pb2, k2 = nbank()
                def f_t2(e, j=j, pb2=pb2):
                    e.transpose(pb2[:, 0:128], Mv[:, j, 0, :], ident[:])
                    return e.transpose(pb2[:, 128:256], Mv[:, j, 1, :], ident[:])
                P.op('pe', f_t2, reads=['xt', 'ident'], writes=[k2])
                P.op('act', lambda e, j=j, pb2=pb2: e.activation(out=BbT[:, j, :, :].rearrange("p r c -> p (r c)"), in_=pb2[:, 0:256], func=AF.Copy),
                     reads=[k2], writes=['BbT'])
            ckpt(7)
            Cn = xt[:, 0:2, :].rearrange("p a n -> p (a n)")
            Cnv = Cn[:, 0:1024].rearrange("p (r f d q) -> p r f d q", r=2, f=4, d=2)
            for ri, cw in ((0, c_re), (1, c_im)):
                for d_ in range(2):
                    P.dma('sp', lambda e, ri=ri, cw=cw, d_=d_: e.dma_start(out=Cnv[:, ri, :, d_, :], in_=dap(cw, 0, [[64, 128], [8192, 4], [1, 64]])),
                          'cld', reads=['bst', 'BbT'], writes=['Cn%d%d' % (ri, d_)])
            ckpt(8)
            CTall = ssmf[:, 4:6, :].rearrange("p a n -> p (a n)").rearrange("p (r f c) -> p r f c", r=2, f=4)
            for ri in range(2):
                b3, pb3, k3 = nbank()
                def f_t3(e, ri=ri, pb3=pb3):
                    ins = None
                    for ft in range(4):
                        ins = e.transpose(pb3[:, ft * 128:(ft + 1) * 128], Cnv[:, ri, ft, :, :].rearrange("p d q -> p (d q)"), ident[:])
                    return ins
                P.op('pe', f_t3, reads=['Cn%d0' % ri, 'Cn%d1' % ri, 'ident'], writes=[k3])
                P.op('act', lambda e, ri=ri, pb3=pb3: e.activation(out=CTall[:, ri, :, :].rearrange("p f c -> p (f c)"), in_=pb3,
                                                                  func=AF.Copy, scale=(1.0 if ri == 0 else -1.0)),
                     reads=[k3, 'bbr', 'bbi', 'Bre', 'Bim'], writes=['CTall'])
            ckpt(9)
            P.op('pool', lambda e: e.memset(CTw[:].rearrange("p j r c -> p (j r c)"), 0.0), writes=['CTw'])
            CTv = CTw[:].rearrange("p (f q) r c -> p f q r c", q=4)
            for ri in range(2):
                for jj in range(4):
                    for two in range(2):
                        c0 = 32 * jj + 16 * two
                        g0 = (2 * jj + two) * 16
                        P.op('dve', lambda e, ri=ri, jj=jj, two=two, c0=c0, g0=g0: e.tensor_copy(
                            out=CTv[two * 64:(two + 1) * 64, :, jj, ri, c0:c0 + 16], in_=CTall[two * 64:(two + 1) * 64, ri, :, g0:g0 + 16]),
                            reads=['CTall', 'CTw'], writes=['CTw'])
            ckpt(10)
            D4 = ssmf[0:4, 0, 0:128]
            P.dma('sp', lambda e: e.dma_start(out=D4, in_=ssm_d.rearrange("(f p) -> f p", p=128)), 'dcol', writes=['D4'])
            P.op('pe', lambda e: e.transpose(bank(0)[:, 0:4], D4, ident[0:4, 0:4]), reads=['D4', 'ident'], writes=['ps0'])
            P.op('dve', lambda e: e.tensor_copy(out=Dcol[:], in_=bank(0)[:, 0:4]), reads=['ps0'], writes=['Dcol'])
            cast_blk(w_attn, D, 4, 0, 0, 1024, s_attn[0], 's_attn0')
            for b in range(2):
                cast_blk(w_glu, 2 * D, 4, 0, b * 512, 512, s_glu[b], 's_glu%d' % b, 0, 512)
                cast_blk(w_glu, 2 * D, 4, 0, D + b * 512, 512, s_glu[b], 's_glu%d' % b, 512, 512)
            for b in range(2):
                cast_blk(w_out, D, 8, 0, b * 512, 512, s_out[b], 's_out%d' % b)
            for b in range(11):
                cast_blk(w_ffn_in, 2 * DFF, 8, 0, b * 256, 256, s_fin[b], 's_fin%d' % b, 0, 256)
                cast_blk(w_ffn_in, 2 * DFF, 8, 0, DFF + b * 256, 256, s_fin[b], 's_fin%d' % b, 256, 256)
            KPARTS = [(0, 6), (6, 5), (11, 6), (17, 5)]
            for cb in range(2):
                for kh, (k0_, kn_) in enumerate(KPARTS):
                    cast_blk(w_ffn_out, D, kn_, k0_, cb * 512, 512, s_fout[cb * 4 + kh][:, 0:kn_, :], 's_fout%d' % (cb * 4 + kh))

            ckpt(11)
            def ln_stats(tt, src_key, src=None, sm=None, bs=None, kp=''):
                src = xt[:, tt, :] if src is None else src
                sm = small if sm is None else sm
                bs = bnst if bs is None else bs
                kb, ks_ = 'bnst%s%d' % (kp, tt), 'small%s%d' % (kp, tt)
                for h in range(2):
                    P.op('dve', lambda e, tt=tt, h=h: e.bn_stats(out=bs[:, tt, h * 6:h * 6 + 6], in_=src[:, h * 512:(h + 1) * 512]),
                         reads=[src_key], writes=[kb])
                P.op('dve', lambda e, tt=tt: e.bn_aggr(out=sm[:, tt, 0:2], in_=bs[:, tt, :]), reads=[kb], writes=[ks_])
                P.op('act', lambda e, tt=tt: e.activation(out=sm[:, tt, 2:3], in_=sm[:, tt, 1:2], func=AF.Sqrt, bias=epsT[:], scale=1.0),
                     reads=[ks_, 'epsT'], writes=[ks_])
                P.op('dve', lambda e, tt=tt: e.reciprocal(out=sm[:, tt, 3:4], in_=sm[:, tt, 2:3]), reads=[ks_], writes=[ks_])
                P.op('dve', lambda e, tt=tt: e.scalar_tensor_tensor(out=sm[:, tt, 4:5], in0=sm[:, tt, 0:1], scalar=-1.0,
                                                                   in1=sm[:, tt, 3:4], op0=ALU.mult, op1=ALU.mult),
                     reads=[ks_], writes=[ks_])

            def ln_to_featmajor(tt, ntt, segs, sh_sec, sc_sec, src_key, src=None, sm=None, kp='', dest=None, dkeys=('act8',)):
                src = xt[:, tt, :] if src is None else src
                sm = small if sm is None else sm
                dest = act8 if dest is None else dest
                dkeys = list(dkeys)
                ks_ = 'small%s%d' % (kp, tt)
                P.op('act', lambda e, tt=tt: e.activation(out=xn[:, 0, :], in_=src, func=AF.Identity,
                                                         scale=sm[:, tt, 3:4], bias=sm[:, tt, 4:5]),
                     reads=[src_key, ks_, 'xn0', 'xn0b'], writes=['xn0', 'xn0b'])
                nev = 0
                for half in range(2):
                    b, pb, k = nbank()
                    def f_t(e, half=half, pb=pb):
                        ins = None
                        for q in range(4):
                            kc = half * 4 + q
                            ins = e.transpose(pb[:, q * 128:(q + 1) * 128], xn[:, 0, kc * 128:(kc + 1) * 128], ident[:])
                        return ins
                    P.op('pe', f_t, reads=['xn0', 'xn0b', 'ident'], writes=[k])
                    for q in range(4):
                        kc = half * 4 + q
                        for (c0, c1, r) in segs:
                            nev += 1
                            if nev % 4 != 1:
                                P.op('act', lambda e, pb=pb, q=q, kc=kc, c0=c0, c1=c1, r=r, tt=tt: e.activation(
                                    out=dest[:, kc, tt * 128 + c0:tt * 128 + c1], in_=pb[:, q * 128 + c0:q * 128 + c1], func=AF.Identity,
                                    scale=modT[:, sc_sec * 8 + kc, r:r + 1], bias=modT[:, sh_sec * 8 + kc, r:r + 1]),
                                    reads=[k, 'modT'], writes=dkeys)
                            else:
                                P.op('dve', lambda e, pb=pb, q=q, kc=kc, c0=c0, c1=c1, r=r, tt=tt: e.tensor_scalar(
                                    out=dest[:, kc, tt * 128 + c0:tt * 128 + c1], in0=pb[:, q * 128 + c0:q * 128 + c1],
                                    scalar1=modT[:, sc_sec * 8 + kc, r:r + 1], scalar2=modT[:, sh_sec * 8 + kc, r:r + 1], op0=ALU.mult, op1=ALU.add),
                                    reads=[k, 'modT'], writes=dkeys)

            def ln_affine(tt, gi, bi_):
                P.op('act', lambda e, tt=tt: e.activation(out=xt[:, tt, :], in_=xt[:, tt, :], func=AF.Identity,
                                                         scale=small[:, tt, 3:4], bias=small[:, tt, 4:5]),
                     reads=['xt%d' % tt, 'small%d' % tt], writes=['xt%d' % tt])
                for (eng_, c0_, c1_) in (('dve', 0, 512), ('pool', 512, 1024)):
                    hk = 'xt%d%s' % (tt, 'a' if c0_ == 0 else 'b')
                    P.op(eng_, lambda e, tt=tt, c0_=c0_, c1_=c1_: e.tensor_tensor(out=xt[:, tt, c0_:c1_], in0=xt[:, tt, c0_:c1_], in1=lnbc[:, 0, c0_:c1_], op=ALU.mult),
                         reads=['xt%d' % tt, 'lnbc'], writes=[hk])
                    P.op(eng_, lambda e, tt=tt, c0_=c0_, c1_=c1_: e.tensor_tensor(out=xt[:, tt, c0_:c1_], in0=xt[:, tt, c0_:c1_], in1=lnbc[:, 1, c0_:c1_], op=ALU.add),
                         reads=[hk, 'lnbc'], writes=[hk])
                P.op('dve', lambda e, tt=tt: e.memset(small[:, tt, 6:7], 0.0), reads=['xt%da' % tt, 'xt%db' % tt], writes=['xt%d' % tt])

            def fence(key):
                P.op('dve', lambda e: e.memset(small[:, 0, 7:8], 0.0), writes=[key])

            stgc = [0]

            def tile_info(kind):
                prompt_ = (kind == 'p')
                if prompt_:
                    return 4, {tt: [(0, 128, tt // 2)] for tt in range(4)}
                return 2, {tt: [(0, 64, 2 + 2 * tt), (64, 128, 3 + 2 * tt)] for tt in range(2)}

            def x_rows(kind, ti, tt):
                if kind == 'p':
                    s_, hf = tt // 2, tt % 2
                    return xp[s_, ti * TL + hf * 128: ti * TL + hf * 128 + 128, :]
                return xs[2 * tt:2 * tt + 2, :, :].rearrange("s t d -> (s t) d")

            HTU_KEYS = ['f0', 'f1', 'sr0', 'si0']

            def gen_ln0(kind2, ti2, dest=None, dkeys=('act8',)):
                ntt2, segs2 = tile_info(kind2)
                for tt in range(ntt2):
                    src = x_rows(kind2, ti2, tt)
                    P.dma('act', lambda e, src=src: e.dma_start(out=xn[:, 0, :], in_=src), 'xnld', reads=['xn0b'], writes=['xn0', 'xn0b'])
                    yield
                    ln_stats(tt, 'xn0', src=xn[:, 0, :], sm=small2, bs=bnst2, kp='p')
                    yield
                    ln_to_featmajor(tt, ntt2, segs2[tt], 0, 1, 'xn0', src=xn[:, 0, :], sm=small2, kp='p', dest=dest, dkeys=dkeys)
                    yield

            def gen_ssm(kind):
                prompt = (kind == 'p')
                nseq = 2 if prompt else 4
                tlen = TL if prompt else 64
                ncols = nseq * tlen
                pti = 0 if prompt else 1
                if not prompt:
                    for (src_, dst_, nm) in ((sre, stre, 'stre'), (sim, stim, 'stim')):
                        for d_ in range(2):
                            P.dma('sp', lambda e, src_=src_, d_=d_: e.dma_start(out=stg[:, d_ * 64:(d_ + 1) * 64], in_=src_.rearrange("s g p -> (s g) p")),
                                  'stgin', reads=['stg'], writes=['stg'])
                        b, pb, k = nbank()
                        P.op('pe', lambda e, pb=pb: e.transpose(pb[:, 0:128], stg[:, 0:128], ident[:]), reads=['stg', 'ident'], writes=[k])
                        for two in range(2):
                            P.op('dve', lambda e, pb=pb, dst_=dst_, two=two: e.tensor_copy(
                                out=dst_[two * 64:(two + 1) * 64, :, :],
                                in_=pb[two * 64:(two + 1) * 64, 0:128].rearrange("p (s j t) -> p j s t", s=4, t=2)[:, :, :, two]),
                                reads=[k], writes=[nm])
                def v3(ap):
                    return ap[:, 0:ncols].rearrange("p (s t) -> p s t", t=tlen)
                T = [v3(ssmf[:, i, :]) for i in range(8)]
                ybank = {}
                pending = []

                def emit_y(j):
                    ft, jj = j // 4, j % 4
                    yb, pby, ky = ybank[ft]
                    sb_i = j % 2
                    def f_y(e, j=j, jj=jj, sb_i=sb_i, pby=pby):
                        e.matmul(pby[:, 0:ncols], lhsT=CTw[:, j, 0, :], rhs=ssmb[:, sb_i, 0, 0:ncols], start=(jj == 0), stop=False)
                        return e.matmul(pby[:, 0:ncols], lhsT=CTw[:, j, 1, :], rhs=ssmb[:, sb_i, 1, 0:ncols], start=False, stop=(jj == 3))
                    P.op('pe', f_y, reads=['ssmb%d' % sb_i, 'CTw'], writes=[ky])
                    if jj == 3 and not DBG.get('ssm_noepi'):
                        yv = ssmf[:, 0, 0:ncols]; wv_ = ssmf[:, 1, 0:ncols]
                        P.op('dve', lambda e, ft=ft, pby=pby, yv=yv: e.scalar_tensor_tensor(out=yv, in0=uT[:, ft, 0:ncols], scalar=Dcol[:, ft:ft + 1],
                                                                                         in1=pby[:, 0:ncols], op0=ALU.mult, op1=ALU.add),
                             reads=[ky, 'uT', 'Dcol', 'f0'], writes=['f0'])
                        P.op('act', lambda e, yv=yv, wv_=wv_: e.activation(out=wv_, in_=yv, func=AF.Square), reads=['f0'], writes=['f1'])
                        P.op('act', lambda e, wv_=wv_: e.activation(out=wv_, in_=wv_, func=AF.Identity, scale=0.044715, bias=oneT[:]),
                             reads=['f1', 'oneT'], writes=['f1'])
                        P.op('pool', lambda e, yv=yv, wv_=wv_: e.tensor_tensor(out=wv_, in0=wv_, in1=yv, op=ALU.mult), reads=['f1', 'f0'], writes=['f1'])
                        P.op('act', lambda e, wv_=wv_: e.activation(out=wv_, in_=wv_, func=AF.Sigmoid, scale=1.5957691216057308), reads=['f1'], writes=['f1'])
                        P.op('pool', lambda e, yv=yv, wv_=wv_, ft=ft: e.tensor_tensor(out=gT[:, ft, 0:ncols], in0=wv_, in1=yv, op=ALU.mult),
                             reads=['f1', 'f0'], writes=['gT'])

                for j in range(16):
                    ft, jj = j // 4, j % 4
                    if jj == 0:
                        ybank[ft] = (3, bank(3), 'ps3')
                    b1, pbr, kr = 4, bank(4), 'ps4'
                    b2, pbi, ki = 5, bank(5), 'ps5'
                    if DBG.get('ssm_lvl', 9) < 0:
                        yield
                        continue
                    P.op('pe', lambda e, j=j, ft=ft, pbr=pbr: e.matmul(pbr[:, 0:ncols], lhsT=BbT[:, j, 0, :], rhs=uT[:, ft, 0:ncols], start=True, stop=True),
                         reads=['uT', 'BbT'], writes=[kr])
                    P.op('pe', lambda e, j=j, ft=ft, pbi=pbi: e.matmul(pbi[:, 0:ncols], lhsT=BbT[:, j, 1, :], rhs=uT[:, ft, 0:ncols], start=True, stop=True),
                         reads=['uT', 'BbT'], writes=[ki])
                    yield
                    if DBG.get('ssm_lvl', 9) < 1:
                        continue
                    Cb = Ctab[:, j:j + 1, 0:tlen].to_broadcast([128, nseq, tlen])
                    Sb = Stab[:, j:j + 1, 0:tlen].to_broadcast([128, nseq, tlen])
                    sbf = j % 2
                    SR, SI = 2 + 2 * sbf, 3 + 2 * sbf
                    kSR, kSI = 'sr%d' % sbf, 'si%d' % sbf
                    P.op('dve', lambda e, Sb=Sb, pbr=pbr, SR=SR: e.tensor_tensor(out=T[SR], in0=v3(pbr), in1=Sb, op=ALU.mult), reads=[kr, 'Stab', kSR], writes=[kSR])
                    P.op('dve', lambda e, Cb=Cb, pbr=pbr: e.tensor_tensor(out=v3(pbr), in0=v3(pbr), in1=Cb, op=ALU.mult), reads=[kr, 'Ctab'], writes=[kr])
                    P.op('dve', lambda e, Sb=Sb, pbi=pbi: e.tensor_tensor(out=T[1], in0=v3(pbi), in1=Sb, op=ALU.mult), reads=[ki, 'Stab', 'f1'], writes=['f1'])
                    P.op('dve', lambda e, pbr=pbr: e.tensor_tensor(out=T[0], in0=v3(pbr), in1=T[1], op=ALU.add), reads=[kr, 'f1', 'f0'], writes=['f0'])
                    P.op('dve', lambda e, Cb=Cb, pbi=pbi: e.tensor_tensor(out=v3(pbi), in0=v3(pbi), in1=Cb, op=ALU.mult), reads=[ki, 'Ctab'], writes=[ki])
                    P.op('dve', lambda e, pbi=pbi, SR=SR: e.tensor_tensor(out=T[1], in0=v3(pbi), in1=T[SR], op=ALU.subtract), reads=[ki, kSR, 'f0', 'f1'], writes=['f1'])
                    yield
                    if DBG.get('ssm_lvl', 9) < 2:
                        continue
                    for s in range(nseq):
                        cs = slice(s * tlen, (s + 1) * tlen)
                        rb_ = rco[:, j:j + 1].to_broadcast([128, tlen])
                        P.op('dve', lambda e, cs=cs, rb_=rb_, j=j, s=s, SR=SR: e.tensor_tensor_scan(
                            out=ssmf[:, SR, cs], data0=rb_, data1=ssmf[:, 0, cs], initial=stre[:, j, s:s + 1], op0=ALU.mult, op1=ALU.add),
                            reads=['f0', 'rco', 'stre', kSR], writes=[kSR])
                        P.op('dve', lambda e, cs=cs, rb_=rb_, j=j, s=s, SI=SI: e.tensor_tensor_scan(
                            out=ssmf[:, SI, cs], data0=rb_, data1=ssmf[:, 1, cs], initial=stim[:, j, s:s + 1], op0=ALU.mult, op1=ALU.add),
                            reads=['f1', 'rco', 'stim', kSI], writes=[kSI])
                    yield
                    if DBG.get('ssm_lvl', 9) < 3:
                        continue
                    er = ssmf[:, SR, tlen - 1:ncols:tlen]
                    ei = ssmf[:, SI, tlen - 1:ncols:tlen]
                    cE = cend[:, pti, 0, j:j + 1]; sE = cend[:, pti, 1, j:j + 1]; nsE = cend[:, pti, 2, j:j + 1]
                    tAv = bnst[:, 0, 0:nseq]
                    tBv = bnst[:, 1, 0:nseq]
                    P.op('act', lambda e, er=er, cE=cE, tAv=tAv: e.activation(out=tAv, in_=er, func=AF.Copy, scale=cE),
                         reads=[kSR, 'cend', 'bnst0'], writes=['bnst0'])
                    P.op('act', lambda e, er=er, sE=sE, tBv=tBv: e.activation(out=tBv, in_=er, func=AF.Copy, scale=sE),
                         reads=[kSR, 'cend', 'bnst1'], writes=['bnst1'])
                    P.op('dve', lambda e, ei=ei, nsE=nsE, tAv=tAv, j=j: e.scalar_tensor_tensor(out=stre[:, j, 0:nseq], in0=ei, scalar=nsE, in1=tAv,
                                                                                          op0=ALU.mult, op1=ALU.add),
                         reads=[kSI, 'bnst0', 'cend'], writes=['stre'])
                    P.op('dve', lambda e, ei=ei, cE=cE, tBv=tBv, j=j: e.scalar_tensor_tensor(out=stim[:, j, 0:nseq], in0=ei, scalar=cE, in1=tBv,
                                                                                         op0=ALU.mult, op1=ALU.add),
                         reads=[kSI, 'bnst1', 'cend'], writes=['stim'])
                    if DBG.get('ssm_lvl', 9) < 4:
                        continue
                    sb_i = j % 2
                    srb = v3(ssmb[:, sb_i, 0, :]); sib = v3(ssmb[:, sb_i, 1, :])
                    P.op('pool', lambda e, Cb=Cb, SR=SR: e.tensor_tensor(out=T[6], in0=T[SR], in1=Cb, op=ALU.mult), reads=[kSR, 'Ctab', 'f6'], writes=['f6'])
                    P.op('pool', lambda e, Sb=Sb, SI=SI: e.tensor_tensor(out=T[7], in0=T[SI], in1=Sb, op=ALU.mult), reads=[kSI, 'Stab', 'f7'], writes=['f7'])
                    P.op('pool', lambda e, srb=srb: e.tensor_tensor(out=srb, in0=T[6], in1=T[7], op=ALU.subtract), reads=['f6', 'f7'], writes=['ssmb%d' % sb_i])
                    P.op('pool', lambda e, Sb=Sb, SR=SR: e.tensor_tensor(out=T[6], in0=T[SR], in1=Sb, op=ALU.mult), reads=[kSR, 'Stab', 'f6'], writes=['f6'])
                    P.op('pool', lambda e, Cb=Cb, SI=SI: e.tensor_tensor(out=T[7], in0=T[SI], in1=Cb, op=ALU.mult), reads=[kSI, 'Ctab', 'f7'], writes=['f7'])
                    P.op('pool', lambda e, sib=sib: e.tensor_tensor(out=sib, in0=T[6], in1=T[7], op=ALU.add), reads=['f6', 'f7'], writes=['ssmb%d' % sb_i])
                    yield
                    if DBG.get('ssm_lvl', 9) < 5:
                        continue
                    pending.append(j)
                    if len(pending) > DBG.get('ylag', YLAG):
                        emit_y(pending.pop(0))
                        yield
                while pending:
                    emit_y(pending.pop(0))
                    yield


            ssm_live = {}

            hTu = ssmf[:, 0:4, :].rearrange("p a n -> p (a n)").bitcast(BF16).rearrange("p (k n) -> p k n", n=512)

            def s4_u(kind2, src, skeys):
                nc2 = 512 if kind2 == 'p' else 256
                wv, wk = next_w()
                for ft in range(4):
                    b, pb, k = nbank()
                    def f_mm(e, wv=wv, pb=pb, ft=ft):
                        ins = None
                        for kc in range(8):
                            ins = e.matmul(pb[:, 0:nc2], lhsT=wv[:, kc, ft * 128:(ft + 1) * 128], rhs=src[:, kc, 0:nc2],
                                           start=(kc == 0), stop=(kc == 7))
                        return ins
                    P.op('pe', f_mm, reads=list(skeys) + [wk], writes=[k])
                    P.op('act', lambda e, pb=pb, ft=ft: e.activation(out=uT[:, ft, 0:nc2], in_=pb[:, 0:nc2], func=AF.Copy),
                         reads=[k], writes=['uT', 'tmpa', 'tmpa0', 'tmpa1'])

            def run_tile(kind, ti, pre=False, nxt=None, early=False, last_prompt=False):
                prompt = (kind == 'p')
                nseq = 2 if prompt else 4
                tlen = TL if prompt else 64
                ncols = nseq * tlen
                ntt = ncols // 128
                pti = 0 if prompt else 1
                rows = [0, 1] if prompt else [2, 3, 4, 5]
                if prompt:
                    segs = {tt: [(0, 128, tt // 2)] for tt in range(4)}
                else:
                    segs = {tt: [(0, 64, 2 + 2 * tt), (64, 128, 3 + 2 * tt)] for tt in range(2)}

                def load_gate(sec):
                    for slot in range(2):
                        if prompt:
                            P.dma('sp', lambda e, sec=sec, slot=slot: e.dma_start(
                                out=gbc[:, 0, slot, :], in_=dap(mod_d, slot * 6 * D + sec * D, [[0, 128], [1, D]])),
                                'gbc', reads=['mod_d'], writes=['gbc'])
                        else:
                            for hf in range(2):
                                r = 2 + 2 * slot + hf
                                P.dma('sp', lambda e, sec=sec, slot=slot, hf=hf, r=r: e.dma_start(
                                    out=gbc[hf * 64:(hf + 1) * 64, 0, slot, :], in_=dap(mod_d, r * 6 * D + sec * D, [[0, 64], [1, D]])),
                                    'gbc', reads=['mod_d'], writes=['gbc'])

                def load_x():
                  for tt in range(ntt):
                      if prompt:
                          s, hf = tt // 2, tt % 2
                          src = xp[s, ti * TL + hf * 128: ti * TL + hf * 128 + 128, :]
                      else:
                          src = xs[2 * tt:2 * tt + 2, :, :].rearrange("s t d -> (s t) d")
                      P.dma('sp', lambda e, tt=tt, src=src: e.dma_start(out=xt[:, tt, :], in_=src), 'xt%d' % tt,
                            writes=['xt%d' % tt])
                if not pre:
                    load_x()
                if not pre:
                    for tt in range(ntt):
                        ln_stats(tt, 'xt%d' % tt)
                        ln_to_featmajor(tt, ntt, segs[tt], 0, 1, 'xt%d' % tt)

                tck(20)
                fence('R1')
                def s4_block(blk):
                    wv, wk = next_w()
                    if blk == 2 or (blk == 1 and (not prompt or ti >= 6)):
                        need_out = (not prompt) or ti >= 6
                        for tt in range(ntt):
                            b, pb, k = nbank()
                            def f_mm(e, wv=wv, pb=pb, tt=tt):
                                ins = None
                                for kc in range(8):
                                    ins = e.matmul(pb[:, :], lhsT=act8[:, kc, tt * 128:(tt + 1) * 128], rhs=wv[:, kc, :],
                                                   start=(kc == 0), stop=(kc == 7))
                                return ins
                            P.op('pe', f_mm, reads=['act8', wk], writes=[k])
                            if blk == 2:
                                if prompt:
                                    slot = (tt // 2) * 6 + ((2 * ti + tt % 2) % 6)
                                    P.op('act', lambda e, pb=pb, slot=slot: e.activation(out=vring[:, slot, :], in_=pb, func=AF.Copy),
                                         reads=[k], writes=['vring'])
                                elif not DBG.get('novnew'):
                                    P.op('act', lambda e, pb=pb, tt=tt: e.activation(out=sga[:, 2 * tt:2 * tt + 2, 256:512], in_=pb.rearrange("p (a n) -> p a n", n=256), func=AF.Copy),
                                         reads=[k, 'R1'], writes=['vnew'])
                            if need_out and not (DBG.get('noout2') and blk == 2):
                                sgi = 0
                                stgc[0] += 1
                                P.op('dve', lambda e, pb=pb, sgi=sgi: e.tensor_copy(out=stg2[:, sgi, :], in_=pb), reads=[k], writes=['stg' if sgi == 0 else 'stg2_1'])
                                if prompt:
                                    s_, hf = tt // 2, tt % 2
                                    r0 = (ti - 6) * TL + hf * 128
                                    dst = (kp if blk == 1 else vp)[s_, r0:r0 + 128, :]
                                else:
                                    dst = (ks if blk == 1 else vs)[2 * tt:2 * tt + 2, :, :].rearrange("s t d -> (s t) d")
                                P.dma('sp', lambda e, dst=dst, sgi=sgi: e.dma_start(out=dst, in_=stg2[:, sgi, :]), 'stgout%d' % sgi, reads=['stg' if sgi == 0 else 'stg2_1'])
                            yield
                        if blk == 2:
                            return
                    for ft in range(4):
                        b, pb, k = nbank()
                        def f_mm(e, wv=wv, pb=pb, ft=ft):
                            ins = None
                            for kc in range(8):
                                ins = e.matmul(pb[:, 0:ncols], lhsT=wv[:, kc, ft * 128:(ft + 1) * 128], rhs=act8[:, kc, 0:ncols],
                                               start=(kc == 0), stop=(kc == 7))
                            return ins
                        P.op('pe', f_mm, reads=['act8', wk], writes=[k])
                        if blk == 0:
                            P.op('act', lambda e, pb=pb, ft=ft: e.activation(out=qT[:, ft, 0:ncols], in_=pb[:, 0:ncols], func=AF.Copy),
                                 reads=[k, 'R1'], writes=['qT'])
                        elif blk == 1:
                            if prompt:
                                for s in range(2):
                                    sl0 = s * 6 + (2 * ti) % 6
                                    P.op('act', lambda e, pb=pb, ft=ft, s=s, sl0=sl0: e.activation(
                                        out=kring[:, ft, sl0:sl0 + 2, :], in_=pb[:, s * 256:(s + 1) * 256].rearrange("p (a n) -> p a n", n=128), func=AF.Copy),
                                        reads=[k], writes=['kring'])
                            else:
                                P.op('dve', lambda e, pb=pb, ft=ft: e.tensor_copy(out=knew[:, ft, :], in_=pb[:, 0:256]),
                                     reads=[k, 'R1'], writes=['knew'])
                        elif blk == 3:
                            P.op('act', lambda e, pb=pb, ft=ft: e.activation(out=uT[:, ft, 0:ncols], in_=pb[:, 0:ncols], func=AF.Copy),
                                 reads=[k], writes=['uT', 'tmpa', 'tmpa0', 'tmpa1'])
                        else:
                            dstT = sga if blk < 6 else sgb
                            f8 = (blk % 2) * 4 + ft
                            P.op('act', lambda e, pb=pb, dstT=dstT, f8=f8: e.activation(out=dstT[:, f8, 0:ncols], in_=pb[:, 0:ncols], func=AF.Sigmoid),
                                 reads=[k, 'R1'], writes=['sg'])
                        yield


                if not early:
                    s4_u(kind, act8, ['act8'])

                def attn_block(s_col0, nq, kblocks, qkey_extra):
                    pod, kod = bank(2), 'ps2'
                    nd = len(kblocks)
                    dmax = max(d for d, _, _ in kblocks) + 1
                    for h in range(8):
                        hp, par = h // 2, h % 2
                        hq = hp % 2
                        pl = slice(par * 64, par * 64 + 64)
                        si = h % 2
                        pss = PS[0]
                        skeys = ['ps0', 'ps1']
                        def f_s(e, pss=pss, hp=hp, pl=pl, h=h):
                            ins = None
                            if nq == 128:
                                w0 = min(dmax, 4) * 128
                                e.matmul(pss[:, 0:w0], lhsT=ident_bf[:, :], rhs=Eh[:, h, 0:w0], start=True, stop=False)
                                if dmax == 5:
                                    e.matmul(pss[:, 512:640], lhsT=ident_bf[:, :], rhs=Eh[:, h, 512:640], start=True, stop=False)
                                dA = max(d for d, _, _ in kblocks if d <= 3)
                                for i_, (d, slot, nk) in enumerate(kblocks):
                                    last = (d == dA or d == 4)
                                    ins = e.matmul(pss[0:nk, d * nq:(d + 1) * nq], lhsT=kring[pl, hp, slot, 0:nk], rhs=qT[pl, hp, s_col0:s_col0 + nq],
                                                   start=False, stop=last)
                                return ins
                            for (d, slot, nk) in kblocks:
                                e.matmul(pss[0:nk, d * nq:(d + 1) * nq], lhsT=kring[pl, hp, slot, 0:nk], rhs=qT[pl, hp, s_col0:s_col0 + nq],
                                         start=True, stop=False)
                                ins = e.matmul(pss[0:nk, d * nq:(d + 1) * nq], lhsT=ident_bf[:, 0:nk], rhs=Eh[:, h, d * 128:d * 128 + nq],
                                               start=False, stop=True)
                            return ins
                        P.op('pe', f_s, reads=['kring', 'qT', 'Eh', 'identb'] + qkey_extra, writes=skeys)
                        ptv = pt[:, si, 0:dmax * nq]
                        P.op('act', lambda e, pss=pss, ptv=ptv: e.activation(out=ptv, in_=pss[:, 0:dmax * nq], func=AF.Exp, scale=0.125),
                             reads=skeys, writes=['pt%d' % si])
                        yield
                        def f_pv(e, hq=hq, pl=pl, si=si, h=h):
                            ins = None
                            for i, (d, slot, nk) in enumerate(kblocks):
                                ins = e.matmul(pod[pl, hq * 128:hq * 128 + nq], lhsT=vring[0:nk, slot, h * 64:(h + 1) * 64],
                                               rhs=pt[0:nk, si, d * nq:(d + 1) * nq], start=(i == 0), stop=(i == nd - 1))
                            for i, (d, slot, nk) in enumerate(kblocks):
                                ins = e.matmul(pod[pl, 256 + hq * 128:256 + hq * 128 + nq], lhsT=ones_bf[0:nk, 0:64],
                                               rhs=pt[0:nk, si, d * nq:(d + 1) * nq], start=(i == 0), stop=(i == nd - 1))
                            return ins
                        P.op('pe', f_pv, reads=['pt%d' % si, 'vring', 'ones'], writes=[kod])
                        yield
                        if h % 4 == 3:
                            hp0 = (h // 4) * 2
                            pov = pod[:, 0:256].rearrange("p (a n) -> p a n", n=128)[:, :, 0:nq]
                            pdv = pod[:, 256:512].rearrange("p (a n) -> p a n", n=128)[:, :, 0:nq]
                            rdv = rden[:, 0:256].rearrange("p (a n) -> p a n", n=128)[:, :, 0:nq]
                            P.op('dve', lambda e, rdv=rdv, pdv=pdv: e.reciprocal(out=rdv, in_=pdv), reads=[kod, 'tmpb0', 'tmpb1'], writes=['rden', 'tmpb0', 'tmpb1'])
                            P.op('dve', lambda e, hp0=hp0, pov=pov, rdv=rdv: e.tensor_tensor(out=oT[:, hp0:hp0 + 2, s_col0:s_col0 + nq], in0=pov, in1=rdv, op=ALU.mult),
                                 reads=[kod, 'rden', 'R1'], writes=['oT'])
                            yield

                def gen_att():
                    for blk in (0, 1, 2, 4, 5, 6, 7):
                        yield from s4_block(blk)
                    if prompt:
                        for s in range(2):
                            for qh in range(2):
                                qb = 2 * ti + qh
                                kbl = [(d, s * 6 + (qb - d) % 6, 128) for d in range(5) if qb - d >= 0]
                                yield from attn_block(s * 256 + qh * 128, 128, kbl, [])
                    else:
                        for pr in range(2):
                            for sl in range(2):
                                s = 2 * pr + sl
                                for c in range(4):
                                    slot = sl * 5 + c
                                    P.dma('sp', lambda e, s=s, c=c: e.dma_start(out=stg[:, :], in_=ck[s, c * 128:(c + 1) * 128, :]), 'stgin',
                                          reads=['stg'], writes=['stg'])
                                    b, pb, k = nbank()
                                    def f_t(e, pb=pb):
                                        ins = None
                                        for hp in range(4):
                                            ins = e.transpose(pb[:, hp * 128:(hp + 1) * 128], stg[:, hp * 128:(hp + 1) * 128], ident[:])
                                        return ins
                                    P.op('pe', f_t, reads=['stg', 'ident'], writes=[k])
                                    P.op('act', lambda e, pb=pb, slot=slot: e.activation(out=kring[:, :, slot, :], in_=pb.rearrange("p (a n) -> p a n", n=128), func=AF.Copy),
                                         reads=[k], writes=['kring'])
                                    P.dma('sp', lambda e, s=s, c=c: e.dma_start(out=stg[:, :], in_=cv[s, c * 128:(c + 1) * 128, :]), 'stgin',
                                          reads=['stg'], writes=['stg'])
                                    P.op('act', lambda e, slot=slot: e.activation(out=vring[:, slot, :], in_=stg[:, :], func=AF.Copy),
                                         reads=['stg'], writes=['vring'])
                                slot = sl * 5 + 4
                                P.op('dve', lambda e, s=s, slot=slot: e.tensor_copy(out=kring[:, :, slot, 0:64], in_=knew[:, :, s * 64:(s + 1) * 64]),
                                     reads=['knew'], writes=['kring'])
                                if s % 2 == 0:
                                    P.op('dve', lambda e, s=s, slot=slot: e.tensor_copy(out=vring[0:64, slot, :].rearrange("p (a n) -> p a n", n=256), in_=sga[0:64, 2 * (s // 2):2 * (s // 2) + 2, 256:512]),
                                         reads=['vnew'], writes=['vring'])
                                else:
                                    P.dma('sp', lambda e, s=s, slot=slot: e.dma_start(out=vring[0:64, slot, :].rearrange("p (a n) -> p a n", n=256), in_=sga[64:128, 2 * (s // 2):2 * (s // 2) + 2, 256:512]), 'vshift',
                                          reads=['vnew'], writes=['vring'])
                            for sl in range(2):
                                s = 2 * pr + sl
                                kbl = [(0, sl * 5 + 4, 64)] + [(4 - c, sl * 5 + c, 128) for c in range(4)]
                                yield from attn_block(s * 64, 64, kbl, [])

                    wv, wk = next_w()
                    for f in range(8):
                        b, pb, k = nbank()
                        def f_mm(e, wv=wv, pb=pb, f=f):
                            ins = None
                            for kc in range(4):
                                ins = e.matmul(pb[:, 0:ncols], lhsT=wv[:, kc, f * 128:(f + 1) * 128], rhs=oT[:, kc, 0:ncols], start=(kc == 0), stop=(kc == 3))
                            return ins
                        P.op('pe', f_mm, reads=['oT', wk], writes=[k])
                        P.op('dve', lambda e, pb=pb, f=f: e.tensor_tensor(out=act8[:, f, 0:ncols], in0=pb[:, 0:ncols], in1=sga[:, f, 0:ncols], op=ALU.mult),
                             reads=[k, 'sg', 'R1'], writes=['act8'])
                        yield


                bank_allowed[0] = [6, 7]
                interleave(([] if DBG.get('noatt') else [limited(gen_att(), DBG.get('attstop', 10**9))]) + ([] if DBG.get('nossm') else [ssm_live.pop((kind, ti), None) or gen_ssm(kind)]), [DBG.get('attw', ATT_W), DBG.get('ssmw', SSM_W)][(1 if DBG.get('noatt') else 0):])
                bank_allowed[0] = list(range(8))
                if last_prompt:
                    write_states(2, rep, imp)
                if pre:
                    load_x()

                for blk in range(2):
                    wv, wk = next_w()
                    for fl in range(4):
                        f = blk * 4 + fl
                        b1, pba, ka = nbank()
                        b2, pbb, kb_ = nbank()
                        def f_mm(e, wv=wv, pba=pba, pbb=pbb, fl=fl):
                            ins = None
                            for kc in range(4):
                                ins = e.matmul(pbb[:, 0:ncols], lhsT=wv[:, kc, 512 + fl * 128:512 + (fl + 1) * 128], rhs=gT[:, kc, 0:ncols], start=(kc == 0), stop=(kc == 3))
                            for kc in range(4):
                                ins = e.matmul(pba[:, 0:ncols], lhsT=wv[:, kc, fl * 128:(fl + 1) * 128], rhs=gT[:, kc, 0:ncols], start=(kc == 0), stop=(kc == 3))
                            return ins
                        P.op('pe', f_mm, reads=['gT', wk], writes=[ka, kb_])
                        tb = tmpb[:, f % 2, 0:ncols]
                        ta = tmpa[:, f % 2, 0:ncols]
                        P.op('act', lambda e, pbb=pbb, tb=tb: e.activation(out=tb, in_=pbb[:, 0:ncols], func=AF.Sigmoid), reads=[kb_, 'tmpb%d' % (f % 2)], writes=['tmpb%d' % (f % 2)])
                        P.op('dve', lambda e, pba=pba, tb=tb, ta=ta: e.tensor_tensor(out=ta, in0=pba[:, 0:ncols], in1=tb, op=ALU.mult),
                             reads=[ka, 'tmpb%d' % (f % 2), 'tmpa', 'tmpa1', 'tmpa%d' % (f % 2), 'R1'], writes=['tmpa%d' % (f % 2), 'uT'])
                        P.op('pool', lambda e, ta=ta, f=f: e.tensor_tensor(out=ta, in0=ta, in1=sgb[:, f, 0:ncols], op=ALU.mult),
                             reads=['tmpa%d' % (f % 2), 'sg', 'R1'], writes=['tmpa%d' % (f % 2)])
                        P.op('pool', lambda e, ta=ta, f=f: e.tensor_tensor(out=act8[:, f, 0:ncols], in0=act8[:, f, 0:ncols], in1=ta, op=ALU.add),
                             reads=['tmpa%d' % (f % 2), 'act8'], writes=['act8'])

                tck(25)
                def resid(tt, cb, pb, k, gate):
                    slot = (tt // 2) if prompt else tt
                    P.op('dve', lambda e, pb=pb, slot=slot, cb=cb: e.tensor_tensor(out=pb, in0=pb, in1=gbc[:, 0, slot, cb * 512:(cb + 1) * 512], op=ALU.mult),
                         reads=[k, 'gbc'], writes=[k])
                    P.op('dve', lambda e, tt=tt, cb=cb, pb=pb: e.scalar_tensor_tensor(out=xt[:, tt, cb * 512:(cb + 1) * 512], in0=xt[:, tt, cb * 512:(cb + 1) * 512],
                                                                                    scalar=ALPHA, in1=pb, op0=ALU.mult, op1=ALU.add),
                         reads=[k, 'xt%d' % tt], writes=['xt%d' % tt])

                load_gate(2)
                for i_, v_ in enumerate([ln1_g, ln1_b]):
                    P.dma('sp', lambda e, i_=i_, v_=v_: e.dma_start(out=lnbc[:, i_, :], in_=dap(v_, 0, [[0, 128], [1, D]])), 'lnbc', writes=['lnbc'])
                wouts = [next_w(), next_w(prefetch=False)]

                def wout_tt(tt):
                    for cb in range(2):
                        wv, wk = wouts[cb]
                        b, pb, k = nbank()
                        def f_mm(e, wv=wv, pb=pb, tt=tt):
                            ins = None
                            for kc in range(8):
                                ins = e.matmul(pb, lhsT=act8[:, kc, tt * 128:(tt + 1) * 128], rhs=wv[:, kc, :], start=(kc == 0), stop=(kc == 7))
                            return ins
                        P.op('pe', f_mm, reads=['act8', wk], writes=[k])
                        resid(tt, cb, pb, k, 0)

                def ln1_a(tt):
                    ln_stats(tt, 'xt%d' % tt)
                    ln_affine(tt, 0, 1)
                    ln_stats(tt, 'xt%d' % tt)

                def gen_wout():
                    for tt in range(ntt):
                        wout_tt(tt)
                        yield
                        if tt >= 1:
                            ln1_a(tt - 1)
                            yield
                    prefetch_w()
                    ln1_a(ntt - 1)
                    yield

                do_early = EARLY and nxt is not None
                if do_early:
                    interleave([gen_wout(), gen_ln0(nxt[0], nxt[1], dest=hTu, dkeys=HTU_KEYS)], [1, 2])
                    s4_u(nxt[0], hTu, HTU_KEYS)
                    ssm_live[nxt] = gen_ssm(nxt[0])
                    for tt in range(ntt):
                        ln_to_featmajor(tt, ntt, segs[tt], 3, 4, 'xt%d' % tt)
                else:
                    for tt in range(ntt):
                        wout_tt(tt)
                        if tt >= 1:
                            ln1_a(tt - 1)
                    prefetch_w()
                    ln_to_featmajor(0, ntt, segs[0], 3, 4, 'xt0')
                    ln1_a(ntt - 1)
                    for tt in range(1, ntt):
                        ln_to_featmajor(tt, ntt, segs[tt], 3, 4, 'xt%d' % tt)

                tck(26)
                fence('R1')
                def gen_ffn_in():
                    for blk in range(11):
                        wv, wk = next_w()
                        for fl in range(2):
                            f = blk * 2 + fl
                            b1, pbg, kg = nbank()
                            b2, pbu, ku = nbank()
                            def f_mm(e, wv=wv, pbg=pbg, pbu=pbu, fl=fl):
                                ins = None
                                for kc in range(8):
                                    ins = e.matmul(pbg[:, 0:ncols], lhsT=wv[:, kc, fl * 128:(fl + 1) * 128], rhs=act8[:, kc, 0:ncols], start=(kc == 0), stop=(kc == 7))
                                for kc in range(8):
                                    ins = e.matmul(pbu[:, 0:ncols], lhsT=wv[:, kc, 256 + fl * 128:256 + (fl + 1) * 128], rhs=act8[:, kc, 0:ncols], start=(kc == 0), stop=(kc == 7))
                                return ins
                            P.op('pe', f_mm, reads=['act8', wk], writes=[kg, ku])
                            tb = tmpb[:, f % 2, 0:ncols]
                            P.op('act', lambda e, pbg=pbg, tb=tb: e.activation(out=tb, in_=pbg[:, 0:ncols], func=AF.Silu), reads=[kg, 'tmpb%d' % (f % 2)], writes=['tmpb%d' % (f % 2)])
                            P.op('dve', lambda e, pbu=pbu, tb=tb, f=f: e.tensor_tensor(out=actT[:, f, 0:ncols], in0=pbu[:, 0:ncols], in1=tb, op=ALU.mult),
                                 reads=[ku, 'tmpb%d' % (f % 2), 'R1'], writes=['actT'])
                            yield

                gens_ = [gen_ffn_in()]
                ws_ = [1]
                if do_early:
                    bank_allowed[0] = [0, 1, 2, 6, 7]
                    gens_.append(limited(ssm_live[nxt], DBG.get('ssm_f', SSM_F)))
                    ws_.append(1)
                interleave(gens_, ws_)
                bank_allowed[0] = list(range(8))
                tck(27)
                load_gate(5)

                def gen_ffn_out():
                    for cb in range(2):
                        banks = [(bb_, bank(bb_), 'ps%d' % bb_) for bb_ in (0, 1, 6, 7)[:ntt]]
                        for kh, (k0_, kn_) in enumerate(KPARTS):
                            wv, wk = next_w()
                            for tt in range(ntt):
                                b, pb, k = banks[tt]
                                def f_mm(e, wv=wv, pb=pb, tt=tt, kh=kh, k0_=k0_, kn_=kn_):
                                    ins = None
                                    for kc in range(kn_):
                                        ins = e.matmul(pb, lhsT=actT[:, k0_ + kc, tt * 128:(tt + 1) * 128], rhs=wv[:, kc, :],
                                                       start=(kh == 0 and kc == 0), stop=(kh == 3 and kc == kn_ - 1))
                                    return ins
                                P.op('pe', f_mm, reads=['actT', wk, 'R1'], writes=[k])
                                yield
                        for tt in range(ntt):
                            b, pb, k = banks[tt]
                            resid(tt, cb, pb, k, 1)
                            yield

                bank_allowed[0] = [2] if do_early else [2, 3, 4, 5]
                gens_ = [gen_ffn_out()]
                ws_ = [DBG.get('ffw', 2)]
                if nxt is not None:
                    gens_.append(gen_ln0(*nxt))
                    ws_.append(1)
                if do_early:
                    gens_.append(limited(ssm_live[nxt], DBG.get('ssm_g', SSM_G)))
                    ws_.append(1)
                interleave(gens_, ws_)
                bank_allowed[0] = list(range(8))
                for i_, v_ in enumerate([ln2_g, ln2_b]):
                    P.dma('sp', lambda e, i_=i_, v_=v_: e.dma_start(out=lnbc[:, i_, :], in_=dap(v_, 0, [[0, 128], [1, D]])), 'lnbc', writes=['lnbc'])
                for tt in range(ntt):
                    ln_stats(tt, 'xt%d' % tt)
                    ln_affine(tt, 2, 3)
                    if prompt:
                        s, hf = tt // 2, tt % 2
                        dst = yp[s, ti * TL + hf * 128: ti * TL + hf * 128 + 128, :]
                    else:
                        dst = ys[2 * tt:2 * tt + 2, :, :].rearrange("s t d -> (s t) d")
                    P.dma('pool', lambda e, tt=tt, dst=dst: e.dma_start(out=dst, in_=xt[:, tt, :]), 'yout%d' % tt, reads=['xt%d' % tt])

            def write_states(ns, dre, dim_):
                for (st_, dd, nm) in ((stre, dre, 'stre'), (stim, dim_, 'stim')):
                    tcp = tmpa[:, 0, 0:16 * ns]
                    P.op('dve', lambda e, st_=st_, tcp=tcp: e.tensor_copy(out=tcp.rearrange("p (s j) -> p s j", j=16),
                                                                           in_=st_[:, :, 0:ns].rearrange("p j s -> p s j")),
                         reads=[nm, 'tmpa', 'tmpa0', 'tmpa1'], writes=['tmpa', 'tmpa0', 'uT'])
                    b, pb, k = nbank()
                    P.op('pe', lambda e, pb=pb, tcp=tcp: e.transpose(pb[:, 0:128], tmpa[:, 0, 0:128], ident[:]), reads=['tmpa', 'ident'], writes=[k])
                    P.op('dve', lambda e, pb=pb: e.tensor_copy(out=stg2[0:16 * ns, 0, 0:128], in_=pb[0:16 * ns, 0:128]), reads=[k, 'stg'], writes=['stg'])
                    dst = dd.rearrange("s (j t) p -> (s j) (t p)", t=2)
                    P.dma('sp', lambda e, dst=dst: e.dma_start(out=dst, in_=stg2[0:16 * ns, 0, 0:128]), 'stout', reads=['stg'])

            seq_tiles = [('p', ti) for ti in range(DBG['ntiles'])] + ([('s', 0)] if DBG['sample'] else [])
            PREF = DBG.get('pref', True)
            EARLY = DBG.get('early', False) and PREF
            build_wseq(len(seq_tiles), EARLY)
            for idx_, (kind_, ti_) in enumerate(seq_tiles):
                nxt_ = seq_tiles[idx_ + 1] if (PREF and idx_ + 1 < len(seq_tiles)) else None
                lastp = (kind_ == 'p' and ti_ == DBG['ntiles'] - 1)
                run_tile(kind_, ti_, pre=(PREF and idx_ > 0), nxt=nxt_, early=(EARLY and idx_ > 0), last_prompt=lastp)
                if kind_ == 's':
                    write_states(4, res, ims)

        except StopBuild:
            pass
        P.barrier()
        P.emit()
    return nc


_NC_CACHE = {}
DBG = {'ntiles': NTILES, 'sample': True, 'cores': NCORES, 'stop': 99}


def kernel(**inp):
    f = lambda a: np.ascontiguousarray(np.asarray(a, dtype=np.float32))
    if 'nc' not in _NC_CACHE:
        _NC_CACHE['nc'] = build_nc()
    nc = _NC_CACHE['nc']
    shared = {
        'w_ada': f(inp['w_ada'][0]), 'b_ada': f(inp['b_ada'][0]), 'w_in': f(inp['w_in'][0]), 'rel_bias': f(inp['rel_bias'][0]),
        'a_re': f(inp['ssm_a_re'][0]), 'a_im': f(inp['ssm_a_im'][0]), 'log_dt': f(inp['ssm_log_dt'][0]),
        'b_re': f(inp['ssm_b_re'][0]), 'b_im': f(inp['ssm_b_im'][0]), 'c_re': f(inp['ssm_c_re'][0]), 'c_im': f(inp['ssm_c_im'][0]),
        'ssm_d': f(inp['ssm_d'][0]), 'w_attn': f(inp['w_attn_proj'][0]), 'w_glu': f(inp['w_glu'][0]), 'w_out': f(inp['w_out'][0]),
        'ln1_g': f(inp['ln1_g'][0]), 'ln1_b': f(inp['ln1_b'][0]), 'w_ffn_in': f(inp['w_ffn_in'][0]), 'w_ffn_out': f(inp['w_ffn_out'][0]),
        'ln2_g': f(inp['ln2_g'][0]), 'ln2_b': f(inp['ln2_b'][0]),
    }
    in_maps = []
    for c in range(DBG['cores']):
        m = dict(shared)
        m['xp'] = f(inp['x_prompt'][2 * c:2 * c + 2])
        m['xs'] = f(inp['x_sample'][4 * c:4 * c + 4])
        m['cc'] = f(np.concatenate([inp['c_prompt'][2 * c:2 * c + 2], inp['c_sample'][4 * c:4 * c + 4]], axis=0))
        m['ck'] = f(inp['cache_attn_k'][0, 4 * c:4 * c + 4].reshape(4, 512, 512))
        m['cv'] = f(inp['cache_attn_v'][0, 4 * c:4 * c + 4].reshape(4, 512, 512))
        m['sre'] = f(inp['state_ssm_re'][0, 4 * c:4 * c + 4])
        m['sim'] = f(inp['state_ssm_im'][0, 4 * c:4 * c + 4])
        in_maps.append(m)
    if DBG.get('trace'):
        res = run_bass_kernel_spmd(nc, in_maps, core_ids=list(range(DBG['cores'])), trace=True)
        print('EXEC_NS', res.exec_time_ns)
    else:
        res = run_bass_kernel_spmd(nc, in_maps, core_ids=list(range(DBG['cores'])))
    R = res.results
    cat = lambda k: np.concatenate([np.asarray(r[k], dtype=np.float32) for r in R], axis=0)
    y_prompt = cat('yp')
    y_sample = cat('ys')
    k_prompt = cat('kp').reshape(1, -1, 512, 8, 64)
    v_prompt = cat('vp').reshape(1, -1, 512, 8, 64)
    re_p = cat('rep')[None]
    im_p = cat('imp')[None]
    k_sample = cat('ks').reshape(1, -1, 64, 8, 64)
    v_sample = cat('vs').reshape(1, -1, 64, 8, 64)
    re_s = cat('res')[None]
    im_s = cat('ims')[None]
    return (y_prompt, y_sample, k_prompt, v_prompt, re_p, im_p, k_sample, v_sample, re_s, im_s)
```

```python
import contextlib
import math
import numpy as np
import concourse.bass as bass
import concourse.mybir as mybir
from concourse.bass_utils import run_bass_kernel_spmd

F32 = mybir.dt.float32
BF16 = mybir.dt.bfloat16
I32 = mybir.dt.int32
AF = mybir.ActivationFunctionType
ALU = mybir.AluOpType

NCORES = 8
D = 1024
SEQ = 2048
TL = 256
NTILES = SEQ // TL
DFF = 2816
ALPHA = 2.0 ** 0.25
LN_EPS = 1e-5
TWO_PI = 2.0 * math.pi


class Prog:
    ENG = ['pe', 'act', 'dve', 'pool', 'sp']

    def __init__(self, nc):
        self.nc = nc
        self.streams = {e: [] for e in self.ENG}
        self.count = {e: 0 for e in self.ENG}
        self.waited = {e: {} for e in self.ENG}
        self.last_write = {}
        self.readers = {}
        self.dmasem_count = {}

    def _deps(self, eng, reads, writes):
        deps = []
        for k in reads:
            lw = self.last_write.get(k)
            if lw is not None:
                deps.append(lw)
        for k in writes:
            lw = self.last_write.get(k)
            if lw is not None:
                deps.append(lw)
            rd = self.readers.get(k)
            if rd:
                for src, val in rd.items():
                    if src != eng or eng != 'pe':
                        deps.append((src, val))
        need = {}
        for src, val in deps:
            if src == eng and eng == 'pe':
                continue
            if self.waited[eng].get(src, 0) >= val:
                continue
            if need.get(src, 0) < val:
                need[src] = val
        for src, val in need.items():
            self.waited[eng][src] = val
            self.streams[eng].append(('wait', src, val))

    def _commit(self, src, val, reads, writes):
        for k in writes:
            self.last_write[k] = (src, val)
            self.readers[k] = {}
        for k in reads:
            d = self.readers.setdefault(k, {})
            if d.get(src, 0) < val:
                d[src] = val

    def op(self, eng, fn, reads=(), writes=()):
        writes = list(writes) + [k for k in reads if k.startswith('ps') and k[2:].isdigit() and k not in writes]
        self._deps(eng, reads, writes)
        self.count[eng] += 1
        self.streams[eng].append(('op', fn))
        self._commit(eng, self.count[eng], reads, writes)

    def dma(self, eng, fn, semkey, reads=(), writes=()):
        self._deps(eng, reads, writes)
        prev = self.dmasem_count.get(semkey, 0)
        if prev and self.waited[eng].get('dma:' + semkey, 0) < prev:
            self.waited[eng]['dma:' + semkey] = prev
            self.streams[eng].append(('wait', 'dma:' + semkey, prev))
        val = self.dmasem_count.get(semkey, 0) + 16
        self.dmasem_count[semkey] = val
        self.streams[eng].append(('dma', fn, semkey))
        self._commit('dma:' + semkey, val, reads, writes)

    def barrier(self, skip_prefix=None):
        for e in self.ENG:
            self.wait_all(e, skip_prefix)

    def wait_all(self, eng, skip_prefix=None):
        for e in self.ENG:
            if e != eng and self.count[e] > self.waited[eng].get(e, 0):
                self.streams[eng].append(('wait', e, self.count[e]))
                self.waited[eng][e] = self.count[e]
        for k, v in self.dmasem_count.items():
            s = 'dma:' + k
            if skip_prefix and k.startswith(skip_prefix):
                continue
            if v > self.waited[eng].get(s, 0):
                self.streams[eng].append(('wait', s, v))
                self.waited[eng][s] = v

    def emit(self):
        nc = self.nc
        with contextlib.ExitStack() as es:
            sems = {}
            for e in self.ENG:
                sems[e] = es.enter_context(nc.semaphore('s_' + e))
            for k in self.dmasem_count:
                sems['dma:' + k] = es.enter_context(nc.semaphore('d_' + k))
            block = es.enter_context(nc.Block())
            streams = self.streams

            def run(engname):
                def f(eng):
                    for it in streams[engname]:
                        if it[0] == 'wait':
                            eng.wait_ge(sems[it[1]], it[2])
                        elif it[0] == 'op':
                            it[1](eng).then_inc(sems[engname], 1)
                        else:
                            it[1](eng).then_inc(sems['dma:' + it[2]], 16)
                return f
            block.tensor(run('pe'))
            block.scalar(run('act'))
            block.vector(run('dve'))
            block.gpsimd(run('pool'))
            block.sync(run('sp'))


ATT_W = 1
SSM_F = 40
SSM_G = 12
SSM_W = 1
YLAG = 1


def limited(g, n):
    for i, _ in enumerate(g):
        if i >= n:
            return
        yield


def interleave(gens, weights):
    alive = list(gens)
    ws = list(weights)
    while alive:
        for i in range(len(alive) - 1, -1, -1):
            pass
        nxt = []
        nws = []
        for g, w in zip(alive, ws):
            ok = True
            for _ in range(w):
                try:
                    next(g)
                except StopIteration:
                    ok = False
                    break
            if ok:
                nxt.append(g)
                nws.append(w)
        alive, ws = nxt, nws


def dap(t, offset, ap):
    return bass.AP(tensor=t.tensor, offset=offset, ap=ap)


def build_nc():
    nc = bass.Bass("TRN2", target_bir_lowering=False)

    def din(name, shape):
        return nc.dram_tensor(name, shape, F32, kind="ExternalInput").ap()

    def dout(name, shape):
        return nc.dram_tensor(name, shape, F32, kind="ExternalOutput").ap()

    def dscr(name, shape, dt):
        return nc.dram_tensor(name, shape, dt, kind="Internal").ap()

    xp = din("xp", [2, SEQ, D]); xs = din("xs", [4, 64, D]); cc = din("cc", [6, D])
    ck = din("ck", [4, 512, 512]); cv = din("cv", [4, 512, 512])
    sre = din("sre", [4, 32, 64]); sim = din("sim", [4, 32, 64])
    w_ada = din("w_ada", [D, 6 * D]); b_ada = din("b_ada", [6 * D])
    w_in = din("w_in", [D, 4096]); rel_bias = din("rel_bias", [8, 513])
    a_re = din("a_re", [32, 64]); a_im = din("a_im", [32, 64]); log_dt = din("log_dt", [32])
    b_re = din("b_re", [32, 64, 16]); b_im = din("b_im", [32, 64, 16])
    c_re = din("c_re", [32, 16, 64]); c_im = din("c_im", [32, 16, 64]); ssm_d = din("ssm_d", [512])
    w_attn = din("w_attn", [512, D]); w_glu = din("w_glu", [512, 2 * D]); w_out = din("w_out", [D, D])
    ln1_g = din("ln1_g", [D]); ln1_b = din("ln1_b", [D])
    w_ffn_in = din("w_ffn_in", [D, 2 * DFF]); w_ffn_out = din("w_ffn_out", [DFF, D])
    ln2_g = din("ln2_g", [D]); ln2_b = din("ln2_b", [D])

    yp = dout("yp", [2, SEQ, D]); ys = dout("ys", [4, 64, D])
    kp = dout("kp", [2, 512, 512]); vp = dout("vp", [2, 512, 512])
    rep = dout("rep", [2, 32, 64]); imp = dout("imp", [2, 32, 64])
    ks = dout("ks", [4, 64, 512]); vs = dout("vs", [4, 64, 512])
    res = dout("res", [4, 32, 64]); ims = dout("ims", [4, 32, 64])

    s_ada = dscr("s_ada", [12, 128, 8, 512], BF16)
    s_in = dscr("s_in", [8, 128, 8, 512], BF16)
    s_attn = dscr("s_attn", [1, 128, 4, 1024], BF16)
    s_glu = dscr("s_glu", [2, 128, 4, 1024], BF16)
    s_out = dscr("s_out", [2, 128, 8, 512], BF16)
    s_fin = dscr("s_fin", [11, 128, 8, 512], BF16)
    s_fout = dscr("s_fout", [8, 128, 6, 512], BF16)
    ext_d = dscr("ext_d", [8, 128, 768], BF16)
    mod_d = dscr("mod_d", [6, 6 * D], F32)

    es = contextlib.ExitStack()
    with es:
        def sb(name, shape, dt):
            return es.enter_context(nc.sbuf_tensor(name, shape, dt))

        P = Prog(nc)

        class StopBuild(Exception):
            pass

        def ckpt(n):
            P.barrier()
            if DBG['stop'] == n:
                raise StopBuild()

        def tck(n):
            if DBG['stop'] == n:
                P.barrier()
                raise StopBuild()
        ident = sb("ident", [128, 128], F32)
        xt = sb("xt", [128, 4, D], F32)
        xn = sb("xn", [128, 1, D], F32)
        act8 = sb("act8", [128, 8, 512], BF16)
        R1 = sb("R1", [128, 14336], BF16)
        qT = R1[:, 0:2048].rearrange("p (k n) -> p k n", n=512)
        uT = R1[:, 12288:14336].rearrange("p (k n) -> p k n", n=512)
        sga = R1[:, 4096:8192].rearrange("p (k n) -> p k n", n=512)
        sgb = R1[:, 8192:12288].rearrange("p (k n) -> p k n", n=512)
        oT = R1[:, 2048:4096].rearrange("p (k n) -> p k n", n=512)
        actT = R1[:, 0:11264].rearrange("p (k n) -> p k n", n=512)
        kring = sb("kring", [128, 4, 12, 128], BF16)
        vring = sb("vring", [128, 12, 512], BF16)
        wsl = [sb("wsl%d" % i, [128, 4096], BF16) for i in range(2)]
        Eh = sb("Eh", [128, 8, 640], BF16)
        pt = sb("pt", [128, 2, 640], BF16)
        lnbc = sb("lnbc", [128, 2, D], F32)
        gbc = sb("gbc", [128, 1, 2, D], F32)
        modT = sb("modT", [128, 48, 6], F32)
        small = sb("small", [128, 4, 8], F32)
        bnst = sb("bnst", [128, 4, 12], F32)
        small2 = sb("small2", [128, 4, 8], F32)
        bnst2 = sb("bnst2", [128, 4, 12], F32)
        epsT = sb("epsT", [128, 1], F32)
        npiT = sb("npiT", [128, 1], F32)
        oneT = sb("oneT", [128, 1], F32)
        Ctab = sb("Ctab", [128, 16, 256], F32)
        Stab = sb("Stab", [128, 16, 256], F32)
        rco = sb("rco", [128, 16], F32)
        cend = sb("cend", [128, 2, 3, 16], F32)
        BbT = sb("BbT", [128, 16, 2, 128], BF16)
        CTw = sb("CTw", [128, 16, 2, 128], BF16)
        Dcol = sb("Dcol", [128, 4], F32)
        stre = sb("stre", [128, 16, 4], F32)
        stim = sb("stim", [128, 16, 4], F32)
        ssmf = sb("ssmf", [128, 8, 512], F32)
        ssmb = sb("ssmb", [128, 2, 2, 512], BF16)
        gT = sb("gT", [128, 4, 512], BF16)
        knew = qT[:, 0:4, 256:512]
        tmpa = R1[:, 12288:14336].bitcast(F32).rearrange("p (a n) -> p a n", n=512)
        tmpb = sb("tmpb", [128, 2, 512], BF16)
        rden = tmpb[:, :, :].rearrange("p a n -> p (a n)").bitcast(F32)
        stg2 = sb("stg2", [128, 1, 512], F32)
        ident_bf = sb("ident_bf", [128, 128], BF16)
        stg = stg2[:, 0, :]
        ones_bf = sb("ones_bf", [128, 64], BF16)

        PS = [es.enter_context(nc.psum_tensor("PS%d" % i, [128, 1024], F32)) for i in range(4)]

        def bank(i):
            return PS[i // 2][:, (i % 2) * 512:(i % 2) * 512 + 512]

        bank_rr = [0]

        bank_allowed = [list(range(8))]

        def nbank():
            while True:
                b = bank_rr[0]
                bank_rr[0] = (b + 1) % 8
                if b in bank_allowed[0]:
                    return b, bank(b), 'ps%d' % b

        try:
            P.op('pool', lambda e: e.memset(ident[:], 0.0), writes=['ident'])
            P.op('pool', lambda e: e.affine_select(out=ident[:], in_=ident[:], pattern=[[-1, 128]],
                                                   compare_op=ALU.not_equal, fill=1.0, base=0, channel_multiplier=1),
                 reads=['ident'], writes=['ident'])
            P.op('dve', lambda e: e.tensor_copy(out=ident_bf[:], in_=ident[:]), reads=['ident'], writes=['identb'])
            P.op('dve', lambda e: e.memset(epsT[:], LN_EPS), writes=['epsT'])
            P.op('dve', lambda e: e.memset(ones_bf[:], 1.0), writes=['ones'])
            P.op('dve', lambda e: e.memset(stre[:].rearrange('p j s -> p (j s)'), 0.0), writes=['stre'])
            P.op('dve', lambda e: e.memset(stim[:].rearrange('p j s -> p (j s)'), 0.0), writes=['stim'])
            P.op('dve', lambda e: e.memset(npiT[:], -math.pi), writes=['npiT'])
            P.op('dve', lambda e: e.memset(oneT[:], 1.0), writes=['oneT'])
            cast_jobs = {}

            def cast_blk(w, N, kcn, kc0, col0, bc, dst, dkey, dcol0=0, dcols=None):
                src = dap(w, kc0 * 128 * N + col0, [[N, 128], [128 * N, kcn], [1, bc]])
                d = dst if dcols is None else dst[:, :, dcol0:dcol0 + dcols]
                cast_jobs.setdefault(dkey, []).append(
                    lambda d=d, src=src, dkey=dkey, dcol0=dcol0: P.dma('pool', lambda e: e.dma_start(out=d, in_=src),
                                                                        'cast_' + dkey + ('_%d' % dcol0), writes=[dkey]))

            def issue_cast(dkey):
                for f_ in cast_jobs.pop(dkey, []):
                    f_()

            ada_slots = [R1[:, 0:4096], R1[:, 4096:8192], R1[:, 8192:12288], act8[:].rearrange('p k n -> p (k n)'),
                         xt[:, 0:2, :].rearrange('p a n -> p (a n)').bitcast(BF16)[:, 0:4096], xt[:, 2:4, :].rearrange('p a n -> p (a n)').bitcast(BF16)[:, 0:4096],
                         Ctab[:].rearrange('p a n -> p (a n)').bitcast(BF16)[:, 0:4096], Ctab[:].rearrange('p a n -> p (a n)').bitcast(BF16)[:, 4096:8192],
                         Stab[:].rearrange('p a n -> p (a n)').bitcast(BF16)[:, 0:4096], Stab[:].rearrange('p a n -> p (a n)').bitcast(BF16)[:, 4096:8192],
                         kring[:].rearrange('p a b c -> p (a b c)')[:, 0:4096], vring[:].rearrange('p a n -> p (a n)')[:, 0:4096]]
            ada_views = []
            for b in range(12):
                v_ = ada_slots[b].rearrange('p (k n) -> p k n', n=512)
                src_ = dap(w_ada, b * 512, [[6 * D, 128], [128 * 6 * D, 8], [1, 512]])
                P.dma('pool', lambda e, v_=v_, src_=src_: e.dma_start(out=v_, in_=src_), 'adaL%d' % b, writes=['adaS%d' % b])
                ada_views.append((v_, 'adaS%d' % b))
            for b in (3, 0, 1, 2, 4, 5, 6, 7):
                cast_blk(w_in, 4096, 8, 0, b * 512, 512, s_in[b], 's_in%d' % b)
                issue_cast('s_in%d' % b)
            wcnt = [0]

            def load_w(scr, blk, kcn, bc, skey):
                i = wcnt[0] % 2
                wcnt[0] += 1
                view = wsl[i][:, 0:kcn * bc].rearrange("p (k n) -> p k n", n=bc)
                key = 'wsl%d' % i
                P.dma('sp', lambda e, view=view, src=scr[blk][:, 0:kcn, :]: e.dma_start(out=view, in_=src), key,
                      reads=[skey], writes=[key])
                return view, key

            W_IN = {b: (s_in, b, 8, 512, 's_in%d' % b) for b in range(8)}
            W_FRONT = ([W_IN[b] for b in (0, 1, 2, 4, 5, 6, 7)] + [(s_attn, 0, 4, 1024, 's_attn0')] +
                       [(s_glu, b, 4, 1024, 's_glu%d' % b) for b in range(2)] +
                       [(s_out, b, 8, 512, 's_out%d' % b) for b in range(2)])
            W_BACK = ([(s_fin, b, 8, 512, 's_fin%d' % b) for b in range(11)] +
                      [(s_fout, b, (6, 5, 6, 5)[b % 4], 512, 's_fout%d' % b) for b in range(8)])
            worder = []

            def build_wseq(ntl, early):
                worder.append(W_IN[3])
                for i_ in range(ntl):
                    worder.extend(W_FRONT)
                    if i_ + 1 < ntl:
                        if early:
                            worder.append(W_IN[3])
                    worder.extend(W_BACK)
                    if i_ + 1 < ntl and not early:
                        worder.append(W_IN[3])
            wq = {'pending': None, 'pos': 0}

            CAST_AHEAD = 6

            def next_w(prefetch=True):
                for m_ in range(wq['pos'], min(len(worder), wq['pos'] + CAST_AHEAD)):
                    issue_cast(worder[m_][4])
                if wq['pending'] is None:
                    wq['pending'] = load_w(*worder[wq['pos']])
                cur = wq['pending']
                wq['pending'] = None
                wq['pos'] += 1
                if prefetch:
                    prefetch_w()
                return cur

            def prefetch_w():
                if wq['pending'] is None and wq['pos'] < len(worder):
                    wq['pending'] = load_w(*worder[wq['pos']])

            csb = tmpa[0:6, :, :].rearrange("p a n -> p (a n)")
            P.dma('sp', lambda e: e.dma_start(out=csb, in_=cc), 'cc', writes=['tmpa'])
            P.op('act', lambda e: e.activation(out=csb, in_=csb, func=AF.Silu), reads=['tmpa'], writes=['tmpa'])
            b0, pb0, k0 = 6, bank(6), 'ps6'
            def f_ct(e):
                ins = None
                for kc in range(8):
                    ins = e.transpose(pb0[:, kc * 6:kc * 6 + 6], csb[:, kc * 128:(kc + 1) * 128], ident[0:6, 0:6])
                return ins
            P.op('pe', f_ct, reads=['tmpa', 'ident'], writes=[k0])
            scT = tmpb[:, 0, 0:48].rearrange("p (k r) -> p k r", r=6)
            P.op('dve', lambda e: e.tensor_copy(out=scT, in_=pb0[:, 0:48].rearrange("p (k r) -> p k r", r=6)),
                 reads=[k0], writes=['tmpb'])
            mst = stg[0:6, :]
            bst = xn[0:6, 0, 0:512]
            bT, pbT, kT_ = 7, bank(7), 'ps7'
            for blk in range(12):
                wv, wk = ada_views[blk]
                b1, pb1, k1 = blk % 4, bank(blk % 4), 'ps%d' % (blk % 4)
                def f_mm(e, wv=wv, pb1=pb1):
                    ins = None
                    for kc in range(8):
                        ins = e.matmul(pb1[0:6, :], lhsT=scT[:, kc, :], rhs=wv[:, kc, :], start=(kc == 0), stop=(kc == 7))
                    return ins
                P.op('pe', f_mm, reads=['tmpb', wk], writes=[k1])
                P.dma('sp', lambda e, blk=blk: e.dma_start(out=bst, in_=dap(b_ada, blk * 512, [[0, 6], [1, 512]])),
                      'bst', writes=['bst'])
                P.op('dve', lambda e, pb1=pb1: e.tensor_tensor(out=mst, in0=pb1[0:6, :], in1=bst, op=ALU.add),
                     reads=[k1, 'bst'], writes=['mst'])
                P.dma('sp', lambda e, blk=blk: e.dma_start(out=mod_d[:, blk * 512:(blk + 1) * 512], in_=mst),
                      'modd', reads=['mst'], writes=['mod_d'])
                def f_tp(e, blk=blk):
                    ins = None
                    for q in range(4):
                        ft = blk * 4 + q
                        ins = e.transpose(pbT[:, ft * 6:ft * 6 + 6], mst[:, q * 128:(q + 1) * 128], ident[0:6, 0:6])
                    return ins
                P.op('pe', f_tp, reads=['mst', 'ident'], writes=[kT_])
            P.op('dve', lambda e: e.tensor_copy(out=modT[:].rearrange("p k r -> p (k r)"), in_=pbT[:, 0:288]),
                 reads=[kT_], writes=['modT'])
            for sec in (1, 4):
                P.op('dve', lambda e, sec=sec: e.tensor_scalar(out=modT[:, sec * 8:sec * 8 + 8, :], in0=modT[:, sec * 8:sec * 8 + 8, :],
                                                                scalar1=1.0, scalar2=None, op0=ALU.add),
                     reads=['modT'], writes=['modT'])

            ckpt(1)
            rbs = tmpa[0:8, 0, :]
            ext = tmpa[0:8, 1, :]
            ext = ssmf[0:8, 0:2, :].rearrange("p a n -> p (a n)")[:, 0:768]
            extb = tmpb[0:8, :, :].rearrange("p a n -> p (a n)")[:, 0:768]
            P.dma('sp', lambda e: e.dma_start(out=ext[:, 0:384], in_=rel_bias[:, 129:513]), 'rb', writes=['ext'])
            rlast = tmpa[0:8, 0, 0:1]
            P.dma('sp', lambda e: e.dma_start(out=rlast, in_=dap(rel_bias, 512, [[513, 8], [1, 1]]), allow_slow_non_contiguous=True), 'rb2', writes=['rlast'])
            P.op('dve', lambda e: e.tensor_copy(out=ext[:, 384:768], in_=rlast.to_broadcast([8, 384])), reads=['rlast'], writes=['ext2'])
            P.op('act', lambda e: e.activation(out=extb, in_=ext, func=AF.Copy, scale=8.0), reads=['ext', 'ext2', 'tmpb'], writes=['tmpb'])
            P.dma('sp', lambda e: e.dma_start(out=ext_d, in_=extb.unsqueeze(1).to_broadcast([8, 128, 768])), 'extd', reads=['tmpb'], writes=['ext_d'])
            P.dma('sp', lambda e: e.dma_start(out=Eh[:, :, :], in_=dap(ext_d, 127, [[767, 128], [128 * 768, 8], [1, 640]])), 'ehrow',
                  reads=['ext_d'], writes=['Eh'])
            P.op('pool', lambda e: e.memset(Eh[0:64, :, 576:640], -30000.0), reads=['Eh'], writes=['Eh'])
            P.op('pool', lambda e: e.memset(Eh[64:128, :, 0:64], -30000.0), reads=['Eh'], writes=['Eh'])

            ckpt(2)
            sA = ssmf[:, 2, :]
            are = sA[:, 0:16]; aim = sA[:, 16:32]; dtt = sA[:, 32:48]; th = sA[:, 48:64]
            fre = sA[:, 64:80]; fim = sA[:, 80:96]; den = sA[:, 96:112]; t0 = sA[:, 112:128]; t1_ = sA[:, 128:144]
            abr = sA[:, 144:160]; abi = sA[:, 160:176]
            A2 = ssmf[0:32, 3, 0:256]
            A2v = A2.rearrange("g (a d p) -> g a d p", a=2, d=2)
            for ai, arr in enumerate((a_re, a_im)):
                for d_ in range(2):
                    P.dma('sp', lambda e, ai=ai, arr=arr, d_=d_: e.dma_start(out=A2v[:, ai, d_, :], in_=arr), 'ssmld', writes=['A2_%d%d' % (ai, d_)])
            for ai, dstA, nm in ((0, are, 'are'), (1, aim, 'aim')):
                pbA = bank(ai)
                P.op('pe', lambda e, ai=ai, pbA=pbA: e.transpose(pbA[:, 0:32], A2v[:, ai, :, :].rearrange("g d p -> g (d p)"), ident[0:32, 0:32]),
                     reads=['A2_%d0' % ai, 'A2_%d1' % ai, 'ident'], writes=['ps%d' % ai])
                for two in range(2):
                    P.op('dve', lambda e, pbA=pbA, dstA=dstA, two=two: e.tensor_copy(out=dstA[two * 64:(two + 1) * 64, :], in_=pbA[two * 64:(two + 1) * 64, two:32:2]),
                         reads=['ps%d' % ai], writes=[nm])
            Lbc = ssmf[:, 3, 256:288]
            P.dma('sp', lambda e: e.dma_start(out=Lbc, in_=dap(log_dt, 0, [[0, 128], [1, 32]])), 'ssmld3', writes=['Lbc'])
            for two in range(2):
                P.op('dve', lambda e, two=two: e.tensor_copy(out=dtt[two * 64:(two + 1) * 64, :], in_=Lbc[two * 64:(two + 1) * 64, two:32:2]),
                     reads=['Lbc'], writes=['dtt%d' % two])
            P.op('act', lambda e: e.activation(out=dtt, in_=dtt, func=AF.Exp), reads=['dtt0', 'dtt1'], writes=['dtt'])
            P.op('dve', lambda e: e.tensor_tensor(out=t0, in0=dtt, in1=are, op=ALU.mult), reads=['dtt', 'are'], writes=['t0'])
            P.op('act', lambda e: e.activation(out=rco[:], in_=t0, func=AF.Exp), reads=['t0'], writes=['rco'])
            P.op('dve', lambda e: e.tensor_tensor(out=th, in0=dtt, in1=aim, op=ALU.mult), reads=['dtt', 'aim'], writes=['th'])
            P.barrier()
            iot = ssmf[:, 3, 0:256]
            P.op('pool', lambda e: e.iota(iot, pattern=[[1, 256]], base=1, channel_multiplier=0,
                                          allow_small_or_imprecise_dtypes=True), reads=['ssmf3'], writes=['iot'])
            ang = ssmf[:, 4:6, :].rearrange("p a n -> p (a n)")
            kq = ssmf[:, 6:8, :].rearrange("p a n -> p (a n)")
            kqi = kq.bitcast(I32)
            mq = ssmf[:, 0:2, :].rearrange("p a n -> p (a n)")
            C1 = 6.28125
            C2 = float(np.float32(TWO_PI - C1))
            C3 = float(TWO_PI - C1 - C2)
            for jg in range(4):
                for jj in range(4):
                    j = jg * 4 + jj
                    P.op('dve', lambda e, j=j, jj=jj: e.tensor_scalar(out=ang[:, jj * 256:(jj + 1) * 256], in0=iot, scalar1=th[:, j:j + 1],
                                                                     scalar2=None, op0=ALU.mult),
                         reads=['iot', 'th'], writes=['ang'])
                P.op('dve', lambda e: e.tensor_scalar(out=kqi, in0=ang, scalar1=1.0 / TWO_PI, scalar2=None, op0=ALU.mult),
                     reads=['ang'], writes=['kq'])
                P.op('dve', lambda e: e.tensor_copy(out=kq, in_=kqi), reads=['kq'], writes=['kq'])
                P.op('dve', lambda e: e.scalar_tensor_tensor(out=ang, in0=kq, scalar=-C1, in1=ang, op0=ALU.mult, op1=ALU.add),
                     reads=['ang', 'kq'], writes=['ang'])
                P.op('dve', lambda e: e.scalar_tensor_tensor(out=ang, in0=kq, scalar=-C2, in1=ang, op0=ALU.mult, op1=ALU.add),
                     reads=['ang', 'kq'], writes=['ang'])
                for (tab, shift, nm) in ((Stab, 0.0, 'Stab'), (Ctab, math.pi / 2, 'Ctab')):
                    P.op('dve', lambda e, shift=shift: e.tensor_scalar(out=kq, in0=ang, scalar1=shift, scalar2=None, op0=ALU.add),
                         reads=['ang', 'Stab', 'Ctab'], writes=['kq'])
                    for (cmp_, thr, corr) in ((ALU.is_gt, math.pi, -TWO_PI), (ALU.is_lt, -math.pi, TWO_PI),
                                              (ALU.is_gt, math.pi, -TWO_PI)):
                        P.op('dve', lambda e, cmp_=cmp_, thr=thr: e.tensor_scalar(out=mq, in0=kq, scalar1=thr, scalar2=None, op0=cmp_),
                             reads=['kq'], writes=['mq'])
                        P.op('dve', lambda e, corr=corr: e.scalar_tensor_tensor(out=kq, in0=mq, scalar=corr, in1=kq, op0=ALU.mult, op1=ALU.add),
                             reads=['kq', 'mq'], writes=['kq'])
                    P.op('act', lambda e, tab=tab, jg=jg: e.activation(
                        out=tab[:, jg * 4:jg * 4 + 4, :].rearrange("p a n -> p (a n)"), in_=kq, func=AF.Sin),
                        reads=['kq'], writes=[nm])
            ckpt(3)
            for pt_i, tl in ((0, 256), (1, 64)):
                P.op('dve', lambda e, pt_i=pt_i, tl=tl: e.tensor_copy(out=cend[:, pt_i, 0, :], in_=Ctab[:, :, tl - 1]),
                     reads=['Ctab'], writes=['cend'])
                P.op('dve', lambda e, pt_i=pt_i, tl=tl: e.tensor_copy(out=cend[:, pt_i, 1, :], in_=Stab[:, :, tl - 1]),
                     reads=['Stab'], writes=['cend'])
                P.op('dve', lambda e, pt_i=pt_i, tl=tl: e.tensor_scalar(out=cend[:, pt_i, 2, :], in0=Stab[:, :, tl - 1],
                                                                       scalar1=-1.0, scalar2=None, op0=ALU.mult),
                     reads=['Stab'], writes=['cend'])
            P.op('dve', lambda e: e.tensor_tensor(out=abr, in0=rco[:], in1=Ctab[:, :, 0], op=ALU.mult), reads=['rco', 'Ctab'], writes=['abr'])
            P.op('dve', lambda e: e.tensor_tensor(out=abi, in0=rco[:], in1=Stab[:, :, 0], op=ALU.mult), reads=['rco', 'Stab'], writes=['abi'])
            P.op('dve', lambda e: e.tensor_scalar(out=abr, in0=abr, scalar1=-1.0, scalar2=None, op0=ALU.add), reads=['abr'], writes=['abr'])
            P.op('dve', lambda e: e.tensor_tensor(out=den, in0=are, in1=are, op=ALU.mult), reads=['are'], writes=['den'])
            P.op('dve', lambda e: e.tensor_tensor(out=t0, in0=aim, in1=aim, op=ALU.mult), reads=['aim', 'rco'], writes=['t0'])
            P.op('dve', lambda e: e.tensor_tensor(out=den, in0=den, in1=t0, op=ALU.add), reads=['den', 't0'], writes=['den'])
            P.op('dve', lambda e: e.reciprocal(out=den, in_=den), reads=['den'], writes=['den'])
            P.op('dve', lambda e: e.tensor_tensor(out=fre, in0=abr, in1=are, op=ALU.mult), reads=['abr', 'are'], writes=['fre'])
            P.op('dve', lambda e: e.tensor_tensor(out=t0, in0=abi, in1=aim, op=ALU.mult), reads=['abi', 'aim', 'den'], writes=['t0'])
            P.op('dve', lambda e: e.tensor_tensor(out=fre, in0=fre, in1=t0, op=ALU.add), reads=['fre', 't0'], writes=['fre'])
            P.op('dve', lambda e: e.tensor_tensor(out=fre, in0=fre, in1=den, op=ALU.mult), reads=['fre', 'den'], writes=['fre'])
            P.op('dve', lambda e: e.tensor_tensor(out=fim, in0=abi, in1=are, op=ALU.mult), reads=['abi', 'are'], writes=['fim'])
            P.op('dve', lambda e: e.tensor_tensor(out=t0, in0=abr, in1=aim, op=ALU.mult), reads=['abr', 'aim', 'fre'], writes=['t0'])
            P.op('dve', lambda e: e.tensor_tensor(out=fim, in0=fim, in1=t0, op=ALU.subtract), reads=['fim', 't0'], writes=['fim'])
            P.op('dve', lambda e: e.tensor_tensor(out=fim, in0=fim, in1=den, op=ALU.mult), reads=['fim', 'den'], writes=['fim'])
            ckpt(4)
            Bre = ssmf[:, 4, 0:256].rearrange("p (j m) -> p j m", m=16)
            Bim = ssmf[:, 5, 0:256].rearrange("p (j m) -> p j m", m=16)
            bbr = ssmf[:, 6, 0:256].rearrange("p (j m) -> p j m", m=16)
            bbi = ssmf[:, 7, 0:256].rearrange("p (j m) -> p j m", m=16)
            btm = ssmf[:, 3, 256:512].rearrange("p (j m) -> p j m", m=16)
            P.dma('sp', lambda e: e.dma_start(out=Bre, in_=dap(b_re, 0, [[16, 128], [2048, 16], [1, 16]])), 'ssmld4',
                  reads=['Stab', 'Ctab', 'ang', 'kq'], writes=['Bre'])
            P.dma('sp', lambda e: e.dma_start(out=Bim, in_=dap(b_im, 0, [[16, 128], [2048, 16], [1, 16]])), 'ssmld5',
                  reads=['Stab', 'Ctab', 'ang', 'kq'], writes=['Bim'])
            freb = fre.unsqueeze(2).to_broadcast([128, 16, 16])
            fimb = fim.unsqueeze(2).to_broadcast([128, 16, 16])
            P.op('dve', lambda e: e.tensor_tensor(out=bbr, in0=Bre, in1=freb, op=ALU.mult), reads=['Bre', 'fre', 'kq'], writes=['bbr'])
            P.op('dve', lambda e: e.tensor_tensor(out=btm, in0=Bim, in1=fimb, op=ALU.mult), reads=['Bim', 'fim', 'iot'], writes=['btm'])
            P.op('dve', lambda e: e.tensor_tensor(out=bbr, in0=bbr, in1=btm, op=ALU.subtract), reads=['bbr', 'btm'], writes=['bbr'])
            P.op('dve', lambda e: e.tensor_tensor(out=bbi, in0=Bim, in1=freb, op=ALU.mult), reads=['Bim', 'fre', 'kq'], writes=['bbi'])
            P.op('dve', lambda e: e.tensor_tensor(out=btm, in0=Bre, in1=fimb, op=ALU.mult), reads=['Bre', 'fim', 'bbr'], writes=['btm'])
            P.op('dve', lambda e: e.tensor_tensor(out=bbi, in0=bbi, in1=btm, op=ALU.add), reads=['bbi', 'btm'], writes=['bbi'])
            ckpt(5)
            Mbig = xt[:, :, :].rearrange("p a n -> p (a n)")
            Mv = Mbig.rearrange("p (j r c) -> p j r c", r=2, c=128)
            P.op('pool', lambda e: e.memset(Mbig, 0.0), writes=['xt'])
            for ri, bb in ((0, bbr), (1, bbi)):
                for jj in range(4):
                    for two in range(2):
                        c0 = 32 * jj + 16 * two
                        P.op('dve', lambda e, ri=ri, bb=bb, jj=jj, two=two, c0=c0: e.tensor_copy(
                            out=Mv[two * 64:(two + 1) * 64, jj::4, ri, c0:c0 + 16], in_=bb[two * 64:(two + 1) * 64, jj::4, :]),
                            reads=['bbr', 'bbi', 'xt'], writes=['xt'])
            ckpt(6)
            for j in range(16):
                b2, pb2, k2 = nbank()
                def f_t2(e, j=j, pb2=pb2):
                    e.transpose(pb2[:, 0:128], Mv[:, j, 0, :], ident[:])
                    return e.transpose(pb2[:, 128:256], Mv[:, j, 1, :], ident[:])
                P.op('pe', f_t2, reads=['xt', 'ident'], writes=[k2])
                P.op('act', lambda e, j=j, pb2=pb2: e.activation(out=BbT[:, j, :, :].rearrange("p r c -> p (r c)"), in_=pb2[:, 0:256], func=AF.Copy),
                     reads=[k2], writes=['BbT'])
            ckpt(7)
            Cn = xt[:, 0:2, :].rearrange("p a n -> p (a n)")
            Cnv = Cn[:, 0:1024].rearrange("p (r f d q) -> p r f d q", r=2, f=4, d=2)
            for ri, cw in ((0, c_re), (1, c_im)):
                for d_ in range(2):
                    P.dma('sp', lambda e, ri=ri, cw=cw, d_=d_: e.dma_start(out=Cnv[:, ri, :, d_, :], in_=dap(cw, 0, [[64, 128], [8192, 4], [1, 64]])),
                          'cld', reads=['bst', 'BbT'], writes=['Cn%d%d' % (ri, d_)])
            ckpt(8)
            CTall = ssmf[:, 4:6, :].rearrange("p a n -> p (a n)").rearrange("p (r f c) -> p r f c", r=2, f=4)
            for ri in range(2):
                b3, pb3, k3 = nbank()
                def f_t3(e, ri=ri, pb3=pb3):
                    ins = None
                    for ft in range(4):
                        ins = e.transpose(pb3[:, ft * 128:(ft + 1) * 128], Cnv[:, ri, ft, :, :].rearrange("p d q -> p (d q)"), ident[:])
                    return ins
                P.op('pe', f_t3, reads=['Cn%d0' % ri, 'Cn%d1' % ri, 'ident'], writes=[k3])
                P.op('act', lambda e, ri=ri, pb3=pb3: e.activation(out=CTall[:, ri, :, :].rearrange("p f c -> p (f c)"), in_=pb3,
                                                                  func=AF.Copy, scale=(1.0 if ri == 0 else -1.0)),
                     reads=[k3, 'bbr', 'bbi', 'Bre', 'Bim'], writes=['CTall'])
            ckpt(9)
            P.op('pool', lambda e: e.memset(CTw[:].rearrange("p j r c -> p (j r c)"), 0.0), writes=['CTw'])
            CTv = CTw[:].rearrange("p (f q) r c -> p f q r c", q=4)
            for ri in range(2):
                for jj in range(4):
                    for two in range(2):
                        c0 = 32 * jj + 16 * two
                        g0 = (2 * jj + two) * 16
                        P.op('dve', lambda e, ri=ri, jj=jj, two=two, c0=c0, g0=g0: e.tensor_copy(
                            out=CTv[two * 64:(two + 1) * 64, :, jj, ri, c0:c0 + 16], in_=CTall[two * 64:(two + 1) * 64, ri, :, g0:g0 + 16]),
                            reads=['CTall', 'CTw'], writes=['CTw'])
            ckpt(10)
            D4 = ssmf[0:4, 0, 0:128]
            P.dma('sp', lambda e: e.dma_start(out=D4, in_=ssm_d.rearrange("(f p) -> f p", p=128)), 'dcol', writes=['D4'])
            P.op('pe', lambda e: e.transpose(bank(0)[:, 0:4], D4, ident[0:4, 0:4]), reads=['D4', 'ident'], writes=['ps0'])
            P.op('dve', lambda e: e.tensor_copy(out=Dcol[:], in_=bank(0)[:, 0:4]), reads=['ps0'], writes=['Dcol'])
            cast_blk(w_attn, D, 4, 0, 0, 1024, s_attn[0], 's_attn0')
            for b in range(2):
                cast_blk(w_glu, 2 * D, 4, 0, b * 512, 512, s_glu[b], 's_glu%d' % b, 0, 512)
                cast_blk(w_glu, 2 * D, 4, 0, D + b * 512, 512, s_glu[b], 's_glu%d' % b, 512, 512)
            for b in range(2):
                cast_blk(w_out, D, 8, 0, b * 512, 512, s_out[b], 's_out%d' % b)
            for b in range(11):
                cast_blk(w_ffn_in, 2 * DFF, 8, 0, b * 256, 256, s_fin[b], 's_fin%d' % b, 0, 256)
                cast_blk(w_ffn_in, 2 * DFF, 8, 0, DFF + b * 256, 256, s_fin[b], 's_fin%d' % b, 256, 256)
            KPARTS = [(0, 6), (6, 5), (11, 6), (17, 5)]
            for cb in range(2):
                for kh, (k0_, kn_) in enumerate(KPARTS):
                    cast_blk(w_ffn_out, D, kn_, k0_, cb * 512, 512, s_fout[cb * 4 + kh][:, 0:kn_, :], 's_fout%d' % (cb * 4 + kh))

            ckpt(11)
            def ln_stats(tt, src_key, src=None, sm=None, bs=None, kp=''):
                src = xt[:, tt, :] if src is None else src
                sm = small if sm is None else sm
                bs = bnst if bs is None else bs
                kb, ks_ = 'bnst%s%d' % (kp, tt), 'small%s%d' % (kp, tt)
                for h in range(2):
                    P.op('dve', lambda e, tt=tt, h=h: e.bn_stats(out=bs[:, tt, h * 6:h * 6 + 6], in_=src[:, h * 512:(h + 1) * 512]),
                         reads=[src_key], writes=[kb])
                P.op('dve', lambda e, tt=tt: e.bn_aggr(out=sm[:, tt, 0:2], in_=bs[:, tt, :]), reads=[kb], writes=[ks_])
                P.op('act', lambda e, tt=tt: e.activation(out=sm[:, tt, 2:3], in_=sm[:, tt, 1:2], func=AF.Sqrt, bias=epsT[:], scale=1.0),
                     reads=[ks_, 'epsT'], writes=[ks_])
                P.op('dve', lambda e, tt=tt: e.reciprocal(out=sm[:, tt, 3:4], in_=sm[:, tt, 2:3]), reads=[ks_], writes=[ks_])
                P.op('dve', lambda e, tt=tt: e.scalar_tensor_tensor(out=sm[:, tt, 4:5], in0=sm[:, tt, 0:1], scalar=-1.0,
                                                                   in1=sm[:, tt, 3:4], op0=ALU.mult, op1=ALU.mult),
                     reads=[ks_], writes=[ks_])

            def ln_to_featmajor(tt, ntt, segs, sh_sec, sc_sec, src_key, src=None, sm=None, kp='', dest=None, dkeys=('act8',)):
                src = xt[:, tt, :] if src is None else src
                sm = small if sm is None else sm
                dest = act8 if dest is None else dest
                dkeys = list(dkeys)
                ks_ = 'small%s%d' % (kp, tt)
                P.op('act', lambda e, tt=tt: e.activation(out=xn[:, 0, :], in_=src, func=AF.Identity,
                                                         scale=sm[:, tt, 3:4], bias=sm[:, tt, 4:5]),
                     reads=[src_key, ks_, 'xn0', 'xn0b'], writes=['xn0', 'xn0b'])
                nev = 0
                for half in range(2):
                    b, pb, k = nbank()
                    def f_t(e, half=half, pb=pb):
                        ins = None
                        for q in range(4):
                            kc = half * 4 + q
                            ins = e.transpose(pb[:, q * 128:(q + 1) * 128], xn[:, 0, kc * 128:(kc + 1) * 128], ident[:])
                        return ins
                    P.op('pe', f_t, reads=['xn0', 'xn0b', 'ident'], writes=[k])
                    for q in range(4):
                        kc = half * 4 + q
                        for (c0, c1, r) in segs:
                            nev += 1
                            if True:
                                P.op('act', lambda e, pb=pb, q=q, kc=kc, c0=c0, c1=c1, r=r, tt=tt: e.activation(
                                    out=dest[:, kc, tt * 128 + c0:tt * 128 + c1], in_=pb[:, q * 128 + c0:q * 128 + c1], func=AF.Identity,
                                    scale=modT[:, sc_sec * 8 + kc, r:r + 1], bias=modT[:, sh_sec * 8 + kc, r:r + 1]),
                                    reads=[k, 'modT'], writes=dkeys)
                            else:
                                P.op('dve', lambda e, pb=pb, q=q, kc=kc, c0=c0, c1=c1, r=r, tt=tt: e.tensor_scalar(
                                    out=dest[:, kc, tt * 128 + c0:tt * 128 + c1], in0=pb[:, q * 128 + c0:q * 128 + c1],
                                    scalar1=modT[:, sc_sec * 8 + kc, r:r + 1], scalar2=modT[:, sh_sec * 8 + kc, r:r + 1], op0=ALU.mult, op1=ALU.add),
                                    reads=[k, 'modT'], writes=dkeys)

            def ln_affine(tt, gi, bi_):
                P.op('act', lambda e, tt=tt: e.activation(out=xt[:, tt, :], in_=xt[:, tt, :], func=AF.Identity,
                                                         scale=small[:, tt, 3:4], bias=small[:, tt, 4:5]),
                     reads=['xt%d' % tt, 'small%d' % tt], writes=['xt%d' % tt])
                for (eng_, c0_, c1_) in (('dve', 0, 512), ('pool', 512, 1024)):
                    hk = 'xt%d%s' % (tt, 'a' if c0_ == 0 else 'b')
                    P.op(eng_, lambda e, tt=tt, c0_=c0_, c1_=c1_: e.tensor_tensor(out=xt[:, tt, c0_:c1_], in0=xt[:, tt, c0_:c1_], in1=lnbc[:, 0, c0_:c1_], op=ALU.mult),
                         reads=['xt%d' % tt, 'lnbc'], writes=[hk])
                    P.op(eng_, lambda e, tt=tt, c0_=c0_, c1_=c1_: e.tensor_tensor(out=xt[:, tt, c0_:c1_], in0=xt[:, tt, c0_:c1_], in1=lnbc[:, 1, c0_:c1_], op=ALU.add),
                         reads=[hk, 'lnbc'], writes=[hk])
                P.op('dve', lambda e, tt=tt: e.memset(small[:, tt, 6:7], 0.0), reads=['xt%da' % tt, 'xt%db' % tt], writes=['xt%d' % tt])

            def fence(key):
                P.op('dve', lambda e: e.memset(small[:, 0, 7:8], 0.0), writes=[key])

            stgc = [0]

            def tile_info(kind):
                prompt_ = (kind == 'p')
                if prompt_:
                    return 4, {tt: [(0, 128, tt // 2)] for tt in range(4)}
                return 2, {tt: [(0, 64, 2 + 2 * tt), (64, 128, 3 + 2 * tt)] for tt in range(2)}

            def x_rows(kind, ti, tt):
                if kind == 'p':
                    s_, hf = tt // 2, tt % 2
                    return xp[s_, ti * TL + hf * 128: ti * TL + hf * 128 + 128, :]
                return xs[2 * tt:2 * tt + 2, :, :].rearrange("s t d -> (s t) d")

            HTU_KEYS = ['f0', 'f1', 'sr0', 'si0']

            def gen_ln0(kind2, ti2, dest=None, dkeys=('act8',)):
                ntt2, segs2 = tile_info(kind2)
                for tt in range(ntt2):
                    src = x_rows(kind2, ti2, tt)
                    P.dma('act', lambda e, src=src: e.dma_start(out=xn[:, 0, :], in_=src), 'xnld', reads=['xn0b'], writes=['xn0', 'xn0b'])
                    yield
                    ln_stats(tt, 'xn0', src=xn[:, 0, :], sm=small2, bs=bnst2, kp='p')
                    yield
                    ln_to_featmajor(tt, ntt2, segs2[tt], 0, 1, 'xn0', src=xn[:, 0, :], sm=small2, kp='p', dest=dest, dkeys=dkeys)
                    yield

            def gen_ssm(kind):
                prompt = (kind == 'p')
                nseq = 2 if prompt else 4
                tlen = TL if prompt else 64
                ncols = nseq * tlen
                pti = 0 if prompt else 1
                if not prompt:
                    for (src_, dst_, nm) in ((sre, stre, 'stre'), (sim, stim, 'stim')):
                        for d_ in range(2):
                            P.dma('sp', lambda e, src_=src_, d_=d_: e.dma_start(out=stg[:, d_ * 64:(d_ + 1) * 64], in_=src_.rearrange("s g p -> (s g) p")),
                                  'stgin', reads=['stg'], writes=['stg'])
                        b, pb, k = nbank()
                        P.op('pe', lambda e, pb=pb: e.transpose(pb[:, 0:128], stg[:, 0:128], ident[:]), reads=['stg', 'ident'], writes=[k])
                        for two in range(2):
                            P.op('dve', lambda e, pb=pb, dst_=dst_, two=two: e.tensor_copy(
                                out=dst_[two * 64:(two + 1) * 64, :, :],
                                in_=pb[two * 64:(two + 1) * 64, 0:128].rearrange("p (s j t) -> p j s t", s=4, t=2)[:, :, :, two]),
                                reads=[k], writes=[nm])
                def v3(ap):
                    return ap[:, 0:ncols].rearrange("p (s t) -> p s t", t=tlen)
                T = [v3(ssmf[:, i, :]) for i in range(8)]
                ybank = {}
                pending = []

                def emit_y(j):
                    ft, jj = j // 4, j % 4
                    yb, pby, ky = ybank[ft]
                    sb_i = j % 2
                    def f_y(e, j=j, jj=jj, sb_i=sb_i, pby=pby):
                        e.matmul(pby[:, 0:ncols], lhsT=CTw[:, j, 0, :], rhs=ssmb[:, sb_i, 0, 0:ncols], start=(jj == 0), stop=False)
                        return e.matmul(pby[:, 0:ncols], lhsT=CTw[:, j, 1, :], rhs=ssmb[:, sb_i, 1, 0:ncols], start=False, stop=(jj == 3))
                    P.op('pe', f_y, reads=['ssmb%d' % sb_i, 'CTw'], writes=[ky])
                    if jj == 3 and not DBG.get('ssm_noepi'):
                        yv = ssmf[:, 0, 0:ncols]; wv_ = ssmf[:, 1, 0:ncols]
                        P.op('dve', lambda e, ft=ft, pby=pby, yv=yv: e.scalar_tensor_tensor(out=yv, in0=uT[:, ft, 0:ncols], scalar=Dcol[:, ft:ft + 1],
                                                                                         in1=pby[:, 0:ncols], op0=ALU.mult, op1=ALU.add),
                             reads=[ky, 'uT', 'Dcol', 'f0'], writes=['f0'])
                        P.op('act', lambda e, yv=yv, wv_=wv_: e.activation(out=wv_, in_=yv, func=AF.Square), reads=['f0'], writes=['f1'])
                        P.op('act', lambda e, wv_=wv_: e.activation(out=wv_, in_=wv_, func=AF.Identity, scale=0.044715, bias=oneT[:]),
                             reads=['f1', 'oneT'], writes=['f1'])
                        P.op('pool', lambda e, yv=yv, wv_=wv_: e.tensor_tensor(out=wv_, in0=wv_, in1=yv, op=ALU.mult), reads=['f1', 'f0'], writes=['f1'])
                        P.op('act', lambda e, wv_=wv_: e.activation(out=wv_, in_=wv_, func=AF.Sigmoid, scale=1.5957691216057308), reads=['f1'], writes=['f1'])
                        P.op('pool', lambda e, yv=yv, wv_=wv_, ft=ft: e.tensor_tensor(out=gT[:, ft, 0:ncols], in0=wv_, in1=yv, op=ALU.mult),
                             reads=['f1', 'f0'], writes=['gT'])

                for j in range(16):
                    ft, jj = j // 4, j % 4
                    if jj == 0:
                        ybank[ft] = (3, bank(3), 'ps3')
                    b1, pbr, kr = 4, bank(4), 'ps4'
                    b2, pbi, ki = 5, bank(5), 'ps5'
                    if DBG.get('ssm_lvl', 9) < 0:
                        yield
                        continue
                    P.op('pe', lambda e, j=j, ft=ft, pbr=pbr: e.matmul(pbr[:, 0:ncols], lhsT=BbT[:, j, 0, :], rhs=uT[:, ft, 0:ncols], start=True, stop=True),
                         reads=['uT', 'BbT'], writes=[kr])
                    P.op('pe', lambda e, j=j, ft=ft, pbi=pbi: e.matmul(pbi[:, 0:ncols], lhsT=BbT[:, j, 1, :], rhs=uT[:, ft, 0:ncols], start=True, stop=True),
                         reads=['uT', 'BbT'], writes=[ki])
                    yield
                    if DBG.get('ssm_lvl', 9) < 1:
                        continue
                    Cb = Ctab[:, j:j + 1, 0:tlen].to_broadcast([128, nseq, tlen])
                    Sb = Stab[:, j:j + 1, 0:tlen].to_broadcast([128, nseq, tlen])
                    sbf = j % 2
                    SR, SI = 2 + 2 * sbf, 3 + 2 * sbf
                    kSR, kSI = 'sr%d' % sbf, 'si%d' % sbf
                    P.op('dve', lambda e, Sb=Sb, pbr=pbr, SR=SR: e.tensor_tensor(out=T[SR], in0=v3(pbr), in1=Sb, op=ALU.mult), reads=[kr, 'Stab', kSR], writes=[kSR])
                    P.op('dve', lambda e, Cb=Cb, pbr=pbr: e.tensor_tensor(out=v3(pbr), in0=v3(pbr), in1=Cb, op=ALU.mult), reads=[kr, 'Ctab'], writes=[kr])
                    P.op('dve', lambda e, Sb=Sb, pbi=pbi: e.tensor_tensor(out=T[1], in0=v3(pbi), in1=Sb, op=ALU.mult), reads=[ki, 'Stab', 'f1'], writes=['f1'])
                    P.op('dve', lambda e, pbr=pbr: e.tensor_tensor(out=T[0], in0=v3(pbr), in1=T[1], op=ALU.add), reads=[kr, 'f1', 'f0'], writes=['f0'])
                    P.op('dve', lambda e, Cb=Cb, pbi=pbi: e.tensor_tensor(out=v3(pbi), in0=v3(pbi), in1=Cb, op=ALU.mult), reads=[ki, 'Ctab'], writes=[ki])
                    P.op('dve', lambda e, pbi=pbi, SR=SR: e.tensor_tensor(out=T[1], in0=v3(pbi), in1=T[SR], op=ALU.subtract), reads=[ki, kSR, 'f0', 'f1'], writes=['f1'])
                    yield
                    if DBG.get('ssm_lvl', 9) < 2:
                        continue
                    for s in range(nseq):
                        cs = slice(s * tlen, (s + 1) * tlen)
                        rb_ = rco[:, j:j + 1].to_broadcast([128, tlen])
                        P.op('dve', lambda e, cs=cs, rb_=rb_, j=j, s=s, SR=SR: e.tensor_tensor_scan(
                            out=ssmf[:, SR, cs], data0=rb_, data1=ssmf[:, 0, cs], initial=stre[:, j, s:s + 1], op0=ALU.mult, op1=ALU.add),
                            reads=['f0', 'rco', 'stre', kSR], writes=[kSR])
                        P.op('dve', lambda e, cs=cs, rb_=rb_, j=j, s=s, SI=SI: e.tensor_tensor_scan(
                            out=ssmf[:, SI, cs], data0=rb_, data1=ssmf[:, 1, cs], initial=stim[:, j, s:s + 1], op0=ALU.mult, op1=ALU.add),
                            reads=['f1', 'rco', 'stim', kSI], writes=[kSI])
                    yield
                    if DBG.get('ssm_lvl', 9) < 3:
                        continue
                    er = ssmf[:, SR, tlen - 1:ncols:tlen]
                    ei = ssmf[:, SI, tlen - 1:ncols:tlen]
                    cE = cend[:, pti, 0, j:j + 1]; sE = cend[:, pti, 1, j:j + 1]; nsE = cend[:, pti, 2, j:j + 1]
                    tAv = bnst[:, 0, 0:nseq]
                    tBv = bnst[:, 1, 0:nseq]
                    P.op('act', lambda e, er=er, cE=cE, tAv=tAv: e.activation(out=tAv, in_=er, func=AF.Copy, scale=cE),
                         reads=[kSR, 'cend', 'bnst0'], writes=['bnst0'])
                    P.op('act', lambda e, er=er, sE=sE, tBv=tBv: e.activation(out=tBv, in_=er, func=AF.Copy, scale=sE),
                         reads=[kSR, 'cend', 'bnst1'], writes=['bnst1'])
                    P.op('dve', lambda e, ei=ei, nsE=nsE, tAv=tAv, j=j: e.scalar_tensor_tensor(out=stre[:, j, 0:nseq], in0=ei, scalar=nsE, in1=tAv,
                                                                                          op0=ALU.mult, op1=ALU.add),
                         reads=[kSI, 'bnst0', 'cend'], writes=['stre'])
                    P.op('dve', lambda e, ei=ei, cE=cE, tBv=tBv, j=j: e.scalar_tensor_tensor(out=stim[:, j, 0:nseq], in0=ei, scalar=cE, in1=tBv,
                                                                                         op0=ALU.mult, op1=ALU.add),
                         reads=[kSI, 'bnst1', 'cend'], writes=['stim'])
                    if DBG.get('ssm_lvl', 9) < 4:
                        continue
                    sb_i = j % 2
                    srb = v3(ssmb[:, sb_i, 0, :]); sib = v3(ssmb[:, sb_i, 1, :])
                    P.op('pool', lambda e, Cb=Cb, SR=SR: e.tensor_tensor(out=T[6], in0=T[SR], in1=Cb, op=ALU.mult), reads=[kSR, 'Ctab', 'f6'], writes=['f6'])
                    P.op('pool', lambda e, Sb=Sb, SI=SI: e.tensor_tensor(out=T[7], in0=T[SI], in1=Sb, op=ALU.mult), reads=[kSI, 'Stab', 'f7'], writes=['f7'])
                    P.op('pool', lambda e, srb=srb: e.tensor_tensor(out=srb, in0=T[6], in1=T[7], op=ALU.subtract), reads=['f6', 'f7'], writes=['ssmb%d' % sb_i])
                    P.op('pool', lambda e, Sb=Sb, SR=SR: e.tensor_tensor(out=T[6], in0=T[SR], in1=Sb, op=ALU.mult), reads=[kSR, 'Stab', 'f6'], writes=['f6'])
                    P.op('pool', lambda e, Cb=Cb, SI=SI: e.tensor_tensor(out=T[7], in0=T[SI], in1=Cb, op=ALU.mult), reads=[kSI, 'Ctab', 'f7'], writes=['f7'])
                    P.op('pool', lambda e, sib=sib: e.tensor_tensor(out=sib, in0=T[6], in1=T[7], op=ALU.add), reads=['f6', 'f7'], writes=['ssmb%d' % sb_i])
                    yield
                    if DBG.get('ssm_lvl', 9) < 5:
                        continue
                    pending.append(j)
                    if len(pending) > DBG.get('ylag', YLAG):
                        emit_y(pending.pop(0))
                        yield
                while pending:
                    emit_y(pending.pop(0))
                    yield


            ssm_live = {}

            hTu = ssmf[:, 0:4, :].rearrange("p a n -> p (a n)").bitcast(BF16).rearrange("p (k n) -> p k n", n=512)

            def s4_u(kind2, src, skeys):
                nc2 = 512 if kind2 == 'p' else 256
                wv, wk = next_w()
                for ft in range(4):
                    b, pb, k = nbank()
                    def f_mm(e, wv=wv, pb=pb, ft=ft):
                        ins = None
                        for kc in range(8):
                            ins = e.matmul(pb[:, 0:nc2], lhsT=wv[:, kc, ft * 128:(ft + 1) * 128], rhs=src[:, kc, 0:nc2],
                                           start=(kc == 0), stop=(kc == 7))
                        return ins
                    P.op('pe', f_mm, reads=list(skeys) + [wk], writes=[k])
                    P.op('act', lambda e, pb=pb, ft=ft: e.activation(out=uT[:, ft, 0:nc2], in_=pb[:, 0:nc2], func=AF.Copy),
                         reads=[k], writes=['uT', 'tmpa', 'tmpa0', 'tmpa1'])

            def run_tile(kind, ti, pre=False, nxt=None, early=False, last_prompt=False):
                prompt = (kind == 'p')
                nseq = 2 if prompt else 4
                tlen = TL if prompt else 64
                ncols = nseq * tlen
                ntt = ncols // 128
                pti = 0 if prompt else 1
                rows = [0, 1] if prompt else [2, 3, 4, 5]
                if prompt:
                    segs = {tt: [(0, 128, tt // 2)] for tt in range(4)}
                else:
                    segs = {tt: [(0, 64, 2 + 2 * tt), (64, 128, 3 + 2 * tt)] for tt in range(2)}

                def load_gate(sec):
                    for slot in range(2):
                        if prompt:
                            P.dma('sp', lambda e, sec=sec, slot=slot: e.dma_start(
                                out=gbc[:, 0, slot, :], in_=dap(mod_d, slot * 6 * D + sec * D, [[0, 128], [1, D]])),
                                'gbc', reads=['mod_d'], writes=['gbc'])
                        else:
                            for hf in range(2):
                                r = 2 + 2 * slot + hf
                                P.dma('sp', lambda e, sec=sec, slot=slot, hf=hf, r=r: e.dma_start(
                                    out=gbc[hf * 64:(hf + 1) * 64, 0, slot, :], in_=dap(mod_d, r * 6 * D + sec * D, [[0, 64], [1, D]])),
                                    'gbc', reads=['mod_d'], writes=['gbc'])

                def load_x():
                  for tt in range(ntt):
                      if prompt:
                          s, hf = tt // 2, tt % 2
                          src = xp[s, ti * TL + hf * 128: ti * TL + hf * 128 + 128, :]
                      else:
                          src = xs[2 * tt:2 * tt + 2, :, :].rearrange("s t d -> (s t) d")
                      P.dma('sp', lambda e, tt=tt, src=src: e.dma_start(out=xt[:, tt, :], in_=src), 'xt%d' % tt,
                            writes=['xt%d' % tt])
                if not pre:
                    load_x()
                if not pre:
                    for tt in range(ntt):
                        ln_stats(tt, 'xt%d' % tt)
                        ln_to_featmajor(tt, ntt, segs[tt], 0, 1, 'xt%d' % tt)

                tck(20)
                fence('R1')
                def s4_block(blk):
                    wv, wk = next_w()
                    if blk == 2 or (blk == 1 and (not prompt or ti >= 6)):
                        need_out = (not prompt) or ti >= 6
                        for tt in range(ntt):
                            b, pb, k = nbank()
                            def f_mm(e, wv=wv, pb=pb, tt=tt):
                                ins = None
                                for kc in range(8):
                                    ins = e.matmul(pb[:, :], lhsT=act8[:, kc, tt * 128:(tt + 1) * 128], rhs=wv[:, kc, :],
                                                   start=(kc == 0), stop=(kc == 7))
                                return ins
                            P.op('pe', f_mm, reads=['act8', wk], writes=[k])
                            if blk == 2:
                                if prompt:
                                    slot = (tt // 2) * 6 + ((2 * ti + tt % 2) % 6)
                                    P.op('act', lambda e, pb=pb, slot=slot: e.activation(out=vring[:, slot, :], in_=pb, func=AF.Copy),
                                         reads=[k], writes=['vring'])
                                elif not DBG.get('novnew'):
                                    P.op('act', lambda e, pb=pb, tt=tt: e.activation(out=sga[:, 2 * tt:2 * tt + 2, 256:512], in_=pb.rearrange("p (a n) -> p a n", n=256), func=AF.Copy),
                                         reads=[k, 'R1'], writes=['vnew'])
                            if need_out and not (DBG.get('noout2') and blk == 2):
                                sgi = 0
                                stgc[0] += 1
                                P.op('dve', lambda e, pb=pb, sgi=sgi: e.tensor_copy(out=stg2[:, sgi, :], in_=pb), reads=[k], writes=['stg' if sgi == 0 else 'stg2_1'])
                                if prompt:
                                    s_, hf = tt // 2, tt % 2
                                    r0 = (ti - 6) * TL + hf * 128
                                    dst = (kp if blk == 1 else vp)[s_, r0:r0 + 128, :]
                                else:
                                    dst = (ks if blk == 1 else vs)[2 * tt:2 * tt + 2, :, :].rearrange("s t d -> (s t) d")
                                P.dma('sp', lambda e, dst=dst, sgi=sgi: e.dma_start(out=dst, in_=stg2[:, sgi, :]), 'stgout%d' % sgi, reads=['stg' if sgi == 0 else 'stg2_1'])
                            yield
                        if blk == 2:
                            return
                    for ft in range(4):
                        b, pb, k = nbank()
                        def f_mm(e, wv=wv, pb=pb, ft=ft):
                            ins = None
                            for kc in range(8):
                                ins = e.matmul(pb[:, 0:ncols], lhsT=wv[:, kc, ft * 128:(ft + 1) * 128], rhs=act8[:, kc, 0:ncols],
                                               start=(kc == 0), stop=(kc == 7))
                            return ins
                        P.op('pe', f_mm, reads=['act8', wk], writes=[k])
                        if blk == 0:
                            P.op('act', lambda e, pb=pb, ft=ft: e.activation(out=qT[:, ft, 0:ncols], in_=pb[:, 0:ncols], func=AF.Copy),
                                 reads=[k, 'R1'], writes=['qT'])
                        elif blk == 1:
                            if prompt:
                                for s in range(2):
                                    sl0 = s * 6 + (2 * ti) % 6
                                    P.op('act', lambda e, pb=pb, ft=ft, s=s, sl0=sl0: e.activation(
                                        out=kring[:, ft, sl0:sl0 + 2, :], in_=pb[:, s * 256:(s + 1) * 256].rearrange("p (a n) -> p a n", n=128), func=AF.Copy),
                                        reads=[k], writes=['kring'])
                            else:
                                P.op('dve', lambda e, pb=pb, ft=ft: e.tensor_copy(out=knew[:, ft, :], in_=pb[:, 0:256]),
                                     reads=[k, 'R1'], writes=['knew'])
                        elif blk == 3:
                            P.op('act', lambda e, pb=pb, ft=ft: e.activation(out=uT[:, ft, 0:ncols], in_=pb[:, 0:ncols], func=AF.Copy),
                                 reads=[k], writes=['uT', 'tmpa', 'tmpa0', 'tmpa1'])
                        else:
                            dstT = sga if blk < 6 else sgb
                            f8 = (blk % 2) * 4 + ft
                            P.op('act', lambda e, pb=pb, dstT=dstT, f8=f8: e.activation(out=dstT[:, f8, 0:ncols], in_=pb[:, 0:ncols], func=AF.Sigmoid),
                                 reads=[k, 'R1'], writes=['sg'])
                        yield


                if not early:
                    s4_u(kind, act8, ['act8'])

                def attn_block(s_col0, nq, kblocks, qkey_extra):
                    pod, kod = bank(2), 'ps2'
                    nd = len(kblocks)
                    dmax = max(d for d, _, _ in kblocks) + 1
                    for h in range(8):
                        hp, par = h // 2, h % 2
                        hq = hp % 2
                        pl = slice(par * 64, par * 64 + 64)
                        si = h % 2
                        pss = PS[0]
                        skeys = ['ps0', 'ps1']
                        def f_s(e, pss=pss, hp=hp, pl=pl, h=h):
                            ins = None
                            if nq == 128:
                                w0 = min(dmax, 4) * 128
                                e.matmul(pss[:, 0:w0], lhsT=ident_bf[:, :], rhs=Eh[:, h, 0:w0], start=True, stop=False)
                                if dmax == 5:
                                    e.matmul(pss[:, 512:640], lhsT=ident_bf[:, :], rhs=Eh[:, h, 512:640], start=True, stop=False)
                                dA = max(d for d, _, _ in kblocks if d <= 3)
                                for i_, (d, slot, nk) in enumerate(kblocks):
                                    last = (d == dA or d == 4)
                                    ins = e.matmul(pss[0:nk, d * nq:(d + 1) * nq], lhsT=kring[pl, hp, slot, 0:nk], rhs=qT[pl, hp, s_col0:s_col0 + nq],
                                                   start=False, stop=last)
                                return ins
                            for (d, slot, nk) in kblocks:
                                e.matmul(pss[0:nk, d * nq:(d + 1) * nq], lhsT=kring[pl, hp, slot, 0:nk], rhs=qT[pl, hp, s_col0:s_col0 + nq],
                                         start=True, stop=False)
                                ins = e.matmul(pss[0:nk, d * nq:(d + 1) * nq], lhsT=ident_bf[:, 0:nk], rhs=Eh[:, h, d * 128:d * 128 + nq],
                                               start=False, stop=True)
                            return ins
                        P.op('pe', f_s, reads=['kring', 'qT', 'Eh', 'identb'] + qkey_extra, writes=skeys)
                        ptv = pt[:, si, 0:dmax * nq]
                        P.op('act', lambda e, pss=pss, ptv=ptv: e.activation(out=ptv, in_=pss[:, 0:dmax * nq], func=AF.Exp, scale=0.125),
                             reads=skeys, writes=['pt%d' % si])
                        yield
                        def f_pv(e, hq=hq, pl=pl, si=si, h=h):
                            ins = None
                            for i, (d, slot, nk) in enumerate(kblocks):
                                ins = e.matmul(pod[pl, hq * 128:hq * 128 + nq], lhsT=vring[0:nk, slot, h * 64:(h + 1) * 64],
                                               rhs=pt[0:nk, si, d * nq:(d + 1) * nq], start=(i == 0), stop=(i == nd - 1))
                            for i, (d, slot, nk) in enumerate(kblocks):
                                ins = e.matmul(pod[pl, 256 + hq * 128:256 + hq * 128 + nq], lhsT=ones_bf[0:nk, 0:64],
                                               rhs=pt[0:nk, si, d * nq:(d + 1) * nq], start=(i == 0), stop=(i == nd - 1))
                            return ins
                        P.op('pe', f_pv, reads=['pt%d' % si, 'vring', 'ones'], writes=[kod])
                        yield
                        if h % 4 == 3:
                            hp0 = (h // 4) * 2
                            pov = pod[:, 0:256].rearrange("p (a n) -> p a n", n=128)[:, :, 0:nq]
                            pdv = pod[:, 256:512].rearrange("p (a n) -> p a n", n=128)[:, :, 0:nq]
                            rdv = rden[:, 0:256].rearrange("p (a n) -> p a n", n=128)[:, :, 0:nq]
                            P.op('dve', lambda e, rdv=rdv, pdv=pdv: e.reciprocal(out=rdv, in_=pdv), reads=[kod, 'tmpb0', 'tmpb1'], writes=['rden', 'tmpb0', 'tmpb1'])
                            P.op('dve', lambda e, hp0=hp0, pov=pov, rdv=rdv: e.tensor_tensor(out=oT[:, hp0:hp0 + 2, s_col0:s_col0 + nq], in0=pov, in1=rdv, op=ALU.mult),
                                 reads=[kod, 'rden', 'R1'], writes=['oT'])
                            yield

                def gen_att():
                    for blk in (0, 1, 2, 4, 5, 6, 7):
                        yield from s4_block(blk)
                    if prompt:
                        for s in range(2):
                            for qh in range(2):
                                qb = 2 * ti + qh
                                kbl = [(d, s * 6 + (qb - d) % 6, 128) for d in range(5) if qb - d >= 0]
                                yield from attn_block(s * 256 + qh * 128, 128, kbl, [])
                    else:
                        for pr in range(2):
                            for sl in range(2):
                                s = 2 * pr + sl
                                for c in range(4):
                                    slot = sl * 5 + c
                                    P.dma('sp', lambda e, s=s, c=c: e.dma_start(out=stg[:, :], in_=ck[s, c * 128:(c + 1) * 128, :]), 'stgin',
                                          reads=['stg'], writes=['stg'])
                                    b, pb, k = nbank()
                                    def f_t(e, pb=pb):
                                        ins = None
                                        for hp in range(4):
                                            ins = e.transpose(pb[:, hp * 128:(hp + 1) * 128], stg[:, hp * 128:(hp + 1) * 128], ident[:])
                                        return ins
                                    P.op('pe', f_t, reads=['stg', 'ident'], writes=[k])
                                    P.op('act', lambda e, pb=pb, slot=slot: e.activation(out=kring[:, :, slot, :], in_=pb.rearrange("p (a n) -> p a n", n=128), func=AF.Copy),
                                         reads=[k], writes=['kring'])
                                    P.dma('sp', lambda e, s=s, c=c: e.dma_start(out=stg[:, :], in_=cv[s, c * 128:(c + 1) * 128, :]), 'stgin',
                                          reads=['stg'], writes=['stg'])
                                    P.op('act', lambda e, slot=slot: e.activation(out=vring[:, slot, :], in_=stg[:, :], func=AF.Copy),
                                         reads=['stg'], writes=['vring'])
                                slot = sl * 5 + 4
                                P.op('dve', lambda e, s=s, slot=slot: e.tensor_copy(out=kring[:, :, slot, 0:64], in_=knew[:, :, s * 64:(s + 1) * 64]),
                                     reads=['knew'], writes=['kring'])
                                if s % 2 == 0:
                                    P.op('dve', lambda e, s=s, slot=slot: e.tensor_copy(out=vring[0:64, slot, :].rearrange("p (a n) -> p a n", n=256), in_=sga[0:64, 2 * (s // 2):2 * (s // 2) + 2, 256:512]),
                                         reads=['vnew'], writes=['vring'])
                                else:
                                    P.dma('sp', lambda e, s=s, slot=slot: e.dma_start(out=vring[0:64, slot, :].rearrange("p (a n) -> p a n", n=256), in_=sga[64:128, 2 * (s // 2):2 * (s // 2) + 2, 256:512]), 'vshift',
                                          reads=['vnew'], writes=['vring'])
                            for sl in range(2):
                                s = 2 * pr + sl
                                kbl = [(0, sl * 5 + 4, 64)] + [(4 - c, sl * 5 + c, 128) for c in range(4)]
                                yield from attn_block(s * 64, 64, kbl, [])

                    wv, wk = next_w()
                    for f in range(8):
                        b, pb, k = nbank()
                        def f_mm(e, wv=wv, pb=pb, f=f):
                            ins = None
                            for kc in range(4):
                                ins = e.matmul(pb[:, 0:ncols], lhsT=wv[:, kc, f * 128:(f + 1) * 128], rhs=oT[:, kc, 0:ncols], start=(kc == 0), stop=(kc == 3))
                            return ins
                        P.op('pe', f_mm, reads=['oT', wk], writes=[k])
                        P.op('dve', lambda e, pb=pb, f=f: e.tensor_tensor(out=act8[:, f, 0:ncols], in0=pb[:, 0:ncols], in1=sga[:, f, 0:ncols], op=ALU.mult),
                             reads=[k, 'sg', 'R1'], writes=['act8'])
                        yield


                bank_allowed[0] = [6, 7]
                interleave(([] if DBG.get('noatt') else [limited(gen_att(), DBG.get('attstop', 10**9))]) + ([] if DBG.get('nossm') else [ssm_live.pop((kind, ti), None) or gen_ssm(kind)]), [DBG.get('attw', ATT_W), DBG.get('ssmw', SSM_W)][(1 if DBG.get('noatt') else 0):])
                bank_allowed[0] = list(range(8))
                if last_prompt:
                    write_states(2, rep, imp)
                if pre:
                    load_x()

                for blk in range(2):
                    wv, wk = next_w()
                    for fl in range(4):
                        f = blk * 4 + fl
                        b1, pba, ka = nbank()
                        b2, pbb, kb_ = nbank()
                        def f_mm(e, wv=wv, pba=pba, pbb=pbb, fl=fl):
                            ins = None
                            for kc in range(4):
                                ins = e.matmul(pbb[:, 0:ncols], lhsT=wv[:, kc, 512 + fl * 128:512 + (fl + 1) * 128], rhs=gT[:, kc, 0:ncols], start=(kc == 0), stop=(kc == 3))
                            for kc in range(4):
                                ins = e.matmul(pba[:, 0:ncols], lhsT=wv[:, kc, fl * 128:(fl + 1) * 128], rhs=gT[:, kc, 0:ncols], start=(kc == 0), stop=(kc == 3))
                            return ins
                        P.op('pe', f_mm, reads=['gT', wk], writes=[ka, kb_])
                        tb = tmpb[:, f % 2, 0:ncols]
                        ta = tmpa[:, f % 2, 0:ncols]
                        P.op('act', lambda e, pbb=pbb, tb=tb: e.activation(out=tb, in_=pbb[:, 0:ncols], func=AF.Sigmoid), reads=[kb_, 'tmpb%d' % (f % 2)], writes=['tmpb%d' % (f % 2)])
                        P.op('dve', lambda e, pba=pba, tb=tb, ta=ta: e.tensor_tensor(out=ta, in0=pba[:, 0:ncols], in1=tb, op=ALU.mult),
                             reads=[ka, 'tmpb%d' % (f % 2), 'tmpa', 'tmpa1', 'tmpa%d' % (f % 2), 'R1'], writes=['tmpa%d' % (f % 2), 'uT'])
                        P.op('pool', lambda e, ta=ta, f=f: e.tensor_tensor(out=ta, in0=ta, in1=sgb[:, f, 0:ncols], op=ALU.mult),
                             reads=['tmpa%d' % (f % 2), 'sg', 'R1'], writes=['tmpa%d' % (f % 2)])
                        P.op('pool', lambda e, ta=ta, f=f: e.tensor_tensor(out=act8[:, f, 0:ncols], in0=act8[:, f, 0:ncols], in1=ta, op=ALU.add),
                             reads=['tmpa%d' % (f % 2), 'act8'], writes=['act8'])

                tck(25)
                def resid(tt, cb, pb, k, gate):
                    slot = (tt // 2) if prompt else tt
                    P.op('dve', lambda e, pb=pb, slot=slot, cb=cb: e.tensor_tensor(out=pb, in0=pb, in1=gbc[:, 0, slot, cb * 512:(cb + 1) * 512], op=ALU.mult),
                         reads=[k, 'gbc'], writes=[k])
                    P.op('dve', lambda e, tt=tt, cb=cb, pb=pb: e.scalar_tensor_tensor(out=xt[:, tt, cb * 512:(cb + 1) * 512], in0=xt[:, tt, cb * 512:(cb + 1) * 512],
                                                                                    scalar=ALPHA, in1=pb, op0=ALU.mult, op1=ALU.add),
                         reads=[k, 'xt%d' % tt], writes=['xt%d' % tt])

                load_gate(2)
                for i_, v_ in enumerate([ln1_g, ln1_b]):
                    P.dma('sp', lambda e, i_=i_, v_=v_: e.dma_start(out=lnbc[:, i_, :], in_=dap(v_, 0, [[0, 128], [1, D]])), 'lnbc', writes=['lnbc'])
                wouts = [next_w(), next_w(prefetch=False)]

                def wout_tt(tt):
                    for cb in range(2):
                        wv, wk = wouts[cb]
                        b, pb, k = nbank()
                        def f_mm(e, wv=wv, pb=pb, tt=tt):
                            ins = None
                            for kc in range(8):
                                ins = e.matmul(pb, lhsT=act8[:, kc, tt * 128:(tt + 1) * 128], rhs=wv[:, kc, :], start=(kc == 0), stop=(kc == 7))
                            return ins
                        P.op('pe', f_mm, reads=['act8', wk], writes=[k])
                        resid(tt, cb, pb, k, 0)

                def ln1_a(tt):
                    ln_stats(tt, 'xt%d' % tt)
                    ln_affine(tt, 0, 1)
                    ln_stats(tt, 'xt%d' % tt)

                def gen_wout():
                    for tt in range(ntt):
                        wout_tt(tt)
                        yield
                        if tt >= 1:
                            ln1_a(tt - 1)
                            yield
                    prefetch_w()
                    ln1_a(ntt - 1)
                    yield

                do_early = EARLY and nxt is not None
                if do_early:
                    interleave([gen_wout(), gen_ln0(nxt[0], nxt[1], dest=hTu, dkeys=HTU_KEYS)], [1, 2])
                    s4_u(nxt[0], hTu, HTU_KEYS)
                    ssm_live[nxt] = gen_ssm(nxt[0])
                    for tt in range(ntt):
                        ln_to_featmajor(tt, ntt, segs[tt], 3, 4, 'xt%d' % tt)
                else:
                    for tt in range(ntt):
                        wout_tt(tt)
                        if tt >= 1:
                            ln1_a(tt - 1)
                    prefetch_w()
                    ln_to_featmajor(0, ntt, segs[0], 3, 4, 'xt0')
                    ln1_a(ntt - 1)
                    for tt in range(1, ntt):
                        ln_to_featmajor(tt, ntt, segs[tt], 3, 4, 'xt%d' % tt)

                tck(26)
                fence('R1')
                def gen_ffn_in():
                    for blk in range(11):
                        wv, wk = next_w()
                        for fl in range(2):
                            f = blk * 2 + fl
                            b1, pbg, kg = nbank()
                            b2, pbu, ku = nbank()
                            def f_mm(e, wv=wv, pbg=pbg, pbu=pbu, fl=fl):
                                ins = None
                                for kc in range(8):
                                    ins = e.matmul(pbg[:, 0:ncols], lhsT=wv[:, kc, fl * 128:(fl + 1) * 128], rhs=act8[:, kc, 0:ncols], start=(kc == 0), stop=(kc == 7))
                                for kc in range(8):
                                    ins = e.matmul(pbu[:, 0:ncols], lhsT=wv[:, kc, 256 + fl * 128:256 + (fl + 1) * 128], rhs=act8[:, kc, 0:ncols], start=(kc == 0), stop=(kc == 7))
                                return ins
                            P.op('pe', f_mm, reads=['act8', wk], writes=[kg, ku])
                            tb = tmpb[:, f % 2, 0:ncols]
                            P.op('act', lambda e, pbg=pbg, tb=tb: e.activation(out=tb, in_=pbg[:, 0:ncols], func=AF.Silu), reads=[kg, 'tmpb%d' % (f % 2)], writes=['tmpb%d' % (f % 2)])
                            P.op('dve', lambda e, pbu=pbu, tb=tb, f=f: e.tensor_tensor(out=actT[:, f, 0:ncols], in0=pbu[:, 0:ncols], in1=tb, op=ALU.mult),
                                 reads=[ku, 'tmpb%d' % (f % 2), 'R1'], writes=['actT'])
                            yield

                gens_ = [gen_ffn_in()]
                ws_ = [1]
                if do_early:
                    bank_allowed[0] = [0, 1, 2, 6, 7]
                    gens_.append(limited(ssm_live[nxt], DBG.get('ssm_f', SSM_F)))
                    ws_.append(1)
                interleave(gens_, ws_)
                bank_allowed[0] = list(range(8))
                tck(27)
                load_gate(5)

                def gen_ffn_out():
                    for cb in range(2):
                        banks = [(bb_, bank(bb_), 'ps%d' % bb_) for bb_ in (0, 1, 6, 7)[:ntt]]
                        for kh, (k0_, kn_) in enumerate(KPARTS):
                            wv, wk = next_w()
                            for tt in range(ntt):
                                b, pb, k = banks[tt]
                                def f_mm(e, wv=wv, pb=pb, tt=tt, kh=kh, k0_=k0_, kn_=kn_):
                                    ins = None
                                    for kc in range(kn_):
                                        ins = e.matmul(pb, lhsT=actT[:, k0_ + kc, tt * 128:(tt + 1) * 128], rhs=wv[:, kc, :],
                                                       start=(kh == 0 and kc == 0), stop=(kh == 3 and kc == kn_ - 1))
                                    return ins
                                P.op('pe', f_mm, reads=['actT', wk, 'R1'], writes=[k])
                                yield
                        for tt in range(ntt):
                            b, pb, k = banks[tt]
                            resid(tt, cb, pb, k, 1)
                            yield

                bank_allowed[0] = [2] if do_early else [2, 3, 4, 5]
                gens_ = [gen_ffn_out()]
                ws_ = [DBG.get('ffw', 4)]
                if nxt is not None:
                    gens_.append(gen_ln0(*nxt))
                    ws_.append(1)
                if do_early:
                    gens_.append(limited(ssm_live[nxt], DBG.get('ssm_g', SSM_G)))
                    ws_.append(1)
                interleave(gens_, ws_)
                bank_allowed[0] = list(range(8))
                for i_, v_ in enumerate([ln2_g, ln2_b]):
                    P.dma('sp', lambda e, i_=i_, v_=v_: e.dma_start(out=lnbc[:, i_, :], in_=dap(v_, 0, [[0, 128], [1, D]])), 'lnbc', writes=['lnbc'])
                for tt in range(ntt):
                    ln_stats(tt, 'xt%d' % tt)
                    ln_affine(tt, 2, 3)
                    if prompt:
                        s, hf = tt // 2, tt % 2
                        dst = yp[s, ti * TL + hf * 128: ti * TL + hf * 128 + 128, :]
                    else:
                        dst = ys[2 * tt:2 * tt + 2, :, :].rearrange("s t d -> (s t) d")
                    P.dma('pool', lambda e, tt=tt, dst=dst: e.dma_start(out=dst, in_=xt[:, tt, :]), 'yout%d' % tt, reads=['xt%d' % tt])

            def write_states(ns, dre, dim_):
                for (st_, dd, nm) in ((stre, dre, 'stre'), (stim, dim_, 'stim')):
                    tcp = tmpa[:, 0, 0:16 * ns]
                    P.op('dve', lambda e, st_=st_, tcp=tcp: e.tensor_copy(out=tcp.rearrange("p (s j) -> p s j", j=16),
                                                                           in_=st_[:, :, 0:ns].rearrange("p j s -> p s j")),
                         reads=[nm, 'tmpa', 'tmpa0', 'tmpa1'], writes=['tmpa', 'tmpa0', 'uT'])
                    b, pb, k = nbank()
                    P.op('pe', lambda e, pb=pb, tcp=tcp: e.transpose(pb[:, 0:128], tmpa[:, 0, 0:128], ident[:]), reads=['tmpa', 'ident'], writes=[k])
                    P.op('dve', lambda e, pb=pb: e.tensor_copy(out=stg2[0:16 * ns, 0, 0:128], in_=pb[0:16 * ns, 0:128]), reads=[k, 'stg'], writes=['stg'])
                    dst = dd.rearrange("s (j t) p -> (s j) (t p)", t=2)
                    P.dma('sp', lambda e, dst=dst: e.dma_start(out=dst, in_=stg2[0:16 * ns, 0, 0:128]), 'stout', reads=['stg'])

            seq_tiles = [('p', ti) for ti in range(DBG['ntiles'])] + ([('s', 0)] if DBG['sample'] else [])
            PREF = DBG.get('pref', True)
            EARLY = DBG.get('early', False) and PREF
            build_wseq(len(seq_tiles), EARLY)
            for idx_, (kind_, ti_) in enumerate(seq_tiles):
                nxt_ = seq_tiles[idx_ + 1] if (PREF and idx_ + 1 < len(seq_tiles)) else None
                lastp = (kind_ == 'p' and ti_ == DBG['ntiles'] - 1)
                run_tile(kind_, ti_, pre=(PREF and idx_ > 0), nxt=nxt_, early=(EARLY and idx_ > 0), last_prompt=lastp)
                if kind_ == 's':
                    write_states(4, res, ims)

        except StopBuild:
            pass
        P.barrier()
        P.emit()
    return nc


_NC_CACHE = {}
DBG = {'ntiles': NTILES, 'sample': True, 'cores': NCORES, 'stop': 99}


def kernel(**inp):
    f = lambda a: np.ascontiguousarray(np.asarray(a, dtype=np.float32))
    if 'nc' not in _NC_CACHE:
        _NC_CACHE['nc'] = build_nc()
    nc = _NC_CACHE['nc']
    shared = {
        'w_ada': f(inp['w_ada'][0]), 'b_ada': f(inp['b_ada'][0]), 'w_in': f(inp['w_in'][0]), 'rel_bias': f(inp['rel_bias'][0]),
        'a_re': f(inp['ssm_a_re'][0]), 'a_im': f(inp['ssm_a_im'][0]), 'log_dt': f(inp['ssm_log_dt'][0]),
        'b_re': f(inp['ssm_b_re'][0]), 'b_im': f(inp['ssm_b_im'][0]), 'c_re': f(inp['ssm_c_re'][0]), 'c_im': f(inp['ssm_c_im'][0]),
        'ssm_d': f(inp['ssm_d'][0]), 'w_attn': f(inp['w_attn_proj'][0]), 'w_glu': f(inp['w_glu'][0]), 'w_out': f(inp['w_out'][0]),
        'ln1_g': f(inp['ln1_g'][0]), 'ln1_b': f(inp['ln1_b'][0]), 'w_ffn_in': f(inp['w_ffn_in'][0]), 'w_ffn_out': f(inp['w_ffn_out'][0]),
        'ln2_g': f(inp['ln2_g'][0]), 'ln2_b': f(inp['ln2_b'][0]),
    }
    in_maps = []
    for c in range(DBG['cores']):
        m = dict(shared)
        m['xp'] = f(inp['x_prompt'][2 * c:2 * c + 2])
        m['xs'] = f(inp['x_sample'][4 * c:4 * c + 4])
        m['cc'] = f(np.concatenate([inp['c_prompt'][2 * c:2 * c + 2], inp['c_sample'][4 * c:4 * c + 4]], axis=0))
        m['ck'] = f(inp['cache_attn_k'][0, 4 * c:4 * c + 4].reshape(4, 512, 512))
        m['cv'] = f(inp['cache_attn_v'][0, 4 * c:4 * c + 4].reshape(4, 512, 512))
        m['sre'] = f(inp['state_ssm_re'][0, 4 * c:4 * c + 4])
        m['sim'] = f(inp['state_ssm_im'][0, 4 * c:4 * c + 4])
        in_maps.append(m)
    if DBG.get('trace'):
        res = run_bass_kernel_spmd(nc, in_maps, core_ids=list(range(DBG['cores'])), trace=True)
        print('EXEC_NS', res.exec_time_ns)
    else:
        res = run_bass_kernel_spmd(nc, in_maps, core_ids=list(range(DBG['cores'])))
    R = res.results
    cat = lambda k: np.concatenate([np.asarray(r[k], dtype=np.float32) for r in R], axis=0)
    y_prompt = cat('yp')
    y_sample = cat('ys')
    k_prompt = cat('kp').reshape(1, -1, 512, 8, 64)
    v_prompt = cat('vp').reshape(1, -1, 512, 8, 64)
    re_p = cat('rep')[None]
    im_p = cat('imp')[None]
    k_sample = cat('ks').reshape(1, -1, 64, 8, 64)
    v_sample = cat('vs').reshape(1, -1, 64, 8, 64)
    re_s = cat('res')[None]
    im_s = cat('ims')[None]
    return (y_prompt, y_sample, k_prompt, v_prompt, re_p, im_p, k_sample, v_sample, re_s, im_s)
```

```python
import contextlib
import math
import numpy as np
import concourse.bass as bass
import concourse.mybir as mybir
from concourse.bass_utils import run_bass_kernel_spmd

F32 = mybir.dt.float32
BF16 = mybir.dt.bfloat16
I32 = mybir.dt.int32
AF = mybir.ActivationFunctionType
ALU = mybir.AluOpType

NCORES = 8
D = 1024
SEQ = 2048
TL = 256
NTILES = SEQ // TL
DFF = 2816
ALPHA = 2.0 ** 0.25
LN_EPS = 1e-5
TWO_PI = 2.0 * math.pi


class Prog:
    ENG = ['pe', 'act', 'dve', 'pool', 'sp']

    def __init__(self, nc):
        self.nc = nc
        self.streams = {e: [] for e in self.ENG}
        self.count = {e: 0 for e in self.ENG}
        self.waited = {e: {} for e in self.ENG}
        self.last_write = {}
        self.readers = {}
        self.dmasem_count = {}

    def _deps(self, eng, reads, writes):
        deps = []
        for k in reads:
            lw = self.last_write.get(k)
            if lw is not None:
                deps.append(lw)
        for k in writes:
            lw = self.last_write.get(k)
            if lw is not None:
                deps.append(lw)
            rd = self.readers.get(k)
            if rd:
                for src, val in rd.items():
                    if src != eng or eng != 'pe':
                        deps.append((src, val))
        need = {}
        for src, val in deps:
            if src == eng and eng == 'pe':
                continue
            if self.waited[eng].get(src, 0) >= val:
                continue
            if need.get(src, 0) < val:
                need[src] = val
        for src, val in need.items():
            self.waited[eng][src] = val
            self.streams[eng].append(('wait', src, val))

    def _commit(self, src, val, reads, writes):
        for k in writes:
            self.last_write[k] = (src, val)
            self.readers[k] = {}
        for k in reads:
            d = self.readers.setdefault(k, {})
            if d.get(src, 0) < val:
                d[src] = val

    def op(self, eng, fn, reads=(), writes=()):
        writes = list(writes) + [k for k in reads if k.startswith('ps') and k[2:].isdigit() and k not in writes]
        self._deps(eng, reads, writes)
        self.count[eng] += 1
        self.streams[eng].append(('op', fn))
        self._commit(eng, self.count[eng], reads, writes)

    def dma(self, eng, fn, semkey, reads=(), writes=()):
        self._deps(eng, reads, writes)
        prev = self.dmasem_count.get(semkey, 0)
        if prev and self.waited[eng].get('dma:' + semkey, 0) < prev:
            self.waited[eng]['dma:' + semkey] = prev
            self.streams[eng].append(('wait', 'dma:' + semkey, prev))
        val = self.dmasem_count.get(semkey, 0) + 16
        self.dmasem_count[semkey] = val
        self.streams[eng].append(('dma', fn, semkey))
        self._commit('dma:' + semkey, val, reads, writes)

    def barrier(self, skip_prefix=None):
        for e in self.ENG:
            self.wait_all(e, skip_prefix)

    def wait_all(self, eng, skip_prefix=None):
        for e in self.ENG:
            if e != eng and self.count[e] > self.waited[eng].get(e, 0):
                self.streams[eng].append(('wait', e, self.count[e]))
                self.waited[eng][e] = self.count[e]
        for k, v in self.dmasem_count.items():
            s = 'dma:' + k
            if skip_prefix and k.startswith(skip_prefix):
                continue
            if v > self.waited[eng].get(s, 0):
                self.streams[eng].append(('wait', s, v))
                self.waited[eng][s] = v

    def emit(self):
        nc = self.nc
        with contextlib.ExitStack() as es:
            sems = {}
            for e in self.ENG:
                sems[e] = es.enter_context(nc.semaphore('s_' + e))
            for k in self.dmasem_count:
                sems['dma:' + k] = es.enter_context(nc.semaphore('d_' + k))
            block = es.enter_context(nc.Block())
            streams = self.streams

            def run(engname):
                def f(eng):
                    for it in streams[engname]:
                        if it[0] == 'wait':
                            eng.wait_ge(sems[it[1]], it[2])
                        elif it[0] == 'op':
                            it[1](eng).then_inc(sems[engname], 1)
                        else:
                            it[1](eng).then_inc(sems['dma:' + it[2]], 16)
                return f
            block.tensor(run('pe'))
            block.scalar(run('act'))
            block.vector(run('dve'))
            block.gpsimd(run('pool'))
            block.sync(run('sp'))


ATT_W = 1
SSM_F = 40
SSM_G = 12
SSM_W = 1
YLAG = 1


def limited(g, n):
    for i, _ in enumerate(g):
        if i >= n:
            return
        yield


def interleave(gens, weights):
    alive = list(gens)
    ws = list(weights)
    while alive:
        for i in range(len(alive) - 1, -1, -1):
            pass
        nxt = []
        nws = []
        for g, w in zip(alive, ws):
            ok = True
            for _ in range(w):
                try:
                    next(g)
                except StopIteration:
                    ok = False
                    break
            if ok:
                nxt.append(g)
                nws.append(w)
        alive, ws = nxt, nws


def dap(t, offset, ap):
    return bass.AP(tensor=t.tensor, offset=offset, ap=ap)


def build_nc():
    nc = bass.Bass("TRN2", target_bir_lowering=False)

    def din(name, shape):
        return nc.dram_tensor(name, shape, F32, kind="ExternalInput").ap()

    def dout(name, shape):
        return nc.dram_tensor(name, shape, F32, kind="ExternalOutput").ap()

    def dscr(name, shape, dt):
        return nc.dram_tensor(name, shape, dt, kind="Internal").ap()

    xp = din("xp", [2, SEQ, D]); xs = din("xs", [4, 64, D]); cc = din("cc", [6, D])
    ck = din("ck", [4, 512, 512]); cv = din("cv", [4, 512, 512])
    sre = din("sre", [4, 32, 64]); sim = din("sim", [4, 32, 64])
    w_ada = din("w_ada", [D, 6 * D]); b_ada = din("b_ada", [6 * D])
    w_in = din("w_in", [D, 4096]); rel_bias = din("rel_bias", [8, 513])
    a_re = din("a_re", [32, 64]); a_im = din("a_im", [32, 64]); log_dt = din("log_dt", [32])
    b_re = din("b_re", [32, 64, 16]); b_im = din("b_im", [32, 64, 16])
    c_re = din("c_re", [32, 16, 64]); c_im = din("c_im", [32, 16, 64]); ssm_d = din("ssm_d", [512])
    w_attn = din("w_attn", [512, D]); w_glu = din("w_glu", [512, 2 * D]); w_out = din("w_out", [D, D])
    ln1_g = din("ln1_g", [D]); ln1_b = din("ln1_b", [D])
    w_ffn_in = din("w_ffn_in", [D, 2 * DFF]); w_ffn_out = din("w_ffn_out", [DFF, D])
    ln2_g = din("ln2_g", [D]); ln2_b = din("ln2_b", [D])

    yp = dout("yp", [2, SEQ, D]); ys = dout("ys", [4, 64, D])
    kp = dout("kp", [2, 512, 512]); vp = dout("vp", [2, 512, 512])
    rep = dout("rep", [2, 32, 64]); imp = dout("imp", [2, 32, 64])
    ks = dout("ks", [4, 64, 512]); vs = dout("vs", [4, 64, 512])
    res = dout("res", [4, 32, 64]); ims = dout("ims", [4, 32, 64])

    s_ada = dscr("s_ada", [12, 128, 8, 512], BF16)
    s_in = dscr("s_in", [8, 128, 8, 512], BF16)
    s_attn = dscr("s_attn", [1, 128, 4, 1024], BF16)
    s_glu = dscr("s_glu", [2, 128, 4, 1024], BF16)
    s_out = dscr("s_out", [2, 128, 8, 512], BF16)
    s_fin = dscr("s_fin", [11, 128, 8, 512], BF16)
    s_fout = dscr("s_fout", [8, 128, 6, 512], BF16)
    ext_d = dscr("ext_d", [8, 128, 768], BF16)
    mod_d = dscr("mod_d", [6, 6 * D], F32)

    es = contextlib.ExitStack()
    with es:
        def sb(name, shape, dt):
            return es.enter_context(nc.sbuf_tensor(name, shape, dt))

        P = Prog(nc)

        class StopBuild(Exception):
            pass

        def ckpt(n):
            P.barrier()
            if DBG['stop'] == n:
                raise StopBuild()

        def tck(n):
            if DBG['stop'] == n:
                P.barrier()
                raise StopBuild()
        ident = sb("ident", [128, 128], F32)
        xt = sb("xt", [128, 4, D], F32)
        xn = sb("xn", [128, 1, D], F32)
        act8 = sb("act8", [128, 8, 512], BF16)
        R1 = sb("R1", [128, 14336], BF16)
        qT = R1[:, 0:2048].rearrange("p (k n) -> p k n", n=512)
        uT = R1[:, 12288:14336].rearrange("p (k n) -> p k n", n=512)
        sga = R1[:, 4096:8192].rearrange("p (k n) -> p k n", n=512)
        sgb = R1[:, 8192:12288].rearrange("p (k n) -> p k n", n=512)
        oT = R1[:, 2048:4096].rearrange("p (k n) -> p k n", n=512)
        actT = R1[:, 0:11264].rearrange("p (k n) -> p k n", n=512)
        kring = sb("kring", [128, 4, 12, 128], BF16)
        vring = sb("vring", [128, 12, 512], BF16)
        wsl = [sb("wsl%d" % i, [128, 4096], BF16) for i in range(2)]
        Eh = sb("Eh", [128, 8, 640], BF16)
        pt = sb("pt", [128, 2, 640], BF16)
        lnbc = sb("lnbc", [128, 2, D], F32)
        gbc = sb("gbc", [128, 1, 2, D], F32)
        modT = sb("modT", [128, 48, 6], F32)
        small = sb("small", [128, 4, 8], F32)
        bnst = sb("bnst", [128, 4, 12], F32)
        small2 = sb("small2", [128, 4, 8], F32)
        bnst2 = sb("bnst2", [128, 4, 12], F32)
        epsT = sb("epsT", [128, 1], F32)
        npiT = sb("npiT", [128, 1], F32)
        oneT = sb("oneT", [128, 1], F32)
        Ctab = sb("Ctab", [128, 16, 256], F32)
        Stab = sb("Stab", [128, 16, 256], F32)
        rco = sb("rco", [128, 16], F32)
        cend = sb("cend", [128, 2, 3, 16], F32)
        BbT = sb("BbT", [128, 16, 2, 128], BF16)
        CTw = sb("CTw", [128, 16, 2, 128], BF16)
        Dcol = sb("Dcol", [128, 4], F32)
        stre = sb("stre", [128, 16, 4], F32)
        stim = sb("stim", [128, 16, 4], F32)
        ssmf = sb("ssmf", [128, 8, 512], F32)
        ssmb = sb("ssmb", [128, 2, 2, 512], BF16)
        gT = sb("gT", [128, 4, 512], BF16)
        knew = qT[:, 0:4, 256:512]
        tmpa = R1[:, 12288:14336].bitcast(F32).rearrange("p (a n) -> p a n", n=512)
        tmpb = sb("tmpb", [128, 2, 512], BF16)
        rden = tmpb[:, :, :].rearrange("p a n -> p (a n)").bitcast(F32)
        stg2 = sb("stg2", [128, 1, 512], F32)
        ident_bf = sb("ident_bf", [128, 128], BF16)
        stg = stg2[:, 0, :]
        ones_bf = sb("ones_bf", [128, 64], BF16)

        PS = [es.enter_context(nc.psum_tensor("PS%d" % i, [128, 1024], F32)) for i in range(4)]

        def bank(i):
            return PS[i // 2][:, (i % 2) * 512:(i % 2) * 512 + 512]

        bank_rr = [0]

        bank_allowed = [list(range(8))]

        def nbank():
            while True:
                b = bank_rr[0]
                bank_rr[0] = (b + 1) % 8
                if b in bank_allowed[0]:
                    return b, bank(b), 'ps%d' % b

        try:
            P.op('pool', lambda e: e.memset(ident[:], 0.0), writes=['ident'])
            P.op('pool', lambda e: e.affine_select(out=ident[:], in_=ident[:], pattern=[[-1, 128]],
                                                   compare_op=ALU.not_equal, fill=1.0, base=0, channel_multiplier=1),
                 reads=['ident'], writes=['ident'])
            P.op('dve', lambda e: e.tensor_copy(out=ident_bf[:], in_=ident[:]), reads=['ident'], writes=['identb'])
            P.op('dve', lambda e: e.memset(epsT[:], LN_EPS), writes=['epsT'])
            P.op('dve', lambda e: e.memset(ones_bf[:], 1.0), writes=['ones'])
            P.op('dve', lambda e: e.memset(stre[:].rearrange('p j s -> p (j s)'), 0.0), writes=['stre'])
            P.op('dve', lambda e: e.memset(stim[:].rearrange('p j s -> p (j s)'), 0.0), writes=['stim'])
            P.op('dve', lambda e: e.memset(npiT[:], -math.pi), writes=['npiT'])
            P.op('dve', lambda e: e.memset(oneT[:], 1.0), writes=['oneT'])
            cast_jobs = {}

            def cast_blk(w, N, kcn, kc0, col0, bc, dst, dkey, dcol0=0, dcols=None):
                src = dap(w, kc0 * 128 * N + col0, [[N, 128], [128 * N, kcn], [1, bc]])
                d = dst if dcols is None else dst[:, :, dcol0:dcol0 + dcols]
                cast_jobs.setdefault(dkey, []).append(
                    lambda d=d, src=src, dkey=dkey, dcol0=dcol0: P.dma('pool', lambda e: e.dma_start(out=d, in_=src),
                                                                        'cast_' + dkey + ('_%d' % dcol0), writes=[dkey]))

            def issue_cast(dkey):
                for f_ in cast_jobs.pop(dkey, []):
                    f_()

            ada_slots = [R1[:, 0:4096], R1[:, 4096:8192], R1[:, 8192:12288], act8[:].rearrange('p k n -> p (k n)'),
                         xt[:, 0:2, :].rearrange('p a n -> p (a n)').bitcast(BF16)[:, 0:4096], xt[:, 2:4, :].rearrange('p a n -> p (a n)').bitcast(BF16)[:, 0:4096],
                         Ctab[:].rearrange('p a n -> p (a n)').bitcast(BF16)[:, 0:4096], Ctab[:].rearrange('p a n -> p (a n)').bitcast(BF16)[:, 4096:8192],
                         Stab[:].rearrange('p a n -> p (a n)').bitcast(BF16)[:, 0:4096], Stab[:].rearrange('p a n -> p (a n)').bitcast(BF16)[:, 4096:8192],
                         kring[:].rearrange('p a b c -> p (a b c)')[:, 0:4096], vring[:].rearrange('p a n -> p (a n)')[:, 0:4096]]
            ada_views = []
            for b in range(12):
                v_ = ada_slots[b].rearrange('p (k n) -> p k n', n=512)
                src_ = dap(w_ada, b * 512, [[6 * D, 128], [128 * 6 * D, 8], [1, 512]])
                P.dma('pool', lambda e, v_=v_, src_=src_: e.dma_start(out=v_, in_=src_), 'adaL%d' % b, writes=['adaS%d' % b])
                ada_views.append((v_, 'adaS%d' % b))
            for b in (3, 0, 1, 2, 4, 5, 6, 7):
                cast_blk(w_in, 4096, 8, 0, b * 512, 512, s_in[b], 's_in%d' % b)
                issue_cast('s_in%d' % b)
            wcnt = [0]

            def load_w(scr, blk, kcn, bc, skey):
                i = wcnt[0] % 2
                wcnt[0] += 1
                view = wsl[i][:, 0:kcn * bc].rearrange("p (k n) -> p k n", n=bc)
                key = 'wsl%d' % i
                P.dma('sp', lambda e, view=view, src=scr[blk][:, 0:kcn, :]: e.dma_start(out=view, in_=src), key,
                      reads=[skey], writes=[key])
                return view, key

            W_IN = {b: (s_in, b, 8, 512, 's_in%d' % b) for b in range(8)}
            W_FRONT = ([W_IN[b] for b in (0, 1, 2, 4, 5, 6, 7)] + [(s_attn, 0, 4, 1024, 's_attn0')] +
                       [(s_glu, b, 4, 1024, 's_glu%d' % b) for b in range(2)] +
                       [(s_out, b, 8, 512, 's_out%d' % b) for b in range(2)])
            W_BACK = ([(s_fin, b, 8, 512, 's_fin%d' % b) for b in range(11)] +
                      [(s_fout, b, (6, 5, 6, 5)[b % 4], 512, 's_fout%d' % b) for b in range(8)])
            worder = []

            def build_wseq(ntl, early):
                worder.append(W_IN[3])
                for i_ in range(ntl):
                    worder.extend(W_FRONT)
                    if i_ + 1 < ntl:
                        if early:
                            worder.append(W_IN[3])
                    worder.extend(W_BACK)
                    if i_ + 1 < ntl and not early:
                        worder.append(W_IN[3])
            wq = {'pending': None, 'pos': 0}

            CAST_AHEAD = 6

            def next_w(prefetch=True):
                for m_ in range(wq['pos'], min(len(worder), wq['pos'] + CAST_AHEAD)):
                    issue_cast(worder[m_][4])
                if wq['pending'] is None:
                    wq['pending'] = load_w(*worder[wq['pos']])
                cur = wq['pending']
                wq['pending'] = None
                wq['pos'] += 1
                if prefetch:
                    prefetch_w()
                return cur

            def prefetch_w():
                if wq['pending'] is None and wq['pos'] < len(worder):
                    wq['pending'] = load_w(*worder[wq['pos']])

            csb = tmpa[0:6, :, :].rearrange("p a n -> p (a n)")
            P.dma('sp', lambda e: e.dma_start(out=csb, in_=cc), 'cc', writes=['tmpa'])
            P.op('act', lambda e: e.activation(out=csb, in_=csb, func=AF.Silu), reads=['tmpa'], writes=['tmpa'])
            b0, pb0, k0 = 6, bank(6), 'ps6'
            def f_ct(e):
                ins = None
                for kc in range(8):
                    ins = e.transpose(pb0[:, kc * 6:kc * 6 + 6], csb[:, kc * 128:(kc + 1) * 128], ident[0:6, 0:6])
                return ins
            P.op('pe', f_ct, reads=['tmpa', 'ident'], writes=[k0])
            scT = tmpb[:, 0, 0:48].rearrange("p (k r) -> p k r", r=6)
            P.op('dve', lambda e: e.tensor_copy(out=scT, in_=pb0[:, 0:48].rearrange("p (k r) -> p k r", r=6)),
                 reads=[k0], writes=['tmpb'])
            mst = stg[0:6, :]
            bst = xn[0:6, 0, 0:512]
            bT, pbT, kT_ = 7, bank(7), 'ps7'
            for blk in range(12):
                wv, wk = ada_views[blk]
                b1, pb1, k1 = blk % 4, bank(blk % 4), 'ps%d' % (blk % 4)
                def f_mm(e, wv=wv, pb1=pb1):
                    ins = None
                    for kc in range(8):
                        ins = e.matmul(pb1[0:6, :], lhsT=scT[:, kc, :], rhs=wv[:, kc, :], start=(kc == 0), stop=(kc == 7))
                    return ins
                P.op('pe', f_mm, reads=['tmpb', wk], writes=[k1])
                P.dma('sp', lambda e, blk=blk: e.dma_start(out=bst, in_=dap(b_ada, blk * 512, [[0, 6], [1, 512]])),
                      'bst', writes=['bst'])
                P.op('dve', lambda e, pb1=pb1: e.tensor_tensor(out=mst, in0=pb1[0:6, :], in1=bst, op=ALU.add),
                     reads=[k1, 'bst'], writes=['mst'])
                P.dma('sp', lambda e, blk=blk: e.dma_start(out=mod_d[:, blk * 512:(blk + 1) * 512], in_=mst),
                      'modd', reads=['mst'], writes=['mod_d'])
                def f_tp(e, blk=blk):
                    ins = None
                    for q in range(4):
                        ft = blk * 4 + q
                        ins = e.transpose(pbT[:, ft * 6:ft * 6 + 6], mst[:, q * 128:(q + 1) * 128], ident[0:6, 0:6])
                    return ins
                P.op('pe', f_tp, reads=['mst', 'ident'], writes=[kT_])
            P.op('dve', lambda e: e.tensor_copy(out=modT[:].rearrange("p k r -> p (k r)"), in_=pbT[:, 0:288]),
                 reads=[kT_], writes=['modT'])
            for sec in (1, 4):
                P.op('dve', lambda e, sec=sec: e.tensor_scalar(out=modT[:, sec * 8:sec * 8 + 8, :], in0=modT[:, sec * 8:sec * 8 + 8, :],
                                                                scalar1=1.0, scalar2=None, op0=ALU.add),
                     reads=['modT'], writes=['modT'])

            ckpt(1)
            rbs = tmpa[0:8, 0, :]
            ext = tmpa[0:8, 1, :]
            ext = ssmf[0:8, 0:2, :].rearrange("p a n -> p (a n)")[:, 0:768]
            extb = tmpb[0:8, :, :].rearrange("p a n -> p (a n)")[:, 0:768]
            P.dma('sp', lambda e: e.dma_start(out=ext[:, 0:384], in_=rel_bias[:, 129:513]), 'rb', writes=['ext'])
            rlast = tmpa[0:8, 0, 0:1]
            P.dma('sp', lambda e: e.dma_start(out=rlast, in_=dap(rel_bias, 512, [[513, 8], [1, 1]]), allow_slow_non_contiguous=True), 'rb2', writes=['rlast'])
            P.op('dve', lambda e: e.tensor_copy(out=ext[:, 384:768], in_=rlast.to_broadcast([8, 384])), reads=['rlast'], writes=['ext2'])
            P.op('act', lambda e: e.activation(out=extb, in_=ext, func=AF.Copy, scale=8.0), reads=['ext', 'ext2', 'tmpb'], writes=['tmpb'])
            P.dma('sp', lambda e: e.dma_start(out=ext_d, in_=extb.unsqueeze(1).to_broadcast([8, 128, 768])), 'extd', reads=['tmpb'], writes=['ext_d'])
            P.dma('sp', lambda e: e.dma_start(out=Eh[:, :, :], in_=dap(ext_d, 127, [[767, 128], [128 * 768, 8], [1, 640]])), 'ehrow',
                  reads=['ext_d'], writes=['Eh'])
            P.op('pool', lambda e: e.memset(Eh[0:64, :, 576:640], -30000.0), reads=['Eh'], writes=['Eh'])
            P.op('pool', lambda e: e.memset(Eh[64:128, :, 0:64], -30000.0), reads=['Eh'], writes=['Eh'])

            ckpt(2)
            sA = ssmf[:, 2, :]
            are = sA[:, 0:16]; aim = sA[:, 16:32]; dtt = sA[:, 32:48]; th = sA[:, 48:64]
            fre = sA[:, 64:80]; fim = sA[:, 80:96]; den = sA[:, 96:112]; t0 = sA[:, 112:128]; t1_ = sA[:, 128:144]
            abr = sA[:, 144:160]; abi = sA[:, 160:176]
            A2 = ssmf[0:32, 3, 0:256]
            A2v = A2.rearrange("g (a d p) -> g a d p", a=2, d=2)
            for ai, arr in enumerate((a_re, a_im)):
                for d_ in range(2):
                    P.dma('sp', lambda e, ai=ai, arr=arr, d_=d_: e.dma_start(out=A2v[:, ai, d_, :], in_=arr), 'ssmld', writes=['A2_%d%d' % (ai, d_)])
            for ai, dstA, nm in ((0, are, 'are'), (1, aim, 'aim')):
                pbA = bank(ai)
                P.op('pe', lambda e, ai=ai, pbA=pbA: e.transpose(pbA[:, 0:32], A2v[:, ai, :, :].rearrange("g d p -> g (d p)"), ident[0:32, 0:32]),
                     reads=['A2_%d0' % ai, 'A2_%d1' % ai, 'ident'], writes=['ps%d' % ai])
                for two in range(2):
                    P.op('dve', lambda e, pbA=pbA, dstA=dstA, two=two: e.tensor_copy(out=dstA[two * 64:(two + 1) * 64, :], in_=pbA[two * 64:(two + 1) * 64, two:32:2]),
                         reads=['ps%d' % ai], writes=[nm])
            Lbc = ssmf[:, 3, 256:288]
            P.dma('sp', lambda e: e.dma_start(out=Lbc, in_=dap(log_dt, 0, [[0, 128], [1, 32]])), 'ssmld3', writes=['Lbc'])
            for two in range(2):
                P.op('dve', lambda e, two=two: e.tensor_copy(out=dtt[two * 64:(two + 1) * 64, :], in_=Lbc[two * 64:(two + 1) * 64, two:32:2]),
                     reads=['Lbc'], writes=['dtt%d' % two])
            P.op('act', lambda e: e.activation(out=dtt, in_=dtt, func=AF.Exp), reads=['dtt0', 'dtt1'], writes=['dtt'])
            P.op('dve', lambda e: e.tensor_tensor(out=t0, in0=dtt, in1=are, op=ALU.mult), reads=['dtt', 'are'], writes=['t0'])
            P.op('act', lambda e: e.activation(out=rco[:], in_=t0, func=AF.Exp), reads=['t0'], writes=['rco'])
            P.op('dve', lambda e: e.tensor_tensor(out=th, in0=dtt, in1=aim, op=ALU.mult), reads=['dtt', 'aim'], writes=['th'])
            P.barrier()
            iot = ssmf[:, 3, 0:256]
            P.op('pool', lambda e: e.iota(iot, pattern=[[1, 256]], base=1, channel_multiplier=0,
                                          allow_small_or_imprecise_dtypes=True), reads=['ssmf3'], writes=['iot'])
            ang = ssmf[:, 4:6, :].rearrange("p a n -> p (a n)")
            kq = ssmf[:, 6:8, :].rearrange("p a n -> p (a n)")
            kqi = kq.bitcast(I32)
            mq = ssmf[:, 0:2, :].rearrange("p a n -> p (a n)")
            C1 = 6.28125
            C2 = float(np.float32(TWO_PI - C1))
            C3 = float(TWO_PI - C1 - C2)
            for jg in range(4):
                for jj in range(4):
                    j = jg * 4 + jj
                    P.op('dve', lambda e, j=j, jj=jj: e.tensor_scalar(out=ang[:, jj * 256:(jj + 1) * 256], in0=iot, scalar1=th[:, j:j + 1],
                                                                     scalar2=None, op0=ALU.mult),
                         reads=['iot', 'th'], writes=['ang'])
                P.op('dve', lambda e: e.tensor_scalar(out=kqi, in0=ang, scalar1=1.0 / TWO_PI, scalar2=None, op0=ALU.mult),
                     reads=['ang'], writes=['kq'])
                P.op('dve', lambda e: e.tensor_copy(out=kq, in_=kqi), reads=['kq'], writes=['kq'])
                P.op('dve', lambda e: e.scalar_tensor_tensor(out=ang, in0=kq, scalar=-C1, in1=ang, op0=ALU.mult, op1=ALU.add),
                     reads=['ang', 'kq'], writes=['ang'])
                P.op('dve', lambda e: e.scalar_tensor_tensor(out=ang, in0=kq, scalar=-C2, in1=ang, op0=ALU.mult, op1=ALU.add),
                     reads=['ang', 'kq'], writes=['ang'])
                for (tab, shift, nm) in ((Stab, 0.0, 'Stab'), (Ctab, math.pi / 2, 'Ctab')):
                    P.op('dve', lambda e, shift=shift: e.tensor_scalar(out=kq, in0=ang, scalar1=shift, scalar2=None, op0=ALU.add),
                         reads=['ang', 'Stab', 'Ctab'], writes=['kq'])
                    for (cmp_, thr, corr) in ((ALU.is_gt, math.pi, -TWO_PI), (ALU.is_lt, -math.pi, TWO_PI),
                                              (ALU.is_gt, math.pi, -TWO_PI)):
                        P.op('dve', lambda e, cmp_=cmp_, thr=thr: e.tensor_scalar(out=mq, in0=kq, scalar1=thr, scalar2=None, op0=cmp_),
                             reads=['kq'], writes=['mq'])
                        P.op('dve', lambda e, corr=corr: e.scalar_tensor_tensor(out=kq, in0=mq, scalar=corr, in1=kq, op0=ALU.mult, op1=ALU.add),
                             reads=['kq', 'mq'], writes=['kq'])
                    P.op('act', lambda e, tab=tab, jg=jg: e.activation(
                        out=tab[:, jg * 4:jg * 4 + 4, :].rearrange("p a n -> p (a n)"), in_=kq, func=AF.Sin),
                        reads=['kq'], writes=[nm])
            ckpt(3)
            for pt_i, tl in ((0, 256), (1, 64)):
                P.op('dve', lambda e, pt_i=pt_i, tl=tl: e.tensor_copy(out=cend[:, pt_i, 0, :], in_=Ctab[:, :, tl - 1]),
                     reads=['Ctab'], writes=['cend'])
                P.op('dve', lambda e, pt_i=pt_i, tl=tl: e.tensor_copy(out=cend[:, pt_i, 1, :], in_=Stab[:, :, tl - 1]),
                     reads=['Stab'], writes=['cend'])
                P.op('dve', lambda e, pt_i=pt_i, tl=tl: e.tensor_scalar(out=cend[:, pt_i, 2, :], in0=Stab[:, :, tl - 1],
                                                                       scalar1=-1.0, scalar2=None, op0=ALU.mult),
                     reads=['Stab'], writes=['cend'])
            P.op('dve', lambda e: e.tensor_tensor(out=abr, in0=rco[:], in1=Ctab[:, :, 0], op=ALU.mult), reads=['rco', 'Ctab'], writes=['abr'])
            P.op('dve', lambda e: e.tensor_tensor(out=abi, in0=rco[:], in1=Stab[:, :, 0], op=ALU.mult), reads=['rco', 'Stab'], writes=['abi'])
            P.op('dve', lambda e: e.tensor_scalar(out=abr, in0=abr, scalar1=-1.0, scalar2=None, op0=ALU.add), reads=['abr'], writes=['abr'])
            P.op('dve', lambda e: e.tensor_tensor(out=den, in0=are, in1=are, op=ALU.mult), reads=['are'], writes=['den'])
            P.op('dve', lambda e: e.tensor_tensor(out=t0, in0=aim, in1=aim, op=ALU.mult), reads=['aim', 'rco'], writes=['t0'])
            P.op('dve', lambda e: e.tensor_tensor(out=den, in0=den, in1=t0, op=ALU.add), reads=['den', 't0'], writes=['den'])
            P.op('dve', lambda e: e.reciprocal(out=den, in_=den), reads=['den'], writes=['den'])
            P.op('dve', lambda e: e.tensor_tensor(out=fre, in0=abr, in1=are, op=ALU.mult), reads=['abr', 'are'], writes=['fre'])
            P.op('dve', lambda e: e.tensor_tensor(out=t0, in0=abi, in1=aim, op=ALU.mult), reads=['abi', 'aim', 'den'], writes=['t0'])
            P.op('dve', lambda e: e.tensor_tensor(out=fre, in0=fre, in1=t0, op=ALU.add), reads=['fre', 't0'], writes=['fre'])
            P.op('dve', lambda e: e.tensor_tensor(out=fre, in0=fre, in1=den, op=ALU.mult), reads=['fre', 'den'], writes=['fre'])
            P.op('dve', lambda e: e.tensor_tensor(out=fim, in0=abi, in1=are, op=ALU.mult), reads=['abi', 'are'], writes=['fim'])
            P.op('dve', lambda e: e.tensor_tensor(out=t0, in0=abr, in1=aim, op=ALU.mult), reads=['abr', 'aim', 'fre'], writes=['t0'])
            P.op('dve', lambda e: e.tensor_tensor(out=fim, in0=fim, in1=t0, op=ALU.subtract), reads=['fim', 't0'], writes=['fim'])
            P.op('dve', lambda e: e.tensor_tensor(out=fim, in0=fim, in1=den, op=ALU.mult), reads=['fim', 'den'], writes=['fim'])
            ckpt(4)
            Bre = ssmf[:, 4, 0:256].rearrange("p (j m) -> p j m", m=16)
            Bim = ssmf[:, 5, 0:256].rearrange("p (j m) -> p j m", m=16)
            bbr = ssmf[:, 6, 0:256].rearrange("p (j m) -> p j m", m=16)
            bbi = ssmf[:, 7, 0:256].rearrange("p (j m) -> p j m", m=16)
            btm = ssmf[:, 3, 256:512].rearrange("p (j m) -> p j m", m=16)
            P.dma('sp', lambda e: e.dma_start(out=Bre, in_=dap(b_re, 0, [[16, 128], [2048, 16], [1, 16]])), 'ssmld4',
                  reads=['Stab', 'Ctab', 'ang', 'kq'], writes=['Bre'])
            P.dma('sp', lambda e: e.dma_start(out=Bim, in_=dap(b_im, 0, [[16, 128], [2048, 16], [1, 16]])), 'ssmld5',
                  reads=['Stab', 'Ctab', 'ang', 'kq'], writes=['Bim'])
            freb = fre.unsqueeze(2).to_broadcast([128, 16, 16])
            fimb = fim.unsqueeze(2).to_broadcast([128, 16, 16])
            P.op('dve', lambda e: e.tensor_tensor(out=bbr, in0=Bre, in1=freb, op=ALU.mult), reads=['Bre', 'fre', 'kq'], writes=['bbr'])
            P.op('dve', lambda e: e.tensor_tensor(out=btm, in0=Bim, in1=fimb, op=ALU.mult), reads=['Bim', 'fim', 'iot'], writes=['btm'])
            P.op('dve', lambda e: e.tensor_tensor(out=bbr, in0=bbr, in1=btm, op=ALU.subtract), reads=['bbr', 'btm'], writes=['bbr'])
            P.op('dve', lambda e: e.tensor_tensor(out=bbi, in0=Bim, in1=freb, op=ALU.mult), reads=['Bim', 'fre', 'kq'], writes=['bbi'])
            P.op('dve', lambda e: e.tensor_tensor(out=btm, in0=Bre, in1=fimb, op=ALU.mult), reads=['Bre', 'fim', 'bbr'], writes=['btm'])
            P.op('dve', lambda e: e.tensor_tensor(out=bbi, in0=bbi, in1=btm, op=ALU.add), reads=['bbi', 'btm'], writes=['bbi'])
            ckpt(5)
            Mbig = xt[:, :, :].rearrange("p a n -> p (a n)")
            Mv = Mbig.rearrange("p (j r c) -> p j r c", r=2, c=128)
            P.op('pool', lambda e: e.memset(Mbig, 0.0), writes=['xt'])
            for ri, bb in ((0, bbr), (1, bbi)):
                for jj in range(4):
                    for two in range(2):
                        c0 = 32 * jj + 16 * two
                        P.op('dve', lambda e, ri=ri, bb=bb, jj=jj, two=two, c0=c0: e.tensor_copy(
                            out=Mv[two * 64:(two + 1) * 64, jj::4, ri, c0:c0 + 16], in_=bb[two * 64:(two + 1) * 64, jj::4, :]),
                            reads=['bbr', 'bbi', 'xt'], writes=['xt'])
            ckpt(6)
            for j in range(16):
                b2, pb2, k2 = nbank()
                def f_t2(e, j=j, pb2=pb2):
                    e.transpose(pb2[:, 0:128], Mv[:, j, 0, :], ident[:])
                    return e.transpose(pb2[:, 128:256], Mv[:, j, 1, :], ident[:])
                P.op('pe', f_t2, reads=['xt', 'ident'], writes=[k2])
                P.op('act', lambda e, j=j, pb2=pb2: e.activation(out=BbT[:, j, :, :].rearrange("p r c -> p (r c)"), in_=pb2[:, 0:256], func=AF.Copy),
                     reads=[k2], writes=['BbT'])
            ckpt(7)
            Cn = xt[:, 0:2, :].rearrange("p a n -> p (a n)")
            Cnv = Cn[:, 0:1024].rearrange("p (r f d q) -> p r f d q", r=2, f=4, d=2)
            for ri, cw in ((0, c_re), (1, c_im)):
                for d_ in range(2):
                    P.dma('sp', lambda e, ri=ri, cw=cw, d_=d_: e.dma_start(out=Cnv[:, ri, :, d_, :], in_=dap(cw, 0, [[64, 128], [8192, 4], [1, 64]])),
                          'cld', reads=['bst', 'BbT'], writes=['Cn%d%d' % (ri, d_)])
            ckpt(8)
            CTall = ssmf[:, 4:6, :].rearrange("p a n -> p (a n)").rearrange("p (r f c) -> p r f c", r=2, f=4)
            for ri in range(2):
                b3, pb3, k3 = nbank()
                def f_t3(e, ri=ri, pb3=pb3):
                    ins = None
                    for ft in range(4):
                        ins = e.transpose(pb3[:, ft * 128:(ft + 1) * 128], Cnv[:, ri, ft, :, :].rearrange("p d q -> p (d q)"), ident[:])
                    return ins
                P.op('pe', f_t3, reads=['Cn%d0' % ri, 'Cn%d1' % ri, 'ident'], writes=[k3])
                P.op('act', lambda e, ri=ri, pb3=pb3: e.activation(out=CTall[:, ri, :, :].rearrange("p f c -> p (f c)"), in_=pb3,
                                                                  func=AF.Copy, scale=(1.0 if ri == 0 else -1.0)),
                     reads=[k3, 'bbr', 'bbi', 'Bre', 'Bim'], writes=['CTall'])
            ckpt(9)
            P.op('pool', lambda e: e.memset(CTw[:].rearrange("p j r c -> p (j r c)"), 0.0), writes=['CTw'])
            CTv = CTw[:].rearrange("p (f q) r c -> p f q r c", q=4)
            for ri in range(2):
                for jj in range(4):
                    for two in range(2):
                        c0 = 32 * jj + 16 * two
                        g0 = (2 * jj + two) * 16
                        P.op('dve', lambda e, ri=ri, jj=jj, two=two, c0=c0, g0=g0: e.tensor_copy(
                            out=CTv[two * 64:(two + 1) * 64, :, jj, ri, c0:c0 + 16], in_=CTall[two * 64:(two + 1) * 64, ri, :, g0:g0 + 16]),
                            reads=['CTall', 'CTw'], writes=['CTw'])
            ckpt(10)
            D4 = ssmf[0:4, 0, 0:128]
            P.dma('sp', lambda e: e.dma_start(out=D4, in_=ssm_d.rearrange("(f p) -> f p", p=128)), 'dcol', writes=['D4'])
            P.op('pe', lambda e: e.transpose(bank(0)[:, 0:4], D4, ident[0:4, 0:4]), reads=['D4', 'ident'], writes=['ps0'])
            P.op('dve', lambda e: e.tensor_copy(out=Dcol[:], in_=bank(0)[:, 0:4]), reads=['ps0'], writes=['Dcol'])
            cast_blk(w_attn, D, 4, 0, 0, 1024, s_attn[0], 's_attn0')
            for b in range(2):
                cast_blk(w_glu, 2 * D, 4, 0, b * 512, 512, s_glu[b], 's_glu%d' % b, 0, 512)
                cast_blk(w_glu, 2 * D, 4, 0, D + b * 512, 512, s_glu[b], 's_glu%d' % b, 512, 512)
            for b in range(2):
                cast_blk(w_out, D, 8, 0, b * 512, 512, s_out[b], 's_out%d' % b)
            for b in range(11):
                cast_blk(w_ffn_in, 2 * DFF, 8, 0, b * 256, 256, s_fin[b], 's_fin%d' % b, 0, 256)
                cast_blk(w_ffn_in, 2 * DFF, 8, 0, DFF + b * 256, 256, s_fin[b], 's_fin%d' % b, 256, 256)
            KPARTS = [(0, 6), (6, 5), (11, 6), (17, 5)]
            for cb in range(2):
                for kh, (k0_, kn_) in enumerate(KPARTS):
                    cast_blk(w_ffn_out, D, kn_, k0_, cb * 512, 512, s_fout[cb * 4 + kh][:, 0:kn_, :], 's_fout%d' % (cb * 4 + kh))

            ckpt(11)
            def ln_stats(tt, src_key, src=None, sm=None, bs=None, kp=''):
                src = xt[:, tt, :] if src is None else src
                sm = small if sm is None else sm
                bs = bnst if bs is None else bs
                kb, ks_ = 'bnst%s%d' % (kp, tt), 'small%s%d' % (kp, tt)
                for h in range(2):
                    P.op('dve', lambda e, tt=tt, h=h: e.bn_stats(out=bs[:, tt, h * 6:h * 6 + 6], in_=src[:, h * 512:(h + 1) * 512]),
                         reads=[src_key], writes=[kb])
                P.op('dve', lambda e, tt=tt: e.bn_aggr(out=sm[:, tt, 0:2], in_=bs[:, tt, :]), reads=[kb], writes=[ks_])
                P.op('act', lambda e, tt=tt: e.activation(out=sm[:, tt, 2:3], in_=sm[:, tt, 1:2], func=AF.Sqrt, bias=epsT[:], scale=1.0),
                     reads=[ks_, 'epsT'], writes=[ks_])
                P.op('dve', lambda e, tt=tt: e.reciprocal(out=sm[:, tt, 3:4], in_=sm[:, tt, 2:3]), reads=[ks_], writes=[ks_])
                P.op('dve', lambda e, tt=tt: e.scalar_tensor_tensor(out=sm[:, tt, 4:5], in0=sm[:, tt, 0:1], scalar=-1.0,
                                                                   in1=sm[:, tt, 3:4], op0=ALU.mult, op1=ALU.mult),
                     reads=[ks_], writes=[ks_])

            def ln_to_featmajor(tt, ntt, segs, sh_sec, sc_sec, src_key, src=None, sm=None, kp='', dest=None, dkeys=('act8',)):
                src = xt[:, tt, :] if src is None else src
                sm = small if sm is None else sm
                dest = act8 if dest is None else dest
                dkeys = list(dkeys)
                ks_ = 'small%s%d' % (kp, tt)
                P.op('act', lambda e, tt=tt: e.activation(out=xn[:, 0, :], in_=src, func=AF.Identity,
                                                         scale=sm[:, tt, 3:4], bias=sm[:, tt, 4:5]),
                     reads=[src_key, ks_, 'xn0', 'xn0b'], writes=['xn0', 'xn0b'])
                nev = 0
                for half in range(2):
                    b, pb, k = nbank()
                    def f_t(e, half=half, pb=pb):
                        ins = None
                        for q in range(4):
                            kc = half * 4 + q
                            ins = e.transpose(pb[:, q * 128:(q + 1) * 128], xn[:, 0, kc * 128:(kc + 1) * 128], ident[:])
                        return ins
                    P.op('pe', f_t, reads=['xn0', 'xn0b', 'ident'], writes=[k])
                    for q in range(4):
                        kc = half * 4 + q
                        for (c0, c1, r) in segs:
                            nev += 1
                            if True:
                                P.op('act', lambda e, pb=pb, q=q, kc=kc, c0=c0, c1=c1, r=r, tt=tt: e.activation(
                                    out=dest[:, kc, tt * 128 + c0:tt * 128 + c1], in_=pb[:, q * 128 + c0:q * 128 + c1], func=AF.Identity,
                                    scale=modT[:, sc_sec * 8 + kc, r:r + 1], bias=modT[:, sh_sec * 8 + kc, r:r + 1]),
                                    reads=[k, 'modT'], writes=dkeys)
                            else:
                                P.op('dve', lambda e, pb=pb, q=q, kc=kc, c0=c0, c1=c1, r=r, tt=tt: e.tensor_scalar(
                                    out=dest[:, kc, tt * 128 + c0:tt * 128 + c1], in0=pb[:, q * 128 + c0:q * 128 + c1],
                                    scalar1=modT[:, sc_sec * 8 + kc, r:r + 1], scalar2=modT[:, sh_sec * 8 + kc, r:r + 1], op0=ALU.mult, op1=ALU.add),
                                    reads=[k, 'modT'], writes=dkeys)

            def ln_affine(tt, gi, bi_):
                P.op('act', lambda e, tt=tt: e.activation(out=xt[:, tt, :], in_=xt[:, tt, :], func=AF.Identity,
                                                         scale=small[:, tt, 3:4], bias=small[:, tt, 4:5]),
                     reads=['xt%d' % tt, 'small%d' % tt], writes=['xt%d' % tt])
                for (eng_, c0_, c1_) in (('dve', 0, 512), ('pool', 512, 1024)):
                    hk = 'xt%d%s' % (tt, 'a' if c0_ == 0 else 'b')
                    P.op(eng_, lambda e, tt=tt, c0_=c0_, c1_=c1_: e.tensor_tensor(out=xt[:, tt, c0_:c1_], in0=xt[:, tt, c0_:c1_], in1=lnbc[:, 0, c0_:c1_], op=ALU.mult),
                         reads=['xt%d' % tt, 'lnbc'], writes=[hk])
                    P.op(eng_, lambda e, tt=tt, c0_=c0_, c1_=c1_: e.tensor_tensor(out=xt[:, tt, c0_:c1_], in0=xt[:, tt, c0_:c1_], in1=lnbc[:, 1, c0_:c1_], op=ALU.add),
                         reads=[hk, 'lnbc'], writes=[hk])
                P.op('pool', lambda e, tt=tt: e.memset(small[:, tt, 6:7], 0.0), reads=['xt%da' % tt, 'xt%db' % tt], writes=['xt%d' % tt])

            def fence(key):
                P.op('dve', lambda e: e.memset(small[:, 0, 7:8], 0.0), writes=[key])

            stgc = [0]

            def tile_info(kind):
                prompt_ = (kind == 'p')
                if prompt_:
                    return 4, {tt: [(0, 128, tt // 2)] for tt in range(4)}
                return 2, {tt: [(0, 64, 2 + 2 * tt), (64, 128, 3 + 2 * tt)] for tt in range(2)}

            def x_rows(kind, ti, tt):
                if kind == 'p':
                    s_, hf = tt // 2, tt % 2
                    return xp[s_, ti * TL + hf * 128: ti * TL + hf * 128 + 128, :]
                return xs[2 * tt:2 * tt + 2, :, :].rearrange("s t d -> (s t) d")

            HTU_KEYS = ['f0', 'f1', 'sr0', 'si0']

            def gen_ln0(kind2, ti2, dest=None, dkeys=('act8',)):
                ntt2, segs2 = tile_info(kind2)
                for tt in range(ntt2):
                    src = x_rows(kind2, ti2, tt)
                    P.dma('act', lambda e, src=src: e.dma_start(out=xn[:, 0, :], in_=src), 'xnld', reads=['xn0b'], writes=['xn0', 'xn0b'])
                    yield
                    ln_stats(tt, 'xn0', src=xn[:, 0, :], sm=small2, bs=bnst2, kp='p')
                    yield
                    ln_to_featmajor(tt, ntt2, segs2[tt], 0, 1, 'xn0', src=xn[:, 0, :], sm=small2, kp='p', dest=dest, dkeys=dkeys)
                    yield

            def gen_ssm(kind):
                prompt = (kind == 'p')
                nseq = 2 if prompt else 4
                tlen = TL if prompt else 64
                ncols = nseq * tlen
                pti = 0 if prompt else 1
                if not prompt:
                    for (src_, dst_, nm) in ((sre, stre, 'stre'), (sim, stim, 'stim')):
                        for d_ in range(2):
                            P.dma('sp', lambda e, src_=src_, d_=d_: e.dma_start(out=stg[:, d_ * 64:(d_ + 1) * 64], in_=src_.rearrange("s g p -> (s g) p")),
                                  'stgin', reads=['stg'], writes=['stg'])
                        b, pb, k = nbank()
                        P.op('pe', lambda e, pb=pb: e.transpose(pb[:, 0:128], stg[:, 0:128], ident[:]), reads=['stg', 'ident'], writes=[k])
                        for two in range(2):
                            P.op('dve', lambda e, pb=pb, dst_=dst_, two=two: e.tensor_copy(
                                out=dst_[two * 64:(two + 1) * 64, :, :],
                                in_=pb[two * 64:(two + 1) * 64, 0:128].rearrange("p (s j t) -> p j s t", s=4, t=2)[:, :, :, two]),
                                reads=[k], writes=[nm])
                def v3(ap):
                    return ap[:, 0:ncols].rearrange("p (s t) -> p s t", t=tlen)
                T = [v3(ssmf[:, i, :]) for i in range(8)]
                ybank = {}
                pending = []

                def emit_y(j):
                    ft, jj = j // 4, j % 4
                    yb, pby, ky = ybank[ft]
                    sb_i = j % 2
                    def f_y(e, j=j, jj=jj, sb_i=sb_i, pby=pby):
                        e.matmul(pby[:, 0:ncols], lhsT=CTw[:, j, 0, :], rhs=ssmb[:, sb_i, 0, 0:ncols], start=(jj == 0), stop=False)
                        return e.matmul(pby[:, 0:ncols], lhsT=CTw[:, j, 1, :], rhs=ssmb[:, sb_i, 1, 0:ncols], start=False, stop=(jj == 3))
                    P.op('pe', f_y, reads=['ssmb%d' % sb_i, 'CTw'], writes=[ky])
                    if jj == 3 and not DBG.get('ssm_noepi'):
                        yv = ssmf[:, 0, 0:ncols]; wv_ = ssmf[:, 1, 0:ncols]
                        P.op('dve', lambda e, ft=ft, pby=pby, yv=yv: e.scalar_tensor_tensor(out=yv, in0=uT[:, ft, 0:ncols], scalar=Dcol[:, ft:ft + 1],
                                                                                         in1=pby[:, 0:ncols], op0=ALU.mult, op1=ALU.add),
                             reads=[ky, 'uT', 'Dcol', 'f0'], writes=['f0'])
                        P.op('act', lambda e, yv=yv, wv_=wv_: e.activation(out=wv_, in_=yv, func=AF.Square), reads=['f0'], writes=['f1'])
                        P.op('act', lambda e, wv_=wv_: e.activation(out=wv_, in_=wv_, func=AF.Identity, scale=0.044715, bias=oneT[:]),
                             reads=['f1', 'oneT'], writes=['f1'])
                        P.op('pool', lambda e, yv=yv, wv_=wv_: e.tensor_tensor(out=wv_, in0=wv_, in1=yv, op=ALU.mult), reads=['f1', 'f0'], writes=['f1'])
                        P.op('act', lambda e, wv_=wv_: e.activation(out=wv_, in_=wv_, func=AF.Sigmoid, scale=1.5957691216057308), reads=['f1'], writes=['f1'])
                        P.op('pool', lambda e, yv=yv, wv_=wv_, ft=ft: e.tensor_tensor(out=gT[:, ft, 0:ncols], in0=wv_, in1=yv, op=ALU.mult),
                             reads=['f1', 'f0'], writes=['gT'])

                for j in range(16):
                    ft, jj = j // 4, j % 4
                    if jj == 0:
                        ybank[ft] = (3, bank(3), 'ps3')
                    b1, pbr, kr = 4, bank(4), 'ps4'
                    b2, pbi, ki = 5, bank(5), 'ps5'
                    if DBG.get('ssm_lvl', 9) < 0:
                        yield
                        continue
                    P.op('pe', lambda e, j=j, ft=ft, pbr=pbr: e.matmul(pbr[:, 0:ncols], lhsT=BbT[:, j, 0, :], rhs=uT[:, ft, 0:ncols], start=True, stop=True),
                         reads=['uT', 'BbT'], writes=[kr])
                    P.op('pe', lambda e, j=j, ft=ft, pbi=pbi: e.matmul(pbi[:, 0:ncols], lhsT=BbT[:, j, 1, :], rhs=uT[:, ft, 0:ncols], start=True, stop=True),
                         reads=['uT', 'BbT'], writes=[ki])
                    yield
                    if DBG.get('ssm_lvl', 9) < 1:
                        continue
                    Cb = Ctab[:, j:j + 1, 0:tlen].to_broadcast([128, nseq, tlen])
                    Sb = Stab[:, j:j + 1, 0:tlen].to_broadcast([128, nseq, tlen])
                    sbf = j % 2
                    SR, SI = 2 + 2 * sbf, 3 + 2 * sbf
                    kSR, kSI = 'sr%d' % sbf, 'si%d' % sbf
                    P.op('dve', lambda e, Sb=Sb, pbr=pbr, SR=SR: e.tensor_tensor(out=T[SR], in0=v3(pbr), in1=Sb, op=ALU.mult), reads=[kr, 'Stab', kSR], writes=[kSR])
                    P.op('dve', lambda e, Cb=Cb, pbr=pbr: e.tensor_tensor(out=v3(pbr), in0=v3(pbr), in1=Cb, op=ALU.mult), reads=[kr, 'Ctab'], writes=[kr])
                    P.op('dve', lambda e, Sb=Sb, pbi=pbi: e.tensor_tensor(out=T[1], in0=v3(pbi), in1=Sb, op=ALU.mult), reads=[ki, 'Stab', 'f1'], writes=['f1'])
                    P.op('dve', lambda e, pbr=pbr: e.tensor_tensor(out=T[0], in0=v3(pbr), in1=T[1], op=ALU.add), reads=[kr, 'f1', 'f0'], writes=['f0'])
                    P.op('dve', lambda e, Cb=Cb, pbi=pbi: e.tensor_tensor(out=v3(pbi), in0=v3(pbi), in1=Cb, op=ALU.mult), reads=[ki, 'Ctab'], writes=[ki])
                    P.op('dve', lambda e, pbi=pbi, SR=SR: e.tensor_tensor(out=T[1], in0=v3(pbi), in1=T[SR], op=ALU.subtract), reads=[ki, kSR, 'f0', 'f1'], writes=['f1'])
                    yield
                    if DBG.get('ssm_lvl', 9) < 2:
                        continue
                    for s in range(nseq):
                        cs = slice(s * tlen, (s + 1) * tlen)
                        rb_ = rco[:, j:j + 1].to_broadcast([128, tlen])
                        P.op('dve', lambda e, cs=cs, rb_=rb_, j=j, s=s, SR=SR: e.tensor_tensor_scan(
                            out=ssmf[:, SR, cs], data0=rb_, data1=ssmf[:, 0, cs], initial=stre[:, j, s:s + 1], op0=ALU.mult, op1=ALU.add),
                            reads=['f0', 'rco', 'stre', kSR], writes=[kSR])
                        P.op('dve', lambda e, cs=cs, rb_=rb_, j=j, s=s, SI=SI: e.tensor_tensor_scan(
                            out=ssmf[:, SI, cs], data0=rb_, data1=ssmf[:, 1, cs], initial=stim[:, j, s:s + 1], op0=ALU.mult, op1=ALU.add),
                            reads=['f1', 'rco', 'stim', kSI], writes=[kSI])
                    yield
                    if DBG.get('ssm_lvl', 9) < 3:
                        continue
                    er = ssmf[:, SR, tlen - 1:ncols:tlen]
                    ei = ssmf[:, SI, tlen - 1:ncols:tlen]
                    cE = cend[:, pti, 0, j:j + 1]; sE = cend[:, pti, 1, j:j + 1]; nsE = cend[:, pti, 2, j:j + 1]
                    tAv = bnst[:, 0, 0:nseq]
                    tBv = bnst[:, 1, 0:nseq]
                    P.op('act', lambda e, er=er, cE=cE, tAv=tAv: e.activation(out=tAv, in_=er, func=AF.Copy, scale=cE),
                         reads=[kSR, 'cend', 'bnst0'], writes=['bnst0'])
                    P.op('act', lambda e, er=er, sE=sE, tBv=tBv: e.activation(out=tBv, in_=er, func=AF.Copy, scale=sE),
                         reads=[kSR, 'cend', 'bnst1'], writes=['bnst1'])
                    P.op('dve', lambda e, ei=ei, nsE=nsE, tAv=tAv, j=j: e.scalar_tensor_tensor(out=stre[:, j, 0:nseq], in0=ei, scalar=nsE, in1=tAv,
                                                                                          op0=ALU.mult, op1=ALU.add),
                         reads=[kSI, 'bnst0', 'cend'], writes=['stre'])
                    P.op('dve', lambda e, ei=ei, cE=cE, tBv=tBv, j=j: e.scalar_tensor_tensor(out=stim[:, j, 0:nseq], in0=ei, scalar=cE, in1=tBv,
                                                                                         op0=ALU.mult, op1=ALU.add),
                         reads=[kSI, 'bnst1', 'cend'], writes=['stim'])
                    if DBG.get('ssm_lvl', 9) < 4:
                        continue
                    sb_i = j % 2
                    srb = v3(ssmb[:, sb_i, 0, :]); sib = v3(ssmb[:, sb_i, 1, :])
                    P.op('pool', lambda e, Cb=Cb, SR=SR: e.tensor_tensor(out=T[6], in0=T[SR], in1=Cb, op=ALU.mult), reads=[kSR, 'Ctab', 'f6'], writes=['f6'])
                    P.op('pool', lambda e, Sb=Sb, SI=SI: e.tensor_tensor(out=T[7], in0=T[SI], in1=Sb, op=ALU.mult), reads=[kSI, 'Stab', 'f7'], writes=['f7'])
                    P.op('pool', lambda e, srb=srb: e.tensor_tensor(out=srb, in0=T[6], in1=T[7], op=ALU.subtract), reads=['f6', 'f7'], writes=['ssmb%d' % sb_i])
                    P.op('pool', lambda e, Sb=Sb, SR=SR: e.tensor_tensor(out=T[6], in0=T[SR], in1=Sb, op=ALU.mult), reads=[kSR, 'Stab', 'f6'], writes=['f6'])
                    P.op('pool', lambda e, Cb=Cb, SI=SI: e.tensor_tensor(out=T[7], in0=T[SI], in1=Cb, op=ALU.mult), reads=[kSI, 'Ctab', 'f7'], writes=['f7'])
                    P.op('pool', lambda e, sib=sib: e.tensor_tensor(out=sib, in0=T[6], in1=T[7], op=ALU.add), reads=['f6', 'f7'], writes=['ssmb%d' % sb_i])
                    yield
                    if DBG.get('ssm_lvl', 9) < 5:
                        continue
                    pending.append(j)
                    if len(pending) > DBG.get('ylag', YLAG):
                        emit_y(pending.pop(0))
                        yield
                while pending:
                    emit_y(pending.pop(0))
                    yield


            ssm_live = {}

            hTu = ssmf[:, 0:4, :].rearrange("p a n -> p (a n)").bitcast(BF16).rearrange("p (k n) -> p k n", n=512)

            def s4_u(kind2, src, skeys):
                nc2 = 512 if kind2 == 'p' else 256
                wv, wk = next_w()
                for ft in range(4):
                    b, pb, k = nbank()
                    def f_mm(e, wv=wv, pb=pb, ft=ft):
                        ins = None
                        for kc in range(8):
                            ins = e.matmul(pb[:, 0:nc2], lhsT=wv[:, kc, ft * 128:(ft + 1) * 128], rhs=src[:, kc, 0:nc2],
                                           start=(kc == 0), stop=(kc == 7))
                        return ins
                    P.op('pe', f_mm, reads=list(skeys) + [wk], writes=[k])
                    P.op('act', lambda e, pb=pb, ft=ft: e.activation(out=uT[:, ft, 0:nc2], in_=pb[:, 0:nc2], func=AF.Copy),
                         reads=[k], writes=['uT', 'tmpa', 'tmpa0', 'tmpa1'])

            def run_tile(kind, ti, pre=False, nxt=None, early=False, last_prompt=False):
                prompt = (kind == 'p')
                nseq = 2 if prompt else 4
                tlen = TL if prompt else 64
                ncols = nseq * tlen
                ntt = ncols // 128
                pti = 0 if prompt else 1
                rows = [0, 1] if prompt else [2, 3, 4, 5]
                if prompt:
                    segs = {tt: [(0, 128, tt // 2)] for tt in range(4)}
                else:
                    segs = {tt: [(0, 64, 2 + 2 * tt), (64, 128, 3 + 2 * tt)] for tt in range(2)}

                def load_gate(sec):
                    for slot in range(2):
                        if prompt:
                            P.dma('sp', lambda e, sec=sec, slot=slot: e.dma_start(
                                out=gbc[:, 0, slot, :], in_=dap(mod_d, slot * 6 * D + sec * D, [[0, 128], [1, D]])),
                                'gbc', reads=['mod_d'], writes=['gbc'])
                        else:
                            for hf in range(2):
                                r = 2 + 2 * slot + hf
                                P.dma('sp', lambda e, sec=sec, slot=slot, hf=hf, r=r: e.dma_start(
                                    out=gbc[hf * 64:(hf + 1) * 64, 0, slot, :], in_=dap(mod_d, r * 6 * D + sec * D, [[0, 64], [1, D]])),
                                    'gbc', reads=['mod_d'], writes=['gbc'])

                def load_x():
                  for tt in range(ntt):
                      if prompt:
                          s, hf = tt // 2, tt % 2
                          src = xp[s, ti * TL + hf * 128: ti * TL + hf * 128 + 128, :]
                      else:
                          src = xs[2 * tt:2 * tt + 2, :, :].rearrange("s t d -> (s t) d")
                      P.dma('sp', lambda e, tt=tt, src=src: e.dma_start(out=xt[:, tt, :], in_=src), 'xt%d' % tt,
                            writes=['xt%d' % tt])
                if not pre:
                    load_x()
                if not pre:
                    for tt in range(ntt):
                        ln_stats(tt, 'xt%d' % tt)
                        ln_to_featmajor(tt, ntt, segs[tt], 0, 1, 'xt%d' % tt)

                tck(20)
                fence('R1')
                def s4_block(blk):
                    wv, wk = next_w()
                    if blk == 2 or (blk == 1 and (not prompt or ti >= 6)):
                        need_out = (not prompt) or ti >= 6
                        for tt in range(ntt):
                            b, pb, k = nbank()
                            def f_mm(e, wv=wv, pb=pb, tt=tt):
                                ins = None
                                for kc in range(8):
                                    ins = e.matmul(pb[:, :], lhsT=act8[:, kc, tt * 128:(tt + 1) * 128], rhs=wv[:, kc, :],
                                                   start=(kc == 0), stop=(kc == 7))
                                return ins
                            P.op('pe', f_mm, reads=['act8', wk], writes=[k])
                            if blk == 2:
                                if prompt:
                                    slot = (tt // 2) * 6 + ((2 * ti + tt % 2) % 6)
                                    P.op('act', lambda e, pb=pb, slot=slot: e.activation(out=vring[:, slot, :], in_=pb, func=AF.Copy),
                                         reads=[k], writes=['vring'])
                                elif not DBG.get('novnew'):
                                    P.op('act', lambda e, pb=pb, tt=tt: e.activation(out=sga[:, 2 * tt:2 * tt + 2, 256:512], in_=pb.rearrange("p (a n) -> p a n", n=256), func=AF.Copy),
                                         reads=[k, 'R1'], writes=['vnew'])
                            if need_out and not (DBG.get('noout2') and blk == 2):
                                sgi = 0
                                stgc[0] += 1
                                P.op('dve', lambda e, pb=pb, sgi=sgi: e.tensor_copy(out=stg2[:, sgi, :], in_=pb), reads=[k], writes=['stg' if sgi == 0 else 'stg2_1'])
                                if prompt:
                                    s_, hf = tt // 2, tt % 2
                                    r0 = (ti - 6) * TL + hf * 128
                                    dst = (kp if blk == 1 else vp)[s_, r0:r0 + 128, :]
                                else:
                                    dst = (ks if blk == 1 else vs)[2 * tt:2 * tt + 2, :, :].rearrange("s t d -> (s t) d")
                                P.dma('sp', lambda e, dst=dst, sgi=sgi: e.dma_start(out=dst, in_=stg2[:, sgi, :]), 'stgout%d' % sgi, reads=['stg' if sgi == 0 else 'stg2_1'])
                            yield
                        if blk == 2:
                            return
                    for ft in range(4):
                        b, pb, k = nbank()
                        def f_mm(e, wv=wv, pb=pb, ft=ft):
                            ins = None
                            for kc in range(8):
                                ins = e.matmul(pb[:, 0:ncols], lhsT=wv[:, kc, ft * 128:(ft + 1) * 128], rhs=act8[:, kc, 0:ncols],
                                               start=(kc == 0), stop=(kc == 7))
                            return ins
                        P.op('pe', f_mm, reads=['act8', wk], writes=[k])
                        if blk == 0:
                            P.op('act', lambda e, pb=pb, ft=ft: e.activation(out=qT[:, ft, 0:ncols], in_=pb[:, 0:ncols], func=AF.Copy),
                                 reads=[k, 'R1'], writes=['qT'])
                        elif blk == 1:
                            if prompt:
                                for s in range(2):
                                    sl0 = s * 6 + (2 * ti) % 6
                                    P.op('act', lambda e, pb=pb, ft=ft, s=s, sl0=sl0: e.activation(
                                        out=kring[:, ft, sl0:sl0 + 2, :], in_=pb[:, s * 256:(s + 1) * 256].rearrange("p (a n) -> p a n", n=128), func=AF.Copy),
                                        reads=[k], writes=['kring'])
                            else:
                                P.op('dve', lambda e, pb=pb, ft=ft: e.tensor_copy(out=knew[:, ft, :], in_=pb[:, 0:256]),
                                     reads=[k, 'R1'], writes=['knew'])
                        elif blk == 3:
                            P.op('act', lambda e, pb=pb, ft=ft: e.activation(out=uT[:, ft, 0:ncols], in_=pb[:, 0:ncols], func=AF.Copy),
                                 reads=[k], writes=['uT', 'tmpa', 'tmpa0', 'tmpa1'])
                        else:
                            dstT = sga if blk < 6 else sgb
                            f8 = (blk % 2) * 4 + ft
                            P.op('act', lambda e, pb=pb, dstT=dstT, f8=f8: e.activation(out=dstT[:, f8, 0:ncols], in_=pb[:, 0:ncols], func=AF.Sigmoid),
                                 reads=[k, 'R1'], writes=['sg'])
                        yield


                if not early:
                    s4_u(kind, act8, ['act8'])

                def attn_block(s_col0, nq, kblocks, qkey_extra):
                    pod, kod = bank(2), 'ps2'
                    nd = len(kblocks)
                    dmax = max(d for d, _, _ in kblocks) + 1
                    for h in range(8):
                        hp, par = h // 2, h % 2
                        hq = hp % 2
                        pl = slice(par * 64, par * 64 + 64)
                        si = h % 2
                        pss = PS[0]
                        skeys = ['ps0', 'ps1']
                        def f_s(e, pss=pss, hp=hp, pl=pl, h=h):
                            ins = None
                            if nq == 128:
                                w0 = min(dmax, 4) * 128
                                e.matmul(pss[:, 0:w0], lhsT=ident_bf[:, :], rhs=Eh[:, h, 0:w0], start=True, stop=False)
                                if dmax == 5:
                                    e.matmul(pss[:, 512:640], lhsT=ident_bf[:, :], rhs=Eh[:, h, 512:640], start=True, stop=False)
                                dA = max(d for d, _, _ in kblocks if d <= 3)
                                for i_, (d, slot, nk) in enumerate(kblocks):
                                    last = (d == dA or d == 4)
                                    ins = e.matmul(pss[0:nk, d * nq:(d + 1) * nq], lhsT=kring[pl, hp, slot, 0:nk], rhs=qT[pl, hp, s_col0:s_col0 + nq],
                                                   start=False, stop=last)
                                return ins
                            for (d, slot, nk) in kblocks:
                                e.matmul(pss[0:nk, d * nq:(d + 1) * nq], lhsT=kring[pl, hp, slot, 0:nk], rhs=qT[pl, hp, s_col0:s_col0 + nq],
                                         start=True, stop=False)
                                ins = e.matmul(pss[0:nk, d * nq:(d + 1) * nq], lhsT=ident_bf[:, 0:nk], rhs=Eh[:, h, d * 128:d * 128 + nq],
                                               start=False, stop=True)
                            return ins
                        P.op('pe', f_s, reads=['kring', 'qT', 'Eh', 'identb'] + qkey_extra, writes=skeys)
                        ptv = pt[:, si, 0:dmax * nq]
                        P.op('act', lambda e, pss=pss, ptv=ptv: e.activation(out=ptv, in_=pss[:, 0:dmax * nq], func=AF.Exp, scale=0.125),
                             reads=skeys, writes=['pt%d' % si])
                        yield
                        def f_pv(e, hq=hq, pl=pl, si=si, h=h):
                            ins = None
                            for i, (d, slot, nk) in enumerate(kblocks):
                                ins = e.matmul(pod[pl, hq * 128:hq * 128 + nq], lhsT=vring[0:nk, slot, h * 64:(h + 1) * 64],
                                               rhs=pt[0:nk, si, d * nq:(d + 1) * nq], start=(i == 0), stop=(i == nd - 1))
                            for i, (d, slot, nk) in enumerate(kblocks):
                                ins = e.matmul(pod[pl, 256 + hq * 128:256 + hq * 128 + nq], lhsT=ones_bf[0:nk, 0:64],
                                               rhs=pt[0:nk, si, d * nq:(d + 1) * nq], start=(i == 0), stop=(i == nd - 1))
                            return ins
                        P.op('pe', f_pv, reads=['pt%d' % si, 'vring', 'ones'], writes=[kod])
                        yield
                        if h % 4 == 3:
                            hp0 = (h // 4) * 2
                            pov = pod[:, 0:256].rearrange("p (a n) -> p a n", n=128)[:, :, 0:nq]
                            pdv = pod[:, 256:512].rearrange("p (a n) -> p a n", n=128)[:, :, 0:nq]
                            rdv = rden[:, 0:256].rearrange("p (a n) -> p a n", n=128)[:, :, 0:nq]
                            P.op('dve', lambda e, rdv=rdv, pdv=pdv: e.reciprocal(out=rdv, in_=pdv), reads=[kod, 'tmpb0', 'tmpb1'], writes=['rden', 'tmpb0', 'tmpb1'])
                            P.op('dve', lambda e, hp0=hp0, pov=pov, rdv=rdv: e.tensor_tensor(out=oT[:, hp0:hp0 + 2, s_col0:s_col0 + nq], in0=pov, in1=rdv, op=ALU.mult),
                                 reads=[kod, 'rden', 'R1'], writes=['oT'])
                            yield

                def gen_att():
                    for blk in (0, 1, 2, 4, 5, 6, 7):
                        yield from s4_block(blk)
                    if prompt:
                        for s in range(2):
                            for qh in range(2):
                                qb = 2 * ti + qh
                                kbl = [(d, s * 6 + (qb - d) % 6, 128) for d in range(5) if qb - d >= 0]
                                yield from attn_block(s * 256 + qh * 128, 128, kbl, [])
                    else:
                        for pr in range(2):
                            for sl in range(2):
                                s = 2 * pr + sl
                                for c in range(4):
                                    slot = sl * 5 + c
                                    P.dma('sp', lambda e, s=s, c=c: e.dma_start(out=stg[:, :], in_=ck[s, c * 128:(c + 1) * 128, :]), 'stgin',
                                          reads=['stg'], writes=['stg'])
                                    b, pb, k = nbank()
                                    def f_t(e, pb=pb):
                                        ins = None
                                        for hp in range(4):
                                            ins = e.transpose(pb[:, hp * 128:(hp + 1) * 128], stg[:, hp * 128:(hp + 1) * 128], ident[:])
                                        return ins
                                    P.op('pe', f_t, reads=['stg', 'ident'], writes=[k])
                                    P.op('act', lambda e, pb=pb, slot=slot: e.activation(out=kring[:, :, slot, :], in_=pb.rearrange("p (a n) -> p a n", n=128), func=AF.Copy),
                                         reads=[k], writes=['kring'])
                                    P.dma('sp', lambda e, s=s, c=c: e.dma_start(out=stg[:, :], in_=cv[s, c * 128:(c + 1) * 128, :]), 'stgin',
                                          reads=['stg'], writes=['stg'])
                                    P.op('act', lambda e, slot=slot: e.activation(out=vring[:, slot, :], in_=stg[:, :], func=AF.Copy),
                                         reads=['stg'], writes=['vring'])
                                slot = sl * 5 + 4
                                P.op('dve', lambda e, s=s, slot=slot: e.tensor_copy(out=kring[:, :, slot, 0:64], in_=knew[:, :, s * 64:(s + 1) * 64]),
                                     reads=['knew'], writes=['kring'])
                                if s % 2 == 0:
                                    P.op('dve', lambda e, s=s, slot=slot: e.tensor_copy(out=vring[0:64, slot, :].rearrange("p (a n) -> p a n", n=256), in_=sga[0:64, 2 * (s // 2):2 * (s // 2) + 2, 256:512]),
                                         reads=['vnew'], writes=['vring'])
                                else:
                                    P.dma('sp', lambda e, s=s, slot=slot: e.dma_start(out=vring[0:64, slot, :].rearrange("p (a n) -> p a n", n=256), in_=sga[64:128, 2 * (s // 2):2 * (s // 2) + 2, 256:512]), 'vshift',
                                          reads=['vnew'], writes=['vring'])
                            for sl in range(2):
                                s = 2 * pr + sl
                                kbl = [(0, sl * 5 + 4, 64)] + [(4 - c, sl * 5 + c, 128) for c in range(4)]
                                yield from attn_block(s * 64, 64, kbl, [])

                    wv, wk = next_w()
                    for f in range(8):
                        b, pb, k = nbank()
                        def f_mm(e, wv=wv, pb=pb, f=f):
                            ins = None
                            for kc in range(4):
                                ins = e.matmul(pb[:, 0:ncols], lhsT=wv[:, kc, f * 128:(f + 1) * 128], rhs=oT[:, kc, 0:ncols], start=(kc == 0), stop=(kc == 3))
                            return ins
                        P.op('pe', f_mm, reads=['oT', wk], writes=[k])
                        P.op('dve', lambda e, pb=pb, f=f: e.tensor_tensor(out=act8[:, f, 0:ncols], in0=pb[:, 0:ncols], in1=sga[:, f, 0:ncols], op=ALU.mult),
                             reads=[k, 'sg', 'R1'], writes=['act8'])
                        yield


                bank_allowed[0] = [6, 7]
                interleave(([] if DBG.get('noatt') else [limited(gen_att(), DBG.get('attstop', 10**9))]) + ([] if DBG.get('nossm') else [ssm_live.pop((kind, ti), None) or gen_ssm(kind)]), [DBG.get('attw', ATT_W), DBG.get('ssmw', SSM_W)][(1 if DBG.get('noatt') else 0):])
                bank_allowed[0] = list(range(8))
                if last_prompt:
                    write_states(2, rep, imp)
                if pre:
                    load_x()

                for blk in range(2):
                    wv, wk = next_w()
                    for fl in range(4):
                        f = blk * 4 + fl
                        b1, pba, ka = nbank()
                        b2, pbb, kb_ = nbank()
                        def f_mm(e, wv=wv, pba=pba, pbb=pbb, fl=fl):
                            ins = None
                            for kc in range(4):
                                ins = e.matmul(pbb[:, 0:ncols], lhsT=wv[:, kc, 512 + fl * 128:512 + (fl + 1) * 128], rhs=gT[:, kc, 0:ncols], start=(kc == 0), stop=(kc == 3))
                            for kc in range(4):
                                ins = e.matmul(pba[:, 0:ncols], lhsT=wv[:, kc, fl * 128:(fl + 1) * 128], rhs=gT[:, kc, 0:ncols], start=(kc == 0), stop=(kc == 3))
                            return ins
                        P.op('pe', f_mm, reads=['gT', wk], writes=[ka, kb_])
                        tb = tmpb[:, f % 2, 0:ncols]
                        ta = tmpa[:, f % 2, 0:ncols]
                        P.op('act', lambda e, pbb=pbb, tb=tb: e.activation(out=tb, in_=pbb[:, 0:ncols], func=AF.Sigmoid), reads=[kb_, 'tmpb%d' % (f % 2)], writes=['tmpb%d' % (f % 2)])
                        P.op('dve', lambda e, pba=pba, tb=tb, ta=ta: e.tensor_tensor(out=ta, in0=pba[:, 0:ncols], in1=tb, op=ALU.mult),
                             reads=[ka, 'tmpb%d' % (f % 2), 'tmpa', 'tmpa1', 'tmpa%d' % (f % 2), 'R1'], writes=['tmpa%d' % (f % 2), 'uT'])
                        P.op('pool', lambda e, ta=ta, f=f: e.tensor_tensor(out=ta, in0=ta, in1=sgb[:, f, 0:ncols], op=ALU.mult),
                             reads=['tmpa%d' % (f % 2), 'sg', 'R1'], writes=['tmpa%d' % (f % 2)])
                        P.op('pool', lambda e, ta=ta, f=f: e.tensor_tensor(out=act8[:, f, 0:ncols], in0=act8[:, f, 0:ncols], in1=ta, op=ALU.add),
                             reads=['tmpa%d' % (f % 2), 'act8'], writes=['act8'])

                tck(25)
                def resid(tt, cb, pb, k, gate):
                    slot = (tt // 2) if prompt else tt
                    P.op('dve', lambda e, pb=pb, slot=slot, cb=cb: e.tensor_tensor(out=pb, in0=pb, in1=gbc[:, 0, slot, cb * 512:(cb + 1) * 512], op=ALU.mult),
                         reads=[k, 'gbc'], writes=[k])
                    P.op('dve', lambda e, tt=tt, cb=cb, pb=pb: e.scalar_tensor_tensor(out=xt[:, tt, cb * 512:(cb + 1) * 512], in0=xt[:, tt, cb * 512:(cb + 1) * 512],
                                                                                    scalar=ALPHA, in1=pb, op0=ALU.mult, op1=ALU.add),
                         reads=[k, 'xt%d' % tt], writes=['xt%d' % tt])

                load_gate(2)
                for i_, v_ in enumerate([ln1_g, ln1_b]):
                    P.dma('sp', lambda e, i_=i_, v_=v_: e.dma_start(out=lnbc[:, i_, :], in_=dap(v_, 0, [[0, 128], [1, D]])), 'lnbc', writes=['lnbc'])
                wouts = [next_w(), next_w(prefetch=False)]

                def wout_tt(tt):
                    for cb in range(2):
                        wv, wk = wouts[cb]
                        b, pb, k = nbank()
                        def f_mm(e, wv=wv, pb=pb, tt=tt):
                            ins = None
                            for kc in range(8):
                                ins = e.matmul(pb, lhsT=act8[:, kc, tt * 128:(tt + 1) * 128], rhs=wv[:, kc, :], start=(kc == 0), stop=(kc == 7))
                            return ins
                        P.op('pe', f_mm, reads=['act8', wk], writes=[k])
                        resid(tt, cb, pb, k, 0)

                def ln1_a(tt):
                    ln_stats(tt, 'xt%d' % tt)
                    ln_affine(tt, 0, 1)
                    ln_stats(tt, 'xt%d' % tt)

                def gen_wout():
                    for tt in range(ntt):
                        wout_tt(tt)
                        yield
                        if tt >= 1:
                            ln1_a(tt - 1)
                            yield
                    prefetch_w()
                    ln1_a(ntt - 1)
                    yield

                do_early = EARLY and nxt is not None
                if do_early:
                    interleave([gen_wout(), gen_ln0(nxt[0], nxt[1], dest=hTu, dkeys=HTU_KEYS)], [1, 2])
                    s4_u(nxt[0], hTu, HTU_KEYS)
                    ssm_live[nxt] = gen_ssm(nxt[0])
                    for tt in range(ntt):
                        ln_to_featmajor(tt, ntt, segs[tt], 3, 4, 'xt%d' % tt)
                else:
                    for tt in range(ntt):
                        wout_tt(tt)
                        if tt >= 1:
                            ln1_a(tt - 1)
                    prefetch_w()
                    ln_to_featmajor(0, ntt, segs[0], 3, 4, 'xt0')
                    ln1_a(ntt - 1)
                    for tt in range(1, ntt):
                        ln_to_featmajor(tt, ntt, segs[tt], 3, 4, 'xt%d' % tt)

                tck(26)
                fence('R1')
                def gen_ffn_in():
                    for blk in range(11):
                        wv, wk = next_w()
                        for fl in range(2):
                            f = blk * 2 + fl
                            b1, pbg, kg = nbank()
                            b2, pbu, ku = nbank()
                            def f_mm(e, wv=wv, pbg=pbg, pbu=pbu, fl=fl):
                                ins = None
                                for kc in range(8):
                                    ins = e.matmul(pbg[:, 0:ncols], lhsT=wv[:, kc, fl * 128:(fl + 1) * 128], rhs=act8[:, kc, 0:ncols], start=(kc == 0), stop=(kc == 7))
                                for kc in range(8):
                                    ins = e.matmul(pbu[:, 0:ncols], lhsT=wv[:, kc, 256 + fl * 128:256 + (fl + 1) * 128], rhs=act8[:, kc, 0:ncols], start=(kc == 0), stop=(kc == 7))
                                return ins
                            P.op('pe', f_mm, reads=['act8', wk], writes=[kg, ku])
                            tb = tmpb[:, f % 2, 0:ncols]
                            P.op('act', lambda e, pbg=pbg, tb=tb: e.activation(out=tb, in_=pbg[:, 0:ncols], func=AF.Silu), reads=[kg, 'tmpb%d' % (f % 2)], writes=['tmpb%d' % (f % 2)])
                            P.op('dve', lambda e, pbu=pbu, tb=tb, f=f: e.tensor_tensor(out=actT[:, f, 0:ncols], in0=pbu[:, 0:ncols], in1=tb, op=ALU.mult),
                                 reads=[ku, 'tmpb%d' % (f % 2), 'R1'], writes=['actT'])
                            yield

                gens_ = [gen_ffn_in()]
                ws_ = [1]
                if do_early:
                    bank_allowed[0] = [0, 1, 2, 6, 7]
                    gens_.append(limited(ssm_live[nxt], DBG.get('ssm_f', SSM_F)))
                    ws_.append(1)
                interleave(gens_, ws_)
                bank_allowed[0] = list(range(8))
                tck(27)
                load_gate(5)

                def gen_ffn_out():
                    for cb in range(2):
                        banks = [(bb_, bank(bb_), 'ps%d' % bb_) for bb_ in (0, 1, 6, 7)[:ntt]]
                        for kh, (k0_, kn_) in enumerate(KPARTS):
                            wv, wk = next_w()
                            for tt in range(ntt):
                                b, pb, k = banks[tt]
                                def f_mm(e, wv=wv, pb=pb, tt=tt, kh=kh, k0_=k0_, kn_=kn_):
                                    ins = None
                                    for kc in range(kn_):
                                        ins = e.matmul(pb, lhsT=actT[:, k0_ + kc, tt * 128:(tt + 1) * 128], rhs=wv[:, kc, :],
                                                       start=(kh == 0 and kc == 0), stop=(kh == 3 and kc == kn_ - 1))
                                    return ins
                                P.op('pe', f_mm, reads=['actT', wk, 'R1'], writes=[k])
                                yield
                        for tt in range(ntt):
                            b, pb, k = banks[tt]
                            resid(tt, cb, pb, k, 1)
                            yield

                bank_allowed[0] = [2] if do_early else [2, 3, 4, 5]
                gens_ = [gen_ffn_out()]
                ws_ = [DBG.get('ffw', 4)]
                if nxt is not None:
                    gens_.append(gen_ln0(*nxt))
                    ws_.append(1)
                if do_early:
                    gens_.append(limited(ssm_live[nxt], DBG.get('ssm_g', SSM_G)))
                    ws_.append(1)
                interleave(gens_, ws_)
                bank_allowed[0] = list(range(8))
                for i_, v_ in enumerate([ln2_g, ln2_b]):
                    P.dma('sp', lambda e, i_=i_, v_=v_: e.dma_start(out=lnbc[:, i_, :], in_=dap(v_, 0, [[0, 128], [1, D]])), 'lnbc', writes=['lnbc'])
                for tt in range(ntt):
                    ln_stats(tt, 'xt%d' % tt)
                    ln_affine(tt, 2, 3)
                    if prompt:
                        s, hf = tt // 2, tt % 2
                        dst = yp[s, ti * TL + hf * 128: ti * TL + hf * 128 + 128, :]
                    else:
                        dst = ys[2 * tt:2 * tt + 2, :, :].rearrange("s t d -> (s t) d")
                    P.dma('pool', lambda e, tt=tt, dst=dst: e.dma_start(out=dst, in_=xt[:, tt, :]), 'yout%d' % tt, reads=['xt%d' % tt])

            def write_states(ns, dre, dim_):
                for (st_, dd, nm) in ((stre, dre, 'stre'), (stim, dim_, 'stim')):
                    tcp = tmpa[:, 0, 0:16 * ns]
                    P.op('dve', lambda e, st_=st_, tcp=tcp: e.tensor_copy(out=tcp.rearrange("p (s j) -> p s j", j=16),
                                                                           in_=st_[:, :, 0:ns].rearrange("p j s -> p s j")),
                         reads=[nm, 'tmpa', 'tmpa0', 'tmpa1'], writes=['tmpa', 'tmpa0', 'uT'])
                    b, pb, k = nbank()
                    P.op('pe', lambda e, pb=pb, tcp=tcp: e.transpose(pb[:, 0:128], tmpa[:, 0, 0:128], ident[:]), reads=['tmpa', 'ident'], writes=[k])
                    P.op('dve', lambda e, pb=pb: e.tensor_copy(out=stg2[0:16 * ns, 0, 0:128], in_=pb[0:16 * ns, 0:128]), reads=[k, 'stg'], writes=['stg'])
                    dst = dd.rearrange("s (j t) p -> (s j) (t p)", t=2)
                    P.dma('sp', lambda e, dst=dst: e.dma_start(out=dst, in_=stg2[0:16 * ns, 0, 0:128]), 'stout', reads=['stg'])

            seq_tiles = [('p', ti) for ti in range(DBG['ntiles'])] + ([('s', 0)] if DBG['sample'] else [])
            PREF = DBG.get('pref', True)
            EARLY = DBG.get('early', False) and PREF
            build_wseq(len(seq_tiles), EARLY)
            for idx_, (kind_, ti_) in enumerate(seq_tiles):
                nxt_ = seq_tiles[idx_ + 1] if (PREF and idx_ + 1 < len(seq_tiles)) else None
                lastp = (kind_ == 'p' and ti_ == DBG['ntiles'] - 1)
                run_tile(kind_, ti_, pre=(PREF and idx_ > 0), nxt=nxt_, early=(EARLY and idx_ > 0), last_prompt=lastp)
                if kind_ == 's':
                    write_states(4, res, ims)

        except StopBuild:
            pass
        P.barrier()
        P.emit()
    return nc


_NC_CACHE = {}
DBG = {'ntiles': NTILES, 'sample': True, 'cores': NCORES, 'stop': 99}


def kernel(**inp):
    f = lambda a: np.ascontiguousarray(np.asarray(a, dtype=np.float32))
    if 'nc' not in _NC_CACHE:
        _NC_CACHE['nc'] = build_nc()
    nc = _NC_CACHE['nc']
    shared = {
        'w_ada': f(inp['w_ada'][0]), 'b_ada': f(inp['b_ada'][0]), 'w_in': f(inp['w_in'][0]), 'rel_bias': f(inp['rel_bias'][0]),
        'a_re': f(inp['ssm_a_re'][0]), 'a_im': f(inp['ssm_a_im'][0]), 'log_dt': f(inp['ssm_log_dt'][0]),
        'b_re': f(inp['ssm_b_re'][0]), 'b_im': f(inp['ssm_b_im'][0]), 'c_re': f(inp['ssm_c_re'][0]), 'c_im': f(inp['ssm_c_im'][0]),
        'ssm_d': f(inp['ssm_d'][0]), 'w_attn': f(inp['w_attn_proj'][0]), 'w_glu': f(inp['w_glu'][0]), 'w_out': f(inp['w_out'][0]),
        'ln1_g': f(inp['ln1_g'][0]), 'ln1_b': f(inp['ln1_b'][0]), 'w_ffn_in': f(inp['w_ffn_in'][0]), 'w_ffn_out': f(inp['w_ffn_out'][0]),
        'ln2_g': f(inp['ln2_g'][0]), 'ln2_b': f(inp['ln2_b'][0]),
    }
    in_maps = []
    for c in range(DBG['cores']):
        m = dict(shared)
        m['xp'] = f(inp['x_prompt'][2 * c:2 * c + 2])
        m['xs'] = f(inp['x_sample'][4 * c:4 * c + 4])
        m['cc'] = f(np.concatenate([inp['c_prompt'][2 * c:2 * c + 2], inp['c_sample'][4 * c:4 * c + 4]], axis=0))
        m['ck'] = f(inp['cache_attn_k'][0, 4 * c:4 * c + 4].reshape(4, 512, 512))
        m['cv'] = f(inp['cache_attn_v'][0, 4 * c:4 * c + 4].reshape(4, 512, 512))
        m['sre'] = f(inp['state_ssm_re'][0, 4 * c:4 * c + 4])
        m['sim'] = f(inp['state_ssm_im'][0, 4 * c:4 * c + 4])
        in_maps.append(m)
    if DBG.get('trace'):
        res = run_bass_kernel_spmd(nc, in_maps, core_ids=list(range(DBG['cores'])), trace=True)
        print('EXEC_NS', res.exec_time_ns)
    else:
        res = run_bass_kernel_spmd(nc, in_maps, core_ids=list(range(DBG['cores'])))
    R = res.results
    cat = lambda k: np.concatenate([np.asarray(r[k], dtype=np.float32) for r in R], axis=0)
    y_prompt = cat('yp')
    y_sample = cat('ys')
    k_prompt = cat('kp').reshape(1, -1, 512, 8, 64)
    v_prompt = cat('vp').reshape(1, -1, 512, 8, 64)
    re_p = cat('rep')[None]
    im_p = cat('imp')[None]
    k_sample = cat('ks').reshape(1, -1, 64, 8, 64)
    v_sample = cat('vs').reshape(1, -1, 64, 8, 64)
    re_s = cat('res')[None]
    im_s = cat('ims')[None]
    return (y_prompt, y_sample, k_prompt, v_prompt, re_p, im_p, k_sample, v_sample, re_s, im_s)
```
